# Optimizing a Trainium2 kernel written in Bass

```python
import math
import jax, jax.numpy as jnp
from jax import lax
import numpy as np

D_MODEL = 1024
BATCH = 4
SEQ = 4096
DEPTH = 2
DEC_BATCH = 32
DEC_SEQ = 64
PAST_LEN = 2048

CHUNK = 64
D_HEAD = 64
H_FOX = (3 * D_MODEL // 8) // D_HEAD
H_SB = (3 * D_MODEL // 8) // D_HEAD
H_SGU = D_MODEL // D_HEAD - H_FOX - H_SB
W_FOX = H_FOX * D_HEAD
W_SB = H_SB * D_HEAD
W_SGU = H_SGU * D_HEAD
D_MIX = W_FOX + W_SGU + W_SB
D_IN = 3 * W_FOX + H_FOX + 2 * W_SGU + 3 * W_SB
SGU_CHUNK = 128
Q_BLOCK = 128
D_FF = 4 * D_MODEL
ALPHA = (2 * DEPTH) ** 0.25
BETA = (8 * DEPTH) ** -0.25
FORGET_BIAS = 2.0
LN_EPS = 1e-5
RMS_EPS = 1e-6
NEG_INF = -1e30

kernel_name = "hybrid_fox_sgu_stickbreak_stream_encoder"


def _split_points():
    sizes = [W_FOX, W_FOX, W_FOX, H_FOX, W_SGU, W_SGU, W_SB, W_SB, W_SB]
    return np.cumsum(sizes)[:-1].tolist()


def layer_norm(x, g, b):
    xf = x.astype(jnp.float32)
    mu = jnp.mean(xf, axis=-1, keepdims=True)
    var = jnp.mean(jnp.square(xf - mu), axis=-1, keepdims=True)
    return ((xf - mu) * lax.rsqrt(var + LN_EPS) * g + b).astype(x.dtype)


def in_proj(x, w_in, b_f):
    B, T, _ = x.shape
    z = jnp.einsum('btd,de->bte', x, w_in)
    q_f, k_f, v_f, f_lg, u_g, v_g, q_s, k_s, v_s = jnp.split(z, _split_points(), axis=-1)
    heads = lambda a, h: a.reshape(B, T, h, D_HEAD)
    log_f = jax.nn.log_sigmoid((f_lg + b_f).astype(jnp.float32))
    return (heads(q_f, H_FOX), heads(k_f, H_FOX), heads(v_f, H_FOX), log_f,
            u_g, v_g, heads(q_s, H_SB), heads(k_s, H_SB), heads(v_s, H_SB))


def fox_block(q, c_q, qpos, k, v, c_k):
    kpos = jnp.arange(k.shape[1])
    s = jnp.einsum('bqhd,bkhd->bhqk', q, k).astype(jnp.float32) / math.sqrt(D_HEAD)
    s = s + jnp.transpose(c_q, (0, 2, 1))[..., :, None] - jnp.transpose(c_k, (0, 2, 1))[..., None, :]
    mask = kpos[None, :] <= qpos[:, None]
    p = jax.nn.softmax(jnp.where(mask, s, NEG_INF), axis=-1)
    return jnp.einsum('bhqk,bkhd->bqhd', p.astype(v.dtype), v)


def sb_block(q, qpos, k, v):
    kpos = jnp.arange(k.shape[1])
    z = jnp.einsum('bqhd,bkhd->bhqk', q, k).astype(jnp.float32) / math.sqrt(D_HEAD)
    mask = kpos[None, :] < qpos[:, None]
    log_rem = jnp.where(mask, jax.nn.log_sigmoid(-z), 0.0)
    after = lax.cumsum(log_rem, axis=3, reverse=True) - log_rem
    a = jnp.where(mask, jnp.exp(jax.nn.log_sigmoid(z) + after), 0.0)
    return jnp.einsum('bhqk,bkhd->bqhd', a.astype(v.dtype), v)


def sgu_gate(u_g, v_g, g_v, b_v):
    u = jax.nn.gelu(u_g)
    v = layer_norm(jax.nn.gelu(v_g), g_v, b_v)
    return u, v


def sgu_mix(u, v, w_s, b_s):
    B, T, _ = v.shape
    L = min(T, SGU_CHUNK)
    n = T // L
    w = jnp.tril(w_s[:, :L, :L])
    vb = v.reshape(B, n, L, H_SGU, D_HEAD)
    s = jnp.einsum('gij,bnjgc->bnigc', w, vb) + b_s[:, :L].T[None, None, :, :, None]
    return u * s.reshape(B, T, W_SGU)


def merge_heads(o_fox, o_sgu, o_sb, g_mix, w_out):
    B, T = o_sgu.shape[:2]
    o = jnp.concatenate([o_fox.reshape(B, T, W_FOX), o_sgu, o_sb.reshape(B, T, W_SB)], axis=-1)
    oh = o.reshape(B, T, D_MIX // D_HEAD, D_HEAD).astype(jnp.float32)
    oh = oh * lax.rsqrt(jnp.mean(oh * oh, axis=-1, keepdims=True) + RMS_EPS)
    o = (oh.reshape(B, T, D_MIX) * g_mix).astype(o_sgu.dtype)
    return jnp.einsum('bte,ed->btd', o, w_out)


def sq_relu_mlp(x, w_up, w_down):
    h = jnp.square(jax.nn.relu(jnp.einsum('btd,df->btf', x, w_up)))
    return jnp.einsum('btf,fd->btd', h, w_down)


def prompt_attention(q_f, k_f, v_f, c, q_s, k_s, v_s):
    B, T = q_f.shape[:2]
    nb = T // Q_BLOCK

    def blk(i):
        start = i * Q_BLOCK
        qpos = start + jnp.arange(Q_BLOCK)
        qf = lax.dynamic_slice_in_dim(q_f, start, Q_BLOCK, axis=1)
        cq = lax.dynamic_slice_in_dim(c, start, Q_BLOCK, axis=1)
        qs = lax.dynamic_slice_in_dim(q_s, start, Q_BLOCK, axis=1)
        return fox_block(qf, cq, qpos, k_f, v_f, c), sb_block(qs, qpos, k_s, v_s)

    o_f, o_s = lax.map(blk, jnp.arange(nb))
    o_f = jnp.transpose(o_f, (1, 0, 2, 3, 4)).reshape(B, T, H_FOX, D_HEAD)
    o_s = jnp.transpose(o_s, (1, 0, 2, 3, 4)).reshape(B, T, H_SB, D_HEAD)
    return o_f, o_s


def setup_inputs(seed: int = 0) -> dict:
    key = jax.random.key(seed)
    ks = jax.random.split(key, 24)
    nrm = lambda k, shape: jax.random.normal(k, shape, jnp.float32)
    return {
        "x_prompt": nrm(ks[0], (BATCH, SEQ, D_MODEL)),
        "x_sample": nrm(ks[1], (DEC_BATCH, DEC_SEQ, D_MODEL)),
        "cache_fox_k": nrm(ks[2], (DEPTH, DEC_BATCH, PAST_LEN, H_FOX, D_HEAD)),
        "cache_fox_v": nrm(ks[3], (DEPTH, DEC_BATCH, PAST_LEN, H_FOX, D_HEAD)),
        "cache_fox_logf": jax.nn.log_sigmoid(FORGET_BIAS + nrm(ks[4], (DEPTH, DEC_BATCH, PAST_LEN, H_FOX))),
        "cache_sb_k": nrm(ks[5], (DEPTH, DEC_BATCH, PAST_LEN, H_SB, D_HEAD)),
        "cache_sb_v": nrm(ks[6], (DEPTH, DEC_BATCH, PAST_LEN, H_SB, D_HEAD)),
        "w_in": nrm(ks[7], (DEPTH, D_MODEL, D_IN)) * D_MODEL ** -0.5,
        "b_f": FORGET_BIAS + 0.1 * nrm(ks[8], (DEPTH, H_FOX)),
        "g_v": 1.0 + 0.1 * nrm(ks[9], (DEPTH, W_SGU)),
        "b_v": 0.02 * nrm(ks[10], (DEPTH, W_SGU)),
        "w_s": nrm(ks[11], (DEPTH, H_SGU, SGU_CHUNK, SGU_CHUNK)) * SGU_CHUNK ** -0.5,
        "b_s": 1.0 + 0.1 * nrm(ks[12], (DEPTH, H_SGU, SGU_CHUNK)),
        "g_mix": 1.0 + 0.1 * nrm(ks[13], (DEPTH, D_MIX)),
        "w_out": nrm(ks[14], (DEPTH, D_MIX, D_MODEL)) * (D_MIX ** -0.5 * BETA),
        "ln1_g": 1.0 + 0.1 * nrm(ks[15], (DEPTH, D_MODEL)),
        "ln1_b": 0.02 * nrm(ks[16], (DEPTH, D_MODEL)),
        "w_up": nrm(ks[17], (DEPTH, D_MODEL, D_FF)) * D_MODEL ** -0.5,
        "w_down": nrm(ks[18], (DEPTH, D_FF, D_MODEL)) * (D_FF ** -0.5 * BETA),
        "ln2_g": 1.0 + 0.1 * nrm(ks[19], (DEPTH, D_MODEL)),
        "ln2_b": 0.02 * nrm(ks[20], (DEPTH, D_MODEL)),
    }


def reference(x_prompt, x_sample, cache_fox_k, cache_fox_v, cache_fox_logf, cache_sb_k, cache_sb_v,
              w_in, b_f, g_v, b_v, w_s, b_s, g_mix, w_out, ln1_g, ln1_b, w_up, w_down, ln2_g, ln2_b):
    assert x_sample.shape[1] <= CHUNK
    xp, xs = x_prompt, x_sample
    p_fk, p_fv, p_fl, p_sk, p_sv = [], [], [], [], []
    s_fk, s_fv, s_fl, s_sk, s_sv, s_gv = [], [], [], [], [], []
    for l in range(DEPTH):
        q_f, k_f, v_f, log_f, u_g, v_g, q_s, k_s, v_s = in_proj(xp, w_in[l], b_f[l])
        c = lax.cumsum(log_f, axis=1)
        o_f, o_s = prompt_attention(q_f, k_f, v_f, c, q_s, k_s, v_s)
        u, v = sgu_gate(u_g, v_g, g_v[l], b_v[l])
        o_g = sgu_mix(u, v, w_s[l], b_s[l])
        mix = merge_heads(o_f, o_g, o_s, g_mix[l], w_out[l])
        xp = layer_norm(ALPHA * xp + mix, ln1_g[l], ln1_b[l])
        xp = layer_norm(ALPHA * xp + sq_relu_mlp(xp, w_up[l], w_down[l]), ln2_g[l], ln2_b[l])
        p_fk.append(k_f); p_fv.append(v_f); p_fl.append(log_f); p_sk.append(k_s); p_sv.append(v_s)

        q_f, k_f, v_f, log_f, u_g, v_g, q_s, k_s, v_s = in_proj(xs, w_in[l], b_f[l])
        past = cache_fox_k.shape[2]
        qpos = past + jnp.arange(xs.shape[1])
        kf_all = jnp.concatenate([cache_fox_k[l], k_f], axis=1)
        vf_all = jnp.concatenate([cache_fox_v[l], v_f], axis=1)
        c_all = lax.cumsum(jnp.concatenate([cache_fox_logf[l].astype(jnp.float32), log_f], axis=1), axis=1)
        o_f = fox_block(q_f, c_all[:, past:], qpos, kf_all, vf_all, c_all)
        ks_all = jnp.concatenate([cache_sb_k[l], k_s], axis=1)
        vs_all = jnp.concatenate([cache_sb_v[l], v_s], axis=1)
        o_s = sb_block(q_s, qpos, ks_all, vs_all)
        u, v = sgu_gate(u_g, v_g, g_v[l], b_v[l])
        o_g = sgu_mix(u, v, w_s[l], b_s[l])
        mix = merge_heads(o_f, o_g, o_s, g_mix[l], w_out[l])
        xs = layer_norm(ALPHA * xs + mix, ln1_g[l], ln1_b[l])
        xs = layer_norm(ALPHA * xs + sq_relu_mlp(xs, w_up[l], w_down[l]), ln2_g[l], ln2_b[l])
        s_fk.append(k_f); s_fv.append(v_f); s_fl.append(log_f); s_sk.append(k_s); s_sv.append(v_s)
        s_gv.append(v)

    return (xp, xs,
            jnp.stack(p_fk), jnp.stack(p_fv), jnp.stack(p_fl), jnp.stack(p_sk), jnp.stack(p_sv),
            jnp.stack(s_fk), jnp.stack(s_fv), jnp.stack(s_fl), jnp.stack(s_sk), jnp.stack(s_sv),
            jnp.stack(s_gv))
```

```python
import math
from contextlib import ExitStack

import numpy as np
import concourse.bass as bass
import concourse.mybir as mybir
from concourse.bass_utils import run_bass_kernel_spmd

F32 = mybir.dt.float32
BF16 = mybir.dt.bfloat16
AF = mybir.ActivationFunctionType
ALU = mybir.AluOpType

DEPTH = 2
D = 1024
NT = 20
NPB = 16
NS = 4
PAST = 2048
ALPHA = (2 * DEPTH) ** 0.25
LN_EPS = 1e-5
RMS_EPS = 1e-6
GC = math.sqrt(2.0 / math.pi)
WTOK = 2054
WFEAT = 1536
GROUPS = [[0, 1], [2, 3], [4, 5], [6, 7]]
FLAGS = {"c1skew": True, "c2skew": True, "nstream": 4, "na": 1}


class Buf:
    __slots__ = ("w", "r")

    def __init__(self):
        self.w = None
        self.r = {}


class Eng:
    def __init__(self, name, h):
        self.name = name
        self.h = h
        self.sid = None
        self.n = 0
        self.epoch = 0
        self.seen = {}


class Sched:
    NDMA = 24
    LIMIT = 12000

    def __init__(self, nc, stack):
        self.nc = nc
        self.stack = stack
        self.sems = {}
        self.bufs = {}
        self.pe = Eng("pe", nc.tensor)
        self.act = Eng("act", nc.scalar)
        self.dve = Eng("dve", nc.vector)
        self.pool = Eng("pool", nc.gpsimd)
        self.sp = Eng("sp", nc.sync)
        self.engs = [self.pe, self.act, self.dve, self.pool, self.sp]
        self.last = {}
        for e in self.engs:
            self._new_epoch(e)
        self.dsem = [self._mk(f"d{i}") for i in range(self.NDMA)]
        self.duse = [0] * self.NDMA
        self.dnext = 0
        self.cc_n = 0
        self.cc_toks = []

    def _mk(self, name):
        s = self.stack.enter_context(self.nc.semaphore("s_" + name))
        self.sems[name] = s
        return s

    def _new_epoch(self, e):
        if e.sid is not None:
            self.last[e.sid] = e.n
        e.sid = f"{e.name}{e.epoch}"
        e.epoch += 1
        e.n = 0
        self._mk(e.sid)

    def B(self, *key):
        b = self.bufs.get(key)
        if b is None:
            b = Buf()
            self.bufs[key] = b
        return b

    def _wait(self, eng, tok):
        if tok is None:
            return
        sid, val = tok
        if eng.seen.get(sid, 0) >= val:
            return
        eng.h.wait_ge(self.sems[sid], val)
        eng.seen[sid] = val

    def _own(self, eng, sid):
        return sid.startswith(eng.name) and sid[len(eng.name):].isdigit()

    def _pre(self, eng, r, w):
        for b in r:
            self._wait(eng, b.w)
        for b in w:
            if b.w is not None and not self._own(eng, b.w[0]):
                self._wait(eng, b.w)
            for t in b.r.items():
                if not self._own(eng, t[0]):
                    self._wait(eng, t)

    def _post(self, tok, r, w):
        for b in r:
            if b.r.get(tok[0], 0) < tok[1]:
                b.r[tok[0]] = tok[1]
        for b in w:
            b.w = tok
            b.r = {}

    def _tick(self, eng, ins):
        if eng.n >= self.LIMIT:
            self._new_epoch(eng)
        eng.n += 1
        ins.then_inc(self.sems[eng.sid], 1)
        return (eng.sid, eng.n)

    def op(self, eng, fn, r=(), w=()):
        self._pre(eng, r, w)
        tok = self._tick(eng, fn(eng.h))
        self._post(tok, r, w)

    def group(self, eng, fns, r=(), w=()):
        self._pre(eng, r, w)
        ins = None
        for fn in fns:
            ins = fn(eng.h)
        tok = self._tick(eng, ins)
        self._post(tok, r, w)

    def dma(self, q, out, in_, r=(), w=()):
        i = self.dnext
        self.dnext = (self.dnext + 1) % self.NDMA
        sid = f"d{i}"
        if self.duse[i] > 0:
            self._wait(q, (sid, 16 * self.duse[i]))
        self._pre(q, r, w)
        q.h.dma_start(out=out, in_=in_).then_inc(self.dsem[i], 16)
        self.duse[i] += 1
        self._post((sid, 16 * self.duse[i]), r, w)

    def cc(self, ins, outs, r=(), w=()):
        q = self.pool
        self._pre(q, r, w)
        self.cc_n += 1
        name = f"cc{self.cc_n}"
        sem = self._mk(name)
        q.h.collective_compute("AllGather", ALU.bypass, replica_groups=GROUPS,
                               ins=[ins], outs=[outs]).then_inc(sem)
        tok = (name, 1)
        self._post(tok, r, w)
        self.cc_toks.append(tok)
        self._wait(q, tok)

    def barrier(self):
        toks = [(e.sid, e.n) for e in self.engs if e.n > 0]
        toks += list(self.last.items())
        toks += [(f"d{i}", 16 * self.duse[i]) for i in range(self.NDMA) if self.duse[i] > 0]
        toks += self.cc_toks
        for e in self.engs:
            for t in toks:
                if t[1] > 0 and not self._own(e, t[0]):
                    self._wait(e, t)
        for b in self.bufs.values():
            b.w = None
            b.r = {}


def build_nc():
    nc = bass.Bass("TRN2", target_bir_lowering=False)

    def din(name, shape, dt=F32):
        return nc.dram_tensor(name, list(shape), dt, kind="ExternalInput").ap()

    def dout(name, shape):
        return nc.dram_tensor(name, list(shape), F32, kind="ExternalOutput").ap()

    def dint(name, shape, dt):
        return nc.dram_tensor(name, list(shape), dt)

    xin = din("xin", [NT, 128, D])
    cfk = din("cfk", [DEPTH, NS, PAST, 384])
    cfv = din("cfv", [DEPTH, NS, PAST, 384])
    csk = din("csk", [DEPTH, NS, PAST, 384])
    csv = din("csv", [DEPTH, NS, PAST, 384])
    cfl = din("cfl", [DEPTH, NS, PAST, 6])
    w_tok = din("w_tok", [DEPTH, D, WTOK])
    w_feat = din("w_feat", [DEPTH, D, WFEAT])
    b_f = din("b_f", [DEPTH, 6])
    g_v = din("g_v", [DEPTH, 256])
    b_v = din("b_v", [DEPTH, 256])
    w_sT = din("w_sT", [DEPTH, 4, 128, 128])
    b_sT = din("b_sT", [DEPTH, 128, 4])
    g_mixT = din("g_mixT", [DEPTH, 128, 8])
    w_out = din("w_out", [DEPTH, D, D])
    ln1_g = din("ln1_g", [DEPTH, D])
    ln1_b = din("ln1_b", [DEPTH, D])
    ln2_g = din("ln2_g", [DEPTH, D])
    ln2_b = din("ln2_b", [DEPTH, D])
    w_up = din("w_up", [DEPTH, D, 4 * D])
    w_down = din("w_down", [DEPTH, 4 * D, D])
    cst = din("cst", [9, 128, 128])
    sel = din("sel", [128, 2])

    y = dout("y", [NT, 128, D])
    okf = dout("okf", [DEPTH, NT, 128, 384])
    ovf = dout("ovf", [DEPTH, NT, 128, 384])
    oks = dout("oks", [DEPTH, NT, 128, 384])
    ovs = dout("ovs", [DEPTH, NT, 128, 384])
    olf = dout("olf", [DEPTH, NT, 128, 6])
    ogv = dout("ogv", [DEPTH, NS, 128, 256])

    xmid = dint("xmid", [NT, 128, D], F32)
    xl1 = dint("xl1", [NT, 128, D], F32)
    kTf_in = [dint(f"kTf_in{l}", [384, 2048], BF16) for l in range(DEPTH)]
    kTs_in = [dint(f"kTs_in{l}", [384, 2048], BF16) for l in range(DEPTH)]
    vf_in = [dint(f"vf_in{l}", [2048, 390], BF16) for l in range(DEPTH)]
    vs_in = [dint(f"vs_in{l}", [2048, 390], BF16) for l in range(DEPTH)]
    lf_in = [dint(f"lf_in{l}", [2048, 6], F32) for l in range(DEPTH)]
    kTf_g = [dint(f"kTf_g{l}", [768, 2048], BF16) for l in range(DEPTH)]
    kTs_g = [dint(f"kTs_g{l}", [768, 2048], BF16) for l in range(DEPTH)]
    vf_g = [dint(f"vf_g{l}", [4096, 390], BF16) for l in range(DEPTH)]
    vs_g = [dint(f"vs_g{l}", [4096, 390], BF16) for l in range(DEPTH)]
    lf_g = [dint(f"lf_g{l}", [4096, 6], F32) for l in range(DEPTH)]

    with ExitStack() as top:
        S = Sched(nc, top)
        B = S.B
        PE, ACT, DVE, POOL, SP = S.pe, S.act, S.dve, S.pool, S.sp

        uniq = [0]

        def T(stack, name, shape, dt):
            uniq[0] += 1
            return stack.enter_context(nc.sbuf_tensor(f"{name}_{uniq[0]}", list(shape), dt))

        def PS(name, shape, dt):
            return top.enter_context(nc.psum_tensor(name, list(shape), dt))

        pk = [PS(f"pk{i}", [128, 512], F32) for i in range(8)]
        pA, pB, pC, pM, pO0, pO1, pT0f, pT1f = pk
        pT0 = pT0f[:].bitcast(BF16)
        pT1 = pT1f[:].bitcast(BF16)

        cstf = T(top, "cstf", [128, 9, 128], F32)
        cstb = T(top, "cstb", [128, 9, 128], BF16)
        ones512 = T(top, "ones512", [128, 514], F32)
        onesf = T(top, "onesf", [128, 128], F32)
        selt = T(top, "selt", [128, 2], F32)
        S.dma(SP, cstf[:], cst.rearrange("c p q -> p c q"), w=[B("cstf")])
        S.dma(SP, selt[:], sel[:, :], w=[B("selt")])
        S.op(POOL, lambda h: h.tensor_copy(out=cstb[:], in_=cstf[:]), r=[B("cstf")], w=[B("cstb")])
        S.op(DVE, lambda h: h.memset(ones512[:], 1.0), w=[B("ones512")])
        S.op(DVE, lambda h: h.memset(onesf[:], 1.0), w=[B("onesf")])
        identf = cstf[:, 0, :]
        identb = cstb[:, 0, :]
        Uf = cstf[:, 1, :]
        CONST_R = [B("cstf"), B("cstb"), B("ones512"), B("onesf"), B("selt")]

        rot = {}

        def nxt(key, n):
            v = rot.get(key, 0) % n
            rot[key] = (v + 1) % n
            return v

        cast_engs = [POOL, DVE, ACT]

        def cast_op(eng, out, in_, scale=None):
            if scale is not None:
                if eng is ACT:
                    return lambda h: h.activation(out=out, in_=in_, func=AF.Copy, scale=scale)
                return lambda h: h.tensor_scalar(out=out, in0=in_, scalar1=scale, scalar2=None, op0=ALU.mult)
            if eng is ACT:
                return lambda h: h.copy(out=out, in_=in_)
            return lambda h: h.tensor_copy(out=out, in_=in_)

        def load_cast(stg, dst, src, ncols, wb, scale=None, engs=None, view=None):
            i = nxt("stg", len(stg))
            sv = stg[i][:, 0:ncols]
            S.dma(SP, view(sv) if view else sv, src, w=[B("stg", i)])
            engs = engs or cast_engs
            e = engs[nxt("casteng", len(engs))]
            S.op(e, cast_op(e, dst, stg[i][:, 0:ncols], scale), r=[B("stg", i)] + CONST_R, w=[wb])

        def transposes(pt, pbuf, src_fn, nblk, rows=128, r=()):
            fns = []
            for j in range(nblk):
                fns.append((lambda j: lambda h: h.transpose(out=pt[:, j * 128:j * 128 + rows], in_=src_fn(j),
                                                            identity=identb[0:rows, 0:rows]))(j))
            S.group(PE, fns, r=list(r) + [B("cstb")], w=[pbuf])

        def rstd_from(var_ap, out_ap, tmp_ap, scale, eps, bufs_r, buf_w):
            S.op(ACT, lambda h: h.activation(out=tmp_ap, in_=var_ap, func=AF.Ln, bias=eps, scale=scale),
                 r=bufs_r, w=[buf_w])
            S.op(ACT, lambda h: h.activation(out=out_ap, in_=tmp_ap, func=AF.Exp, scale=-0.5),
                 r=[buf_w], w=[buf_w])

        def layer_norm_tile(res, outt, stats, gB, bB, rb, ob, stb, constb):
            S.op(DVE, lambda h: h.bn_stats(out=stats[:, 0:6], in_=res[:, 0:512]), r=[rb], w=[stb])
            S.op(DVE, lambda h: h.bn_stats(out=stats[:, 6:12], in_=res[:, 512:1024]), r=[rb], w=[stb])
            S.op(DVE, lambda h: h.bn_aggr(out=stats[:, 12:14], in_=stats[:, 0:12]), r=[stb], w=[stb])
            rstd_from(stats[:, 13:14], stats[:, 14:15], stats[:, 15:16], 1.0, LN_EPS, [stb], stb)
            S.op(DVE, lambda h: h.tensor_scalar(out=outt, in0=res, scalar1=stats[:, 12:13], scalar2=stats[:, 14:15],
                                                op0=ALU.subtract, op1=ALU.mult), r=[rb, stb], w=[ob])
            S.op(POOL, lambda h: h.tensor_tensor(out=outt, in0=outt, in1=gB, op=ALU.mult), r=[ob, constb], w=[ob])
            S.op(POOL, lambda h: h.tensor_tensor(out=outt, in0=outt, in1=bB, op=ALU.add), r=[ob, constb], w=[ob])

        NSLOT = 4
        SBANK = [(pA, ("pS", 0)), (pB, ("pS", 1)), (pC, ("pS", 2)), (pM, ("pM",))]
        POBANK = [(pO0, ("pO", 0)), (pO1, ("pO", 1)), (pT0f, ("pT", 0)), (pT1f, ("pT", 1))]

        def attend(slot, kind, M, qT_ap, KT_fn, V_fn, nkb, mask_f, mask_b, maskw, cq_ap, ckB, ckb_key, W, out_ap, uid):
            ps_ = slice(0, M)
            chunks = []
            hi = nkb
            first = True
            while hi > 0:
                if first and maskw == 128:
                    lo = hi - 1
                else:
                    lo = max(0, hi - 4)
                chunks.append((lo, hi))
                hi = lo
                first = False
            po = POBANK[slot][0][:, 0:65]
            pob = B(*POBANK[slot][1])
            psb, pskey = SBANK[slot]
            psB = B(*pskey)
            pt = psb[:].bitcast(BF16)[:, 0:512]
            ptB = psB
            t1 = W["t1"][slot]
            e = W["e"][slot]
            lb = W["l"][slot]
            pin = W["pin"][slot]
            a = W["a"][slot]
            aT = W["aT"][slot]
            car = W["carry"]
            kB = lambda nm: B(nm, slot)
            if kind == "sb":
                S.op(POOL, lambda h: h.memset(car[:, slot, 0:1], 0.0), w=[B("carry", slot, 0)])
                yield
                cprev = 0
            nmm = 0
            for ci, (lo, hi) in enumerate(chunks):
                nb = hi - lo
                w = nb * 128
                S.group(PE, [lambda h: h.matmul(psb[ps_, 0:w], lhsT=qT_ap, rhs=KT_fn(lo, hi), start=True, stop=True)],
                        r=[B("qT"), B("KT", uid[0])], w=[psB])
                yield
                top_chunk = ci == 0
                if kind == "fox":
                    S.op(DVE, lambda h: h.scalar_tensor_tensor(
                        out=t1[ps_, 0:w], in0=psb[ps_, 0:w], scalar=0.125, in1=ckB[ps_, lo * 128:hi * 128],
                        op0=ALU.mult, op1=ALU.subtract), r=[psB, B(*ckb_key)], w=[kB("t1")])
                    yield
                    S.op(ACT, lambda h: h.activation(out=a[ps_, 0:w], in_=t1[ps_, 0:w], func=AF.Exp, bias=cq_ap),
                         r=[kB("t1"), B("cq")], w=[kB("a")])
                    yield
                else:
                    S.op(ACT, lambda h: h.activation(out=e[ps_, 0:w], in_=psb[ps_, 0:w], func=AF.Exp, scale=0.125),
                         r=[psB], w=[kB("e")])
                    yield
                    S.op(ACT, lambda h: h.activation(out=lb[ps_, 1:w + 1], in_=e[ps_, 0:w], func=AF.Ln, bias=1.0),
                         r=[kB("e")], w=[kB("l")])
                    yield
                    if top_chunk:
                        S.op(POOL, lambda h: h.tensor_tensor(out=lb[ps_, 1 + w - maskw:1 + w], in0=lb[ps_, 1 + w - maskw:1 + w],
                                                             in1=mask_f, op=ALU.mult), r=[kB("l"), B("cstf"), B("cstf2")], w=[kB("l")])
                        yield
                    S.op(DVE, lambda h: h.tensor_tensor_scan(
                        out=pin[ps_, 0:w + 1], data0=ones512[ps_, 0:w + 1], data1=lb[ps_, 0:w + 1], initial=0.0,
                        op0=ALU.mult, op1=ALU.add), r=[kB("l"), B("ones512")], w=[kB("pin")])
                    yield
                    cnew = 1 - cprev
                    S.op(POOL, lambda h: h.tensor_tensor(out=car[ps_, slot, cnew:cnew + 1], in0=car[ps_, slot, cprev:cprev + 1],
                                                         in1=pin[ps_, w:w + 1], op=ALU.subtract),
                         r=[B("carry", slot, cprev), kB("pin")], w=[B("carry", slot, cnew)])
                    yield
                    S.op(DVE, lambda h: h.scalar_tensor_tensor(
                        out=t1[ps_, 0:w], in0=psb[ps_, 0:w], scalar=0.125, in1=pin[ps_, 0:w],
                        op0=ALU.mult, op1=ALU.add), r=[psB, kB("pin")], w=[kB("t1")])
                    yield
                    S.op(ACT, lambda h: h.activation(out=a[ps_, 0:w], in_=t1[ps_, 0:w], func=AF.Exp, bias=car[ps_, slot, cnew:cnew + 1]),
                         r=[kB("t1"), B("carry", slot, cnew)], w=[kB("a")])
                    yield
                    cprev = cnew
                if top_chunk:
                    S.op(POOL, lambda h: h.tensor_tensor(out=a[ps_, w - maskw:w], in0=a[ps_, w - maskw:w], in1=mask_b, op=ALU.mult),
                         r=[kB("a"), B("cstb"), B("cstb2")], w=[kB("a")])
                    yield
                transposes(pt, ptB, lambda j: a[ps_, j * 128:(j + 1) * 128], nb, rows=M, r=[kB("a")])
                yield
                if M == 128:
                    S.op(ACT, lambda h: h.copy(out=aT[:, 0:w], in_=pt[:, 0:w]), r=[ptB], w=[kB("aT")])
                else:
                    S.op(ACT, lambda h: h.copy(
                        out=aT[:, 0:nb * 128].rearrange("p (b q) -> p b q", q=128)[:, :, 0:M],
                        in_=pt[:, 0:nb * 128].rearrange("p (b q) -> p b q", q=128)[:, :, 0:M]), r=[ptB], w=[kB("aT")])
                yield
                fns = []
                ncol = 65 if kind == "fox" else 64
                for j in range(nb):
                    st = nmm == 0
                    sp_ = nmm == nkb - 1
                    fns.append((lambda j, st, sp_: lambda h: h.matmul(
                        po[ps_, 0:ncol], lhsT=aT[:, j * 128:j * 128 + M], rhs=V_fn(lo + j)[:, 0:ncol],
                        start=st, stop=sp_))(j, st, sp_))
                    nmm += 1
                S.group(PE, fns, r=[kB("aT"), B("V", uid[1])], w=[pob])
                yield
            on = W["on"][slot]
            sq = W["sq"][slot]
            ss = W["ss"]
            eb = kB("ep")
            if kind == "fox":
                S.op(DVE, lambda h: h.reciprocal(out=ss[ps_, slot, 0:1], in_=po[ps_, 64:65]), r=[pob], w=[eb])
                yield
                S.op(DVE, lambda h: h.tensor_scalar(out=on[ps_, :], in0=po[ps_, 0:64], scalar1=ss[ps_, slot, 0:1], scalar2=None,
                                                    op0=ALU.mult), r=[pob, eb], w=[eb])
            else:
                S.op(DVE, lambda h: h.tensor_copy(out=on[ps_, :], in_=po[ps_, 0:64]), r=[pob], w=[eb])
            yield
            S.op(POOL, lambda h: h.memset(ss[ps_, slot, 1:2], 0.0), w=[eb])
            yield
            S.op(ACT, lambda h: h.activation(out=sq[ps_, :], in_=on[ps_, :], func=AF.Square, accum_out=ss[ps_, slot, 1:2]),
                 r=[eb], w=[eb])
            yield
            S.op(ACT, lambda h: h.activation(out=ss[ps_, slot, 2:3], in_=ss[ps_, slot, 1:2], func=AF.Ln, bias=RMS_EPS, scale=1.0 / 64.0),
                 r=[eb], w=[eb])
            yield
            S.op(ACT, lambda h: h.activation(out=ss[ps_, slot, 3:4], in_=ss[ps_, slot, 2:3], func=AF.Exp, scale=-0.5), r=[eb], w=[eb])
            yield
            S.op(DVE, lambda h: h.tensor_scalar(out=out_ap, in0=on[ps_, :], scalar1=ss[ps_, slot, 3:4], scalar2=None,
                                                op0=ALU.mult), r=[eb], w=[B("oatt")])
            yield

        def run_streams(tasks, n):
            active = []
            free = list(range(n))
            i = 0
            while i < len(tasks) or active:
                while i < len(tasks) and (free or tasks[i][0] != "task"):
                    kind_, f = tasks[i]
                    if kind_ == "now":
                        f()
                    elif kind_ == "setup":
                        if active:
                            break
                        f()
                    else:
                        slot = free.pop(0)
                        active.append((slot, f(slot)))
                    i += 1
                for item in list(active):
                    try:
                        next(item[1])
                    except StopIteration:
                        active.remove(item)
                        free.append(item[0])
                        free.sort()

        def c_compute(lfT, nblk, order, W, M=128):
            n = nblk * 6
            lf2 = lfT[:].rearrange("p b h -> p (b h)")
            S.group(PE, [lambda h: h.matmul(pM[:, 0:n], lhsT=Uf, rhs=lf2, start=True, stop=True)],
                    r=[B("lfT"), B("cstf")], w=[B("pM")])
            S.op(ACT, lambda h: h.copy(out=W["cw"][:, 0:n], in_=pM[:, 0:n]), r=[B("pM")], w=[B("cw")])
            S.group(PE, [lambda h: h.matmul(pM[:, 0:n], lhsT=onesf[:], rhs=lf2, start=True, stop=True)],
                    r=[B("lfT"), B("onesf")], w=[B("pM")])
            S.op(ACT, lambda h: h.copy(out=W["tot"][:, 0:n], in_=pM[:, 0:n]), r=[B("pM")], w=[B("tot")])
            offs = W["offs"]
            S.op(DVE, lambda h: h.memset(offs[:, order[0] * 6:order[0] * 6 + 6], 0.0), w=[B("offs")])
            for gi in range(1, nblk):
                a, b_ = order[gi], order[gi - 1]
                S.op(DVE, lambda h, a=a, b_=b_: h.tensor_tensor(out=offs[:, a * 6:a * 6 + 6], in0=offs[:, b_ * 6:b_ * 6 + 6],
                                                                in1=W["tot"][:, b_ * 6:b_ * 6 + 6], op=ALU.add),
                     r=[B("offs"), B("tot")], w=[B("offs")])
            S.op(DVE, lambda h: h.tensor_tensor(out=W["cT"][:, 0:n], in0=W["cw"][:, 0:n], in1=offs[:, 0:n], op=ALU.add),
                 r=[B("cw"), B("offs")], w=[B("cT")])

        def ckB_build(W, ckB, ckb_key, h6, order, nblk, M):
            g = 0
            while g < nblk:
                nb = min(4, nblk - g)
                ci = nxt("cexp", 2)
                cx = W["cexp"][ci]
                for jj in range(nb):
                    slot = order[g + jj]
                    S.op(POOL, lambda h, cx=cx, jj=jj, slot=slot: h.tensor_tensor(
                        out=cx[:, jj * 128:(jj + 1) * 128], in0=identf,
                        in1=W["cT"][:, slot * 6 + h6:slot * 6 + h6 + 1].to_broadcast([128, 128]), op=ALU.mult),
                        r=[B("cT"), B("cstf")], w=[B("cexp", ci)])
                S.group(PE, [lambda h, cx=cx, nb=nb: h.matmul(pM[0:M, 0:nb * 128], lhsT=onesf[:, 0:M], rhs=cx[:, 0:nb * 128],
                                                               start=True, stop=True)],
                        r=[B("cexp", ci), B("onesf")], w=[B("pM")])
                S.op(ACT, lambda h, g=g, nb=nb: h.copy(out=ckB[0:M, g * 128:(g + nb) * 128], in_=pM[0:M, 0:nb * 128]),
                     r=[B("pM")], w=[B(*ckb_key)])
                g += nb

        def attn_work(stack):
            W = {}
            for nm in ["t1", "e"]:
                W[nm] = [T(stack, f"w_{nm}{i}", [128, 512], F32) for i in range(NSLOT)]
            for nm in ["l", "pin"]:
                W[nm] = [T(stack, f"w_{nm}{i}", [128, 514], F32) for i in range(NSLOT)]
            for i in range(NSLOT):
                S.op(POOL, lambda h, i=i: h.memset(W["l"][i][:, 0:1], 0.0), w=[B("l", i)])
            W["a"] = [T(stack, f"w_a{i}", [128, 512], BF16) for i in range(NSLOT)]
            W["aT"] = [T(stack, f"w_aT{i}", [128, 512], BF16) for i in range(NSLOT)]
            W["carry"] = T(stack, "w_carry", [128, NSLOT, 4], F32)
            W["on"] = [T(stack, f"w_on{i}", [128, 64], F32) for i in range(NSLOT)]
            W["sq"] = [T(stack, f"w_sq{i}", [128, 64], F32) for i in range(NSLOT)]
            W["ss"] = T(stack, "w_ss", [128, NSLOT, 4], F32)
            W["cw"] = T(stack, "w_cw", [128, 192], F32)
            W["tot"] = T(stack, "w_tot", [128, 192], F32)
            W["offs"] = T(stack, "w_offs", [128, 192], F32)
            W["cT"] = T(stack, "w_cT", [128, 192], F32)
            W["cexp"] = [T(stack, f"w_cexp{i}", [128, 512], F32) for i in range(2)]
            return W

        for l in range(DEPTH):
            xsrc = xin if l == 0 else xl1.ap()
            xdst = xl1.ap() if l == 0 else y
            with ExitStack() as L1:
                qT = T(L1, "qT", [128, 6, NT * 128], BF16)
                og = T(L1, "og", [128, NT, 256], BF16)
                svnew = T(L1, "svnew", [128, NS, 780], BF16)
                slfnew = T(L1, "slfnew", [128, NS, 6], F32)
                skTn = T(L1, "skTn", [128, 6, 512], BF16)
                S.op(POOL, lambda h: h.memset(svnew[:], 1.0), w=[B("svnew")])
                gather_r = []
                with ExitStack() as A:
                    wtok = T(A, "wtok", [128, 8, WTOK], BF16)
                    wfeat = T(A, "wfeat", [128, 8, WFEAT], BF16)
                    with ExitStack() as A0:
                        stg = [T(A0, f"stgA{i}", [128, WTOK], F32) for i in range(2)]
                        for kc in range(8):
                            load_cast(stg, wtok[:, kc, :], w_tok[l, kc * 128:(kc + 1) * 128, :], WTOK, B("wtok", kc))
                            load_cast(stg, wfeat[:, kc, :], w_feat[l, kc * 128:(kc + 1) * 128, :], WFEAT, B("wfeat", kc))
                        S.barrier()
                    NA = 2
                    xt = [T(A, f"xtA{i}", [128, D], F32) for i in range(2)]
                    xb = [T(A, f"xbA{i}", [128, D], BF16) for i in range(2)]
                    xT = [T(A, f"xTA{i}", [128, 8, 512], BF16) for i in range(3)]
                    kvout = [T(A, f"kvout{i}", [128, 1536], F32) for i in range(NA)]
                    vaug = [T(A, f"vaug{i}", [128, 780], BF16) for i in range(NA)]
                    kst = [T(A, f"kst{i}", [128, 512], BF16) for i in range(3)]
                    gxs = [T(A, f"gx{i}", [128, 512], F32) for i in range(NA)]
                    g2s = [T(A, f"g2{i}", [128, 512], F32) for i in range(NA)]
                    ges = [T(A, f"ge{i}", [128, 512], F32) for i in range(NA)]
                    gls = [T(A, f"gl{i}", [128, 512], F32) for i in range(NA)]
                    vns = [T(A, f"vn{i}", [128, 256], F32) for i in range(NA)]
                    vbs = [T(A, f"vb{i}", [128, 256], BF16) for i in range(NA)]
                    sgbs = [T(A, f"sgb{i}", [128, 256], F32) for i in range(NA)]
                    lfw = T(A, "lfw", [128, NA, 24], F32)
                    sgsts = [T(A, f"sgst{i}", [128, 16], F32) for i in range(NA)]
                    bfB = T(A, "bfB", [128, 6], F32)
                    gvB = T(A, "gvB", [128, 256], F32)
                    bvB = T(A, "bvB", [128, 256], F32)
                    wsf = T(A, "wsf", [128, 4, 128], F32)
                    WsT = T(A, "WsT", [128, 4, 128], BF16)
                    WsTs = T(A, "WsTs", [128, 4, 128], BF16)
                    bsT_t = T(A, "bsT_t", [128, 4], F32)
                    bsB = T(A, "bsB", [128, 256], F32)
                    for i in range(NA):
                        S.op(POOL, lambda h, i=i: h.memset(vaug[i][:], 1.0), w=[B("vaug", i)])
                    S.dma(SP, bfB[:], b_f[l:l + 1, :].partition_broadcast(128), w=[B("cA")])
                    S.dma(SP, gvB[:], g_v[l:l + 1, :].partition_broadcast(128), w=[B("cA")])
                    S.dma(SP, bvB[:], b_v[l:l + 1, :].partition_broadcast(128), w=[B("cA")])
                    S.dma(SP, wsf[:], w_sT[l].rearrange("g j i -> j g i"), w=[B("wsf")])
                    S.dma(SP, bsT_t[:], b_sT[l], w=[B("bsT")])
                    S.op(POOL, lambda h: h.tensor_tensor(out=WsT[:], in0=wsf[:], in1=cstf[:, 1:2, :].to_broadcast([128, 4, 128]),
                                                         op=ALU.mult), r=[B("wsf"), B("cstf")], w=[B("cA")])
                    S.op(POOL, lambda h: h.tensor_tensor(out=WsTs[:], in0=wsf[:], in1=cstf[:, 2:3, :].to_broadcast([128, 4, 128]),
                                                         op=ALU.mult), r=[B("wsf"), B("cstf")], w=[B("cA")])
                    S.op(POOL, lambda h: h.tensor_copy(out=bsB[:].rearrange("p (g c) -> p g c", c=64),
                                                       in_=bsT_t[:].unsqueeze(2).to_broadcast([128, 4, 64])),
                         r=[B("bsT")], w=[B("cA")])
                    WTOK_R = [B("wtok", kc) for kc in range(8)]
                    WFEAT_R = [B("wfeat", kc) for kc in range(8)]

                    def a_prep(t):
                        g, tl = t // 4, t % 4
                        gb = g % 3
                        bi = nxt("xtA", 2)
                        S.dma(SP, xt[bi][:], xsrc[t], w=[B("xtA", bi)])
                        S.op(POOL, lambda h: h.tensor_copy(out=xb[bi][:], in_=xt[bi][:]), r=[B("xtA", bi)], w=[B("xbA", bi)])
                        ti = nxt("pt", 2)
                        pt = [pT0, pT1][ti]
                        transposes(pt, B("pT", ti), lambda j: xb[bi][:, j * 128:(j + 1) * 128], 8, r=[B("xbA", bi)])
                        S.op(ACT, lambda h: h.copy(out=xT[gb][:, :, tl * 128:(tl + 1) * 128],
                                                   in_=pt[:].rearrange("p (k q) -> p k q", q=128)),
                             r=[B("pT", ti)], w=[B("xTA", gb, tl)])

                    def a_compute(slot, t):
                        g, tl = t // 4, t % 4
                        gb = g % 3
                        samp = t >= NPB
                        s_i = t - NPB
                        gx, g2, ge, gl = gxs[slot], g2s[slot], ges[slot], gls[slot]
                        vn, vb, sgb, sgst = vns[slot], vbs[slot], sgbs[slot], sgsts[slot]
                        kB = lambda nm: B(nm, "A", slot)
                        kvb = kB("kvout")
                        for ci, (c0, c1) in enumerate([(0, 384), (384, 768), (768, 1152), (1152, 1536), (1536, 2048), (2048, 2054)]):
                            si = nxt("psA", 3)
                            psb = [pA, pB, pC][si]
                            psB = B("pS", si)
                            wd = c1 - c0
                            S.group(PE, [(lambda kc: lambda h: h.matmul(psb[:, 0:wd], lhsT=xT[gb][:, kc, tl * 128:(tl + 1) * 128],
                                                                         rhs=wtok[:, kc, c0:c1], start=kc == 0, stop=kc == 7))(kc)
                                         for kc in range(8)], r=[B("xTA", gb, tl)] + WTOK_R, w=[psB])
                            yield
                            if ci < 4:
                                e = ACT if ci % 2 == 0 else DVE
                                S.op(e, cast_op(e, kvout[slot][:, c0:c1], psb[:, 0:wd]), r=[psB], w=[kvb])
                                yield
                            elif ci == 4:
                                S.op(ACT, lambda h: h.copy(out=gx[:], in_=psb[:, 0:512]), r=[psB], w=[kB("gx")])
                                yield
                                S.op(POOL, lambda h: h.tensor_tensor(out=g2[:], in0=gx[:], in1=gx[:], op=ALU.mult), r=[kB("gx")], w=[kB("g2")])
                                yield
                                S.op(POOL, lambda h: h.tensor_scalar(out=g2[:], in0=g2[:], scalar1=0.044715, scalar2=1.0,
                                                                     op0=ALU.mult, op1=ALU.add), r=[kB("g2")], w=[kB("g2")])
                                yield
                                S.op(POOL, lambda h: h.tensor_tensor(out=g2[:], in0=g2[:], in1=gx[:], op=ALU.mult),
                                     r=[kB("g2"), kB("gx")], w=[kB("g2")])
                                yield
                                S.op(ACT, lambda h: h.activation(out=ge[:], in_=g2[:], func=AF.Exp, scale=-2.0 * GC), r=[kB("g2")], w=[kB("ge")])
                                yield
                                S.op(DVE, lambda h: h.tensor_scalar(out=ge[:], in0=ge[:], scalar1=1.0, scalar2=None, op0=ALU.add),
                                     r=[kB("ge")], w=[kB("ge")])
                                yield
                                S.op(DVE, lambda h: h.reciprocal(out=ge[:], in_=ge[:]), r=[kB("ge")], w=[kB("ge")])
                                yield
                                S.op(POOL, lambda h: h.tensor_tensor(out=gl[:], in0=gx[:], in1=ge[:], op=ALU.mult),
                                     r=[kB("gx"), kB("ge")], w=[kB("gl")])
                                yield
                                S.op(DVE, lambda h: h.bn_stats(out=sgst[:, 0:6], in_=gl[:, 256:512]), r=[kB("gl")], w=[kB("sgst")])
                                yield
                                S.op(DVE, lambda h: h.bn_aggr(out=sgst[:, 6:8], in_=sgst[:, 0:6]), r=[kB("sgst")], w=[kB("sgst")])
                                yield
                                S.op(ACT, lambda h: h.activation(out=sgst[:, 9:10], in_=sgst[:, 7:8], func=AF.Ln, bias=LN_EPS), r=[kB("sgst")], w=[kB("sgst")])
                                yield
                                S.op(ACT, lambda h: h.activation(out=sgst[:, 8:9], in_=sgst[:, 9:10], func=AF.Exp, scale=-0.5), r=[kB("sgst")], w=[kB("sgst")])
                                yield
                                S.op(DVE, lambda h: h.tensor_scalar(out=vn[:], in0=gl[:, 256:512], scalar1=sgst[:, 6:7],
                                                                    scalar2=sgst[:, 8:9], op0=ALU.subtract, op1=ALU.mult),
                                     r=[kB("gl"), kB("sgst")], w=[kB("vn")])
                                yield
                                S.op(POOL, lambda h: h.tensor_tensor(out=vn[:], in0=vn[:], in1=gvB[:], op=ALU.mult), r=[kB("vn"), B("cA")], w=[kB("vn")])
                                yield
                                S.op(POOL, lambda h: h.tensor_tensor(out=vn[:], in0=vn[:], in1=bvB[:], op=ALU.add), r=[kB("vn"), B("cA")], w=[kB("vn")])
                                yield
                                if samp:
                                    S.dma(POOL, ogv[l, s_i], vn[:], r=[kB("vn")], w=[B("ogv", l, s_i)])
                                S.op(POOL, lambda h: h.tensor_copy(out=vb[:], in_=vn[:]), r=[kB("vn")], w=[kB("vb")])
                                yield
                                Wm = WsTs if samp else WsT
                                S.group(PE, [(lambda gg: lambda h: h.matmul(pM[:, gg * 64:(gg + 1) * 64], lhsT=Wm[:, gg, :],
                                                                             rhs=vb[:, gg * 64:(gg + 1) * 64], start=True, stop=True))(gg)
                                             for gg in range(4)], r=[kB("vb"), B("cA")], w=[B("pM")])
                                yield
                                S.op(DVE, lambda h: h.tensor_tensor(out=sgb[:], in0=pM[:, 0:256], in1=bsB[:], op=ALU.add),
                                     r=[B("pM"), B("cA")], w=[kB("sgb")])
                                yield
                                S.op(POOL, lambda h: h.tensor_tensor(out=sgb[:], in0=sgb[:], in1=gl[:, 0:256], op=ALU.mult),
                                     r=[kB("sgb"), kB("gl")], w=[kB("sgb")])
                                yield
                                S.op(POOL, lambda h: h.tensor_tensor(out=g2[:, 0:256], in0=sgb[:], in1=sgb[:], op=ALU.mult),
                                     r=[kB("sgb"), kB("g2")], w=[kB("g2")])
                                yield
                                S.op(DVE, lambda h: h.reduce_sum(out=sgst[:, 10:14], in_=g2[:, 0:256].rearrange("p (g c) -> p g c", c=64),
                                                                 axis=mybir.AxisListType.X), r=[kB("g2"), kB("sgst")], w=[kB("sgst")])
                                yield
                                S.op(ACT, lambda h: h.activation(out=sgst[:, 10:14], in_=sgst[:, 10:14], func=AF.Ln, bias=RMS_EPS,
                                                                 scale=1.0 / 64.0), r=[kB("sgst")], w=[kB("sgst")])
                                yield
                                S.op(ACT, lambda h: h.activation(out=sgst[:, 10:14], in_=sgst[:, 10:14], func=AF.Exp, scale=-0.5),
                                     r=[kB("sgst")], w=[kB("sgst")])
                                yield
                                S.op(DVE, lambda h: h.tensor_tensor(out=og[:, t, :].rearrange("p (g c) -> p g c", c=64),
                                                                    in0=sgb[:].rearrange("p (g c) -> p g c", c=64),
                                                                    in1=sgst[:, 10:14].unsqueeze(2).to_broadcast([128, 4, 64]),
                                                                    op=ALU.mult), r=[kB("sgb"), kB("sgst")], w=[B("og")])
                                yield
                            else:
                                li = slot
                                lb_ = kB("lfw")
                                S.op(DVE, lambda h: h.tensor_tensor(out=lfw[:, li, 0:6], in0=psb[:, 0:6], in1=bfB[:], op=ALU.add),
                                     r=[psB, B("cA")], w=[lb_])
                                yield
                                S.op(ACT, lambda h: h.activation(out=lfw[:, li, 6:12], in_=lfw[:, li, 0:6], func=AF.Exp, scale=-1.0), r=[lb_], w=[lb_])
                                yield
                                S.op(ACT, lambda h: h.activation(out=lfw[:, li, 12:18], in_=lfw[:, li, 6:12], func=AF.Ln, bias=1.0), r=[lb_], w=[lb_])
                                yield
                                S.op(DVE, lambda h: h.tensor_scalar(out=lfw[:, li, 18:24], in0=lfw[:, li, 12:18], scalar1=-1.0,
                                                                    scalar2=None, op0=ALU.mult), r=[lb_], w=[lb_])
                                yield
                                S.dma(POOL, olf[l, t], lfw[:, li, 18:24], r=[lb_], w=[B("olf", l, t)])
                                if samp:
                                    S.op(POOL, lambda h: h.tensor_copy(out=slfnew[:, s_i, :], in_=lfw[:, li, 18:24]), r=[lb_], w=[B("slfnew")])
                                else:
                                    S.dma(POOL, lf_in[l][t * 128:(t + 1) * 128, :], lfw[:, li, 18:24], r=[lb_], w=[B("lf_in", l, t)])
                                yield
                        for oi_, (oap, c0) in enumerate([(okf, 0), (ovf, 384), (oks, 768), (ovs, 1152)]):
                            S.dma(POOL, oap[l, t], kvout[slot][:, c0:c0 + 384], r=[kvb], w=[B("okv", l, t, oi_)])
                        yield
                        if samp:
                            for hf, c0 in ((0, 384), (1, 1152)):
                                S.op(POOL, lambda h: h.tensor_copy(
                                    out=svnew[:, s_i, hf * 390:(hf + 1) * 390].rearrange("p (a c) -> p a c", c=65)[:, :, 0:64],
                                    in_=kvout[slot][:, c0:c0 + 384].rearrange("p (a c) -> p a c", c=64)), r=[kvb], w=[B("svnew")])
                                yield
                        else:
                            for hf, c0 in ((0, 384), (1, 1152)):
                                S.op(POOL, lambda h: h.tensor_copy(
                                    out=vaug[slot][:, hf * 390:(hf + 1) * 390].rearrange("p (a c) -> p a c", c=65)[:, :, 0:64],
                                    in_=kvout[slot][:, c0:c0 + 384].rearrange("p (a c) -> p a c", c=64)), r=[kvb], w=[B("vaug", slot)])
                                yield
                            for hf, dst in ((0, vf_in[l]), (1, vs_in[l])):
                                S.dma(POOL, dst[t * 128:(t + 1) * 128, :], vaug[slot][:, hf * 390:(hf + 1) * 390],
                                      r=[B("vaug", slot)], w=[B("v_in", l, t, hf)])
                            yield

                    def a_feat(slot, g):
                        gb = g % 3
                        xr = [B("xTA", gb, tl) for tl in range(4)]
                        for cc in range(12):
                            fi = nxt("poA", 2)
                            pf = [pO0, pO1][fi]
                            pfB = B("pO", fi)
                            S.group(PE, [(lambda kc: lambda h: h.matmul(pf[:, 0:512], lhsT=wfeat[:, kc, cc * 128:(cc + 1) * 128],
                                                                         rhs=xT[gb][:, kc, :], start=kc == 0, stop=kc == 7))(kc)
                                         for kc in range(8)], r=xr + WFEAT_R, w=[pfB])
                            yield
                            e = ACT if cc % 2 == 0 else DVE
                            if cc < 3 or 6 <= cc < 9:
                                pr = cc if cc < 3 else cc - 3
                                S.op(e, cast_op(e, qT[:, pr, g * 512:(g + 1) * 512], pf[:, 0:512]), r=[pfB], w=[B("qT")])
                            else:
                                pr = cc - 3 if cc < 6 else cc - 9
                                fox = cc < 6
                                if g == 4:
                                    S.op(e, cast_op(e, skTn[:, pr + (0 if fox else 3), :], pf[:, 0:512]), r=[pfB], w=[B("skTn")])
                                else:
                                    ksi = nxt("kst", 3)
                                    S.op(e, cast_op(e, kst[ksi][:], pf[:, 0:512]), r=[pfB], w=[B("kst", ksi)])
                                    dst = kTf_in[l] if fox else kTs_in[l]
                                    S.dma(POOL, dst[pr * 128:(pr + 1) * 128, g * 512:(g + 1) * 512], kst[ksi][:],
                                          r=[B("kst", ksi)], w=[B("kT_in", l, g, cc)])
                            yield

                    for tl in range(4):
                        a_prep(tl)
                    tasks = []
                    for g in range(5):
                        for tl in range(4):
                            tasks.append(("task", lambda slot, t=4 * g + tl: a_compute(slot, t)))
                            if g + 1 < 5:
                                tasks.append(("now", lambda t=4 * (g + 1) + tl: a_prep(t)))
                        tasks.append(("task", lambda slot, g=g: a_feat(slot, g)))
                    run_streams(tasks, FLAGS['na'])
                    S.barrier()
                for src, dst, nm in ((kTf_in, kTf_g, "kTf"), (kTs_in, kTs_g, "kTs"), (vf_in, vf_g, "vf"),
                                     (vs_in, vs_g, "vs"), (lf_in, lf_g, "lf")):
                    S.cc(src[l].ap().opt(), dst[l].ap().opt(), r=[], w=[B("g_" + nm, l)])
                S.barrier()

                with ExitStack() as BC:
                    oatt = T(BC, "oatt", [128, NT, 768], BF16)
                    with ExitStack() as Bp:
                        W = attn_work(Bp)
                        KTp = [T(Bp, f"KTp{i}", [128, 32 * 128], BF16) for i in range(2)]
                        Vaug = T(Bp, "Vaug", [128, 32, 390], BF16)
                        ckB = T(Bp, "ckB", [128, 4096], F32)
                        lfT = T(Bp, "lfT", [128, 32, 6], F32)
                        cq = T(Bp, "cq", [128, 96], F32)
                        mAB_f = T(Bp, "mAB_f", [128, 2, 256], F32)
                        mAB_b = T(Bp, "mAB_b", [128, 2, 256], BF16)
                        for kd in range(2):
                            S.op(POOL, lambda h, kd=kd: h.tensor_copy(out=mAB_f[:, kd, :].rearrange("p (a q) -> p a q", q=128),
                                                                      in_=cstf[:, 3 + 2 * kd:5 + 2 * kd, :]), r=[B("cstf")], w=[B("cstf2")])
                        S.op(POOL, lambda h: h.tensor_copy(out=mAB_b[:], in_=mAB_f[:]), r=[B("cstf2")], w=[B("cstb2")])
                        order = [(g % 2) * 16 + g // 2 for g in range(32)]
                        S.dma(SP, lfT[:], lf_g[l].ap().rearrange("(b t) h -> t b h", t=128), r=[B("g_lf", l)], w=[B("lfT")])
                        c_compute(lfT, 32, order, W)
                        cT = W["cT"]
                        S.op(DVE, lambda h: h.tensor_scalar(out=cq[:], in0=cT[:, 0:96], scalar1=selt[:, 0:1], scalar2=None, op0=ALU.mult),
                             r=[B("cT"), B("selt")], w=[B("cq")])
                        S.op(DVE, lambda h: h.scalar_tensor_tensor(out=cq[:], in0=cT[:, 96:192], scalar=selt[:, 1:2], in1=cq[:],
                                                                   op0=ALU.mult, op1=ALU.add), r=[B("cT"), B("selt"), B("cq")], w=[B("cq")])
                        tasks = []
                        kbs = {}
                        for kind, kT_g, v_g in (("fox", kTf_g[l], vf_g[l]), ("sb", kTs_g[l], vs_g[l])):
                            kd = 0 if kind == "fox" else 1

                            def load_v(kind=kind, v_g=v_g):
                                for r_ in range(2):
                                    S.dma(SP, Vaug[:].rearrange("p (k r) c -> p k r c", r=2)[:, :, r_, :],
                                          v_g.ap()[r_ * 2048:(r_ + 1) * 2048, :].rearrange("(k t) c -> t k c", t=128),
                                          r=[B("g_vf" if kind == "fox" else "g_vs", l)], w=[B("V", "p")])
                            tasks.append(("setup", load_v))
                            for hh in range(6):
                                pair, half = hh // 2, hh % 2
                                hp = slice(half * 64, half * 64 + 64)
                                if half == 0:
                                    kb = nxt("KTp", 2)

                                    def load_k(kind=kind, kT_g=kT_g, pair=pair, kb=kb):
                                        for r_ in range(2):
                                            S.dma(SP, KTp[kb][:].rearrange("p (k r t) -> p k r t", r=2, t=128)[:, :, r_, :],
                                                  kT_g.ap()[r_ * 384 + pair * 128:r_ * 384 + (pair + 1) * 128, :].rearrange("p (k t) -> p k t", t=128),
                                                  r=[B("g_kTf" if kind == "fox" else "g_kTs", l)], w=[B("KT", ("p", kb))])
                                    tasks.append(("now", load_k))
                                if kind == "fox":
                                    tasks.append(("setup", lambda hh=hh: ckB_build(W, ckB, ("ckB",), hh, order, 32, 128)))
                                hcol = (0 if kind == "fox" else 384) + hh * 64
                                for k in range(NPB - 1, -1, -1):
                                    tasks.append(("task", lambda slot, kind=kind, hp=hp, pair=pair, kd=kd, k=k, kb=kb, hh=hh, hcol=hcol: attend(
                                        slot, kind, 128, qT[hp, pair + 3 * kd, k * 128:(k + 1) * 128],
                                        lambda lo, hi: KTp[kb][hp, lo * 128:hi * 128],
                                        lambda g_: Vaug[:, g_, hh * 65:(hh + 1) * 65],
                                        2 * k + 2, mAB_f[:, kd, :], mAB_b[:, kd, :], 256,
                                        cq[:, k * 6 + hh:k * 6 + hh + 1], ckB, ("ckB",), W, oatt[:, k, hcol:hcol + 64], (("p", kb), "p"))))
                        run_streams(tasks, FLAGS['nstream'])
                        S.barrier()
                    with ExitStack() as Bs:
                        W = attn_work(Bs)
                        sKT = T(Bs, "sKT", [128, 3, 17 * 128], BF16)
                        sV = T(Bs, "sV", [128, 17, 390], BF16)
                        ckBs = [T(Bs, f"ckBs{i}", [128, 17 * 128], F32) for i in range(2)]
                        lfS = T(Bs, "lfS", [128, 17, 6], F32)
                        cstk = [T(Bs, f"cstk{i}", [128, 2, 384], F32) for i in range(2)]
                        cstv = [T(Bs, f"cstv{i}", [128, 2, 384], F32) for i in range(2)]
                        order = list(range(17))
                        S.op(POOL, lambda h: h.memset(sV[:], 1.0), w=[B("V", "s")])
                        S.op(POOL, lambda h: h.memset(sKT[:], 0.0), w=[B("KT", "s")])
                        for s_i in range(NS):
                            t = NPB + s_i
                            for kind, ck, cv in (("fox", cfk, cfv), ("sb", csk, csv)):
                                kd = 0 if kind == "fox" else 1
                                tasks = []

                                def prep(kind=kind, ck=ck, cv=cv, kd=kd, s_i=s_i):
                                    for b2 in range(8):
                                        ci_ = nxt("cstk", 2)
                                        S.dma(SP, cstk[ci_][:], ck[l, s_i, b2 * 256:(b2 + 1) * 256, :].rearrange("(b t) c -> t b c", t=128),
                                              w=[B("cstk", ci_)])
                                        S.dma(SP, cstv[ci_][:], cv[l, s_i, b2 * 256:(b2 + 1) * 256, :].rearrange("(b t) c -> t b c", t=128),
                                              w=[B("cstv", ci_)])
                                        for bb_ in range(2):
                                            blk = b2 * 2 + bb_
                                            ri_ = nxt("ptr", 2)
                                            pt = [pA, pB][ri_]
                                            ptb_ = B("pS", ri_)
                                            S.group(PE, [(lambda j: lambda h: h.transpose(out=pt[:, j * 128:(j + 1) * 128],
                                                                                          in_=cstk[ci_][:, bb_, j * 128:(j + 1) * 128],
                                                                                          identity=identf))(j) for j in range(3)],
                                                    r=[B("cstk", ci_), B("cstf")], w=[ptb_])
                                            S.op(ACT, lambda h: h.copy(out=sKT[:, :, blk * 128:(blk + 1) * 128],
                                                                       in_=pt[:, 0:384].rearrange("p (a q) -> p a q", q=128)),
                                                 r=[ptb_], w=[B("KT", "s")])
                                        S.op(POOL, lambda h: h.tensor_copy(
                                            out=sV[:, b2 * 2:(b2 + 1) * 2, :].rearrange("p b (a c) -> p b a c", c=65)[:, :, :, 0:64],
                                            in_=cstv[ci_][:].rearrange("p b (a c) -> p b a c", c=64)), r=[B("cstv", ci_)], w=[B("V", "s")])
                                    S.op(POOL, lambda h: h.tensor_copy(out=sKT[:, :, 2048:2048 + 64],
                                                                       in_=skTn[:, 3 * kd:3 * kd + 3, s_i * 128:s_i * 128 + 64]),
                                         r=[B("skTn")], w=[B("KT", "s")])
                                    S.op(POOL, lambda h: h.memset(sV[:, 16, :], 0.0), w=[B("V", "s")])
                                    S.op(POOL, lambda h: h.tensor_copy(out=sV[0:64, 16, :], in_=svnew[0:64, s_i, kd * 390:(kd + 1) * 390]),
                                         r=[B("svnew")], w=[B("V", "s")])
                                    if kind == "fox":
                                        S.op(POOL, lambda h: h.memset(lfS[:, 16, :], 0.0), w=[B("lfT")])
                                        S.dma(SP, lfS[:, 0:16, :], cfl[l, s_i].rearrange("(b t) h -> t b h", t=128), w=[B("lfT")])
                                        S.op(POOL, lambda h: h.tensor_copy(out=lfS[0:64, 16, :], in_=slfnew[0:64, s_i, :]),
                                             r=[B("slfnew")], w=[B("lfT")])
                                        c_compute(lfS, 17, order, W)
                                tasks.append(("setup", prep))
                                for hh in range(6):
                                    pair, half = hh // 2, hh % 2
                                    hp = slice(half * 64, half * 64 + 64)
                                    hcol = (0 if kind == "fox" else 384) + hh * 64

                                    def mk(slot, kind=kind, hp=hp, pair=pair, kd=kd, hh=hh, hcol=hcol, t=t):
                                        cb = ckBs[slot % 2]
                                        key = ("ckBs", slot % 2)
                                        if kind == "fox":
                                            ckB_build(W, cb, key, hh, order, 17, 64)
                                        return attend(slot, kind, 64, qT[hp, pair + 3 * kd, t * 128:t * 128 + 64],
                                                      lambda lo, hi: sKT[hp, pair, lo * 128:hi * 128],
                                                      lambda g_: sV[:, g_, hh * 65:(hh + 1) * 65],
                                                      17, cstf[0:64, 7 + kd, :], cstb[0:64, 7 + kd, :], 128,
                                                      W["cT"][0:64, 16 * 6 + hh:16 * 6 + hh + 1], cb, key, W,
                                                      oatt[0:64, t, hcol:hcol + 64], ("s", "s"))
                                    tasks.append(("task", mk))
                                run_streams(tasks, min(FLAGS['nstream'], 2 if kind == "fox" else NSLOT))
                        S.barrier()
                    with ExitStack() as C1:
                        wout = T(C1, "wout", [128, 8, D], BF16)
                        stg = [T(C1, f"stgC{i}", [128, D], F32) for i in range(2)]
                        gmT = T(C1, "gmT", [128, 8], F32)
                        g1B = T(C1, "g1B", [128, D], F32)
                        b1B = T(C1, "b1B", [128, D], F32)
                        oT = [T(C1, f"oT{i}", [128, 8, 128], BF16) for i in range(3)]
                        xt = [T(C1, f"xtC{i}", [128, D], F32) for i in range(3)]
                        res = [T(C1, f"resC{i}", [128, D], F32) for i in range(2)]
                        xn = [T(C1, f"xnC{i}", [128, D], F32) for i in range(2)]
                        stt = [T(C1, f"stC{i}", [128, 16], F32) for i in range(2)]
                        S.dma(SP, gmT[:], g_mixT[l], w=[B("gmT")])
                        S.dma(SP, g1B[:], ln1_g[l:l + 1, :].partition_broadcast(128), w=[B("cC")])
                        S.dma(SP, b1B[:], ln1_b[l:l + 1, :].partition_broadcast(128), w=[B("cC")])
                        for kc in range(8):
                            load_cast(stg, wout[:, kc, :], w_out[l, kc * 128:(kc + 1) * 128, :], D, B("wout", kc),
                                      scale=gmT[:, kc:kc + 1], engs=[POOL, DVE])
                        WOUT_R = [B("wout", kc) for kc in range(8)] + [B("gmT")]
                        def c1_prep(t):
                            bi = t % 3
                            S.dma(SP, xt[bi][:], xsrc[t], w=[B("xtC", bi)])
                            ti = nxt("pt", 2)
                            pt = [pT0, pT1][ti]

                            def osrc(j, t=t):
                                if j < 3:
                                    return oatt[:, t, j * 128:(j + 1) * 128]
                                if j < 5:
                                    return og[:, t, (j - 3) * 128:(j - 2) * 128]
                                return oatt[:, t, 384 + (j - 5) * 128:384 + (j - 4) * 128]
                            transposes(pt, B("pT", ti), osrc, 8, r=[B("oatt"), B("og")])
                            oi_ = t % 3
                            S.op(ACT, lambda h: h.copy(out=oT[oi_][:], in_=pt[:].rearrange("p (k q) -> p k q", q=128)),
                                 r=[B("pT", ti)], w=[B("oT", oi_)])

                        def c1_banks(t):
                            return [(pA, B("pS", 0)), (pB, B("pS", 1))] if t % 2 == 0 else [(pO0, B("pO", 0)), (pO1, B("pO", 1))]

                        def c1_mix(t):
                            oi_ = t % 3
                            for n_, (pb_, pbB) in enumerate(c1_banks(t)):
                                S.group(PE, [(lambda kc: lambda h: h.matmul(pb_[:, 0:512], lhsT=oT[oi_][:, kc, :],
                                                                             rhs=wout[:, kc, n_ * 512:(n_ + 1) * 512],
                                                                             start=kc == 0, stop=kc == 7))(kc) for kc in range(8)],
                                        r=[B("oT", oi_)] + WOUT_R, w=[pbB])

                        def c1_post(t):
                            bi = t % 3
                            ri = t % 2
                            for n_, (pb_, pbB) in enumerate(c1_banks(t)):
                                S.op(DVE, lambda h: h.scalar_tensor_tensor(
                                    out=res[ri][:, n_ * 512:(n_ + 1) * 512], in0=xt[bi][:, n_ * 512:(n_ + 1) * 512], scalar=ALPHA,
                                    in1=pb_[:, 0:512], op0=ALU.mult, op1=ALU.add), r=[B("xtC", bi), pbB], w=[B("resC", ri)])
                            layer_norm_tile(res[ri][:], xn[ri][:], stt[ri], g1B[:], b1B[:], B("resC", ri), B("xnC", ri), B("stC", ri), B("cC"))
                            S.dma(POOL, xmid.ap()[t], xn[ri][:], r=[B("xnC", ri)], w=[B("xmid", t)])

                        if FLAGS["c1skew"]:
                            c1_prep(0)
                            c1_prep(1)
                            for t in range(NT):
                                c1_mix(t)
                                if t + 2 < NT:
                                    c1_prep(t + 2)
                                c1_post(t)
                        else:
                            for t in range(NT):
                                c1_prep(t)
                                c1_mix(t)
                                c1_post(t)
                        S.barrier()
            with ExitStack() as C2:
                wup = T(C2, "wup", [128, 8, 4 * D], BF16)
                wdn = T(C2, "wdn", [128, 32, D], BF16)
                g2B = T(C2, "g2B", [128, D], F32)
                b2B = T(C2, "b2B", [128, D], F32)
                S.dma(SP, g2B[:], ln2_g[l:l + 1, :].partition_broadcast(128), w=[B("cD")])
                S.dma(SP, b2B[:], ln2_b[l:l + 1, :].partition_broadcast(128), w=[B("cD")])
                with ExitStack() as C2a:
                    stg = [T(C2a, f"stgD{i}", [128, 2048], F32) for i in range(3)]
                    for kc in range(8):
                        for hf in range(2):
                            load_cast(stg, wup[:, kc, hf * 2048:(hf + 1) * 2048], w_up[l, kc * 128:(kc + 1) * 128, hf * 2048:(hf + 1) * 2048],
                                      2048, B("wup", kc))
                    for fc2 in range(16):
                        load_cast(stg, wdn[:, 2 * fc2:2 * fc2 + 2, :].rearrange("p a c -> p (a c)"),
                                  w_down[l, fc2 * 256:(fc2 + 1) * 256, :].rearrange("(a p) c -> p a c", p=128), 2048, B("wdn", fc2),
                                  view=lambda a: a.rearrange("p (a c) -> p a c", a=2))
                    S.barrier()
                with ExitStack() as C2b:
                    xt = [T(C2b, f"xtD{i}", [128, D], F32) for i in range(4)]
                    xb = [T(C2b, f"xbD{i}", [128, D], BF16) for i in range(2)]
                    x1T = [T(C2b, f"x1T{i}", [128, 8, 256], BF16) for i in range(2)]
                    hT = [T(C2b, f"hT{i}", [128, 256], BF16) for i in range(4)]
                    hr = [T(C2b, f"hr{i}", [128, 256], F32) for i in range(3)]
                    res = [T(C2b, f"resD{i}", [128, D], F32) for i in range(2)]
                    xn = [T(C2b, f"xnD{i}", [128, D], F32) for i in range(2)]
                    stt = [T(C2b, f"stD{i}", [128, 16], F32) for i in range(2)]
                    accs = [[(pA, B("pS", 0)), (pB, B("pS", 1))], [(pO0, B("pO", 0)), (pO1, B("pO", 1))]]
                    WUP_R = [B("wup", kc) for kc in range(8)]

                    def c2_prep(gp):
                        xg = gp % 2
                        for tl in range(2):
                            t = 2 * gp + tl
                            bi = (2 * gp + tl) % 4
                            S.dma(SP, xt[bi][:], xmid.ap()[t], r=[B("xmid", t)], w=[B("xtD", bi)])
                            ci_ = nxt("xbD", 2)
                            S.op(POOL, lambda h: h.tensor_copy(out=xb[ci_][:], in_=xt[bi][:]), r=[B("xtD", bi)], w=[B("xbD", ci_)])
                            ti = nxt("pt", 2)
                            pt = [pT0, pT1][ti]
                            transposes(pt, B("pT", ti), lambda j: xb[ci_][:, j * 128:(j + 1) * 128], 8, r=[B("xbD", ci_)])
                            S.op(ACT, lambda h: h.copy(out=x1T[xg][:, :, tl * 128:(tl + 1) * 128],
                                                       in_=pt[:].rearrange("p (k q) -> p k q", q=128)),
                                 r=[B("pT", ti)], w=[B("x1T", xg)])

                    def c2_up(gp, fc):
                        xg = gp % 2
                        ph, phB = [(pC, B("pS", 2)), (pM, B("pM"))][fc % 2]
                        S.group(PE, [(lambda kc: lambda h: h.matmul(ph[:, 0:256], lhsT=wup[:, kc, fc * 128:(fc + 1) * 128],
                                                                     rhs=x1T[xg][:, kc, :], start=kc == 0, stop=kc == 7))(kc)
                                     for kc in range(8)], r=[B("x1T", xg)] + WUP_R, w=[phB])
                        hj = fc % 4
                        rj = fc % 3
                        S.op(ACT, lambda h: h.activation(out=hr[rj][:], in_=ph[:, 0:256], func=AF.Relu), r=[phB], w=[B("hr", rj)])
                        S.op(DVE if fc % 2 == 0 else POOL, lambda h: h.tensor_tensor(out=hT[hj][:], in0=hr[rj][:], in1=hr[rj][:], op=ALU.mult),
                             r=[B("hr", rj)], w=[B("hT", hj)])

                    def c2_down(gp, fc):
                        hj = fc % 4
                        for tl in range(2):
                            for n_ in range(2):
                                pb_, pbB = accs[tl][n_]
                                S.group(PE, [lambda h: h.matmul(pb_[:, 0:512], lhsT=hT[hj][:, tl * 128:(tl + 1) * 128],
                                                                rhs=wdn[:, fc, n_ * 512:(n_ + 1) * 512], start=fc == 0, stop=fc == 31)],
                                        r=[B("hT", hj), B("wdn", fc // 2)], w=[pbB])

                    def c2_post(gp):
                        for tl in range(2):
                            t = 2 * gp + tl
                            bi = (2 * gp + tl) % 4
                            ri = tl
                            for n_ in range(2):
                                pb_, pbB = accs[tl][n_]
                                S.op(DVE, lambda h: h.scalar_tensor_tensor(
                                    out=res[ri][:, n_ * 512:(n_ + 1) * 512], in0=xt[bi][:, n_ * 512:(n_ + 1) * 512], scalar=ALPHA,
                                    in1=pb_[:, 0:512], op0=ALU.mult, op1=ALU.add), r=[B("xtD", bi), pbB], w=[B("resD", ri)])
                            layer_norm_tile(res[ri][:], xn[ri][:], stt[ri], g2B[:], b2B[:], B("resD", ri), B("xnD", ri), B("stD", ri), B("cD"))
                            S.dma(POOL, xdst[t], xn[ri][:], r=[B("xnD", ri)], w=[B("xdst", l, t)])

                    NG = NT // 2
                    if FLAGS["c2skew"]:
                        c2_prep(0)
                        for gp in range(NG):
                            c2_up(gp, 0)
                            c2_up(gp, 1)
                            if gp + 1 < NG:
                                c2_prep(gp + 1)
                            for fc in range(32):
                                if fc + 2 < 32:
                                    c2_up(gp, fc + 2)
                                c2_down(gp, fc)
                            c2_post(gp)
                    else:
                        for gp in range(NG):
                            c2_prep(gp)
                            for fc in range(32):
                                c2_up(gp, fc)
                                c2_down(gp, fc)
                            c2_post(gp)
                    S.barrier()
        S.barrier()
    return nc


_NC = None


def _get_nc():
    global _NC
    if _NC is None:
        _NC = build_nc()
    return _NC


def kernel(x_prompt, x_sample, cache_fox_k, cache_fox_v, cache_fox_logf, cache_sb_k, cache_sb_v,
           w_in, b_f, g_v, b_v, w_s, b_s, g_mix, w_out, ln1_g, ln1_b, w_up, w_down, ln2_g, ln2_b):
    f32 = lambda a: np.ascontiguousarray(np.asarray(a), dtype=np.float32)
    x_prompt, x_sample = f32(x_prompt), f32(x_sample)
    w_in = f32(w_in)
    sp = np.cumsum([384, 384, 384, 6, 256, 256, 384, 384, 384])
    seg = lambda i: slice(0 if i == 0 else sp[i - 1], sp[i])
    qf, kf, vf, fl, ug, vg, qs, ks, vs = [w_in[:, :, seg(i)] for i in range(9)]
    w_tok = np.ascontiguousarray(np.concatenate([kf, vf, ks, vs, ug, vg, fl], axis=2))
    w_feat = np.ascontiguousarray(np.concatenate([qf, kf, qs, ks], axis=2))
    w_sT = np.ascontiguousarray(np.transpose(f32(w_s), (0, 1, 3, 2)))
    b_sT = np.ascontiguousarray(np.transpose(f32(b_s), (0, 2, 1)))
    g_mixT = np.ascontiguousarray(np.transpose(f32(g_mix).reshape(DEPTH, 8, 128), (0, 2, 1)))

    ii = np.arange(128)
    tri_le = (ii[:, None] <= ii[None, :]).astype(np.float32)
    ident = np.eye(128, dtype=np.float32)
    ones = np.ones((128, 128), np.float32)
    zeros = np.zeros((128, 128), np.float32)
    sgs = tri_le * (ii[:, None] < 64) * (ii[None, :] < 64)
    fox_tri = (ii[None, :] <= ii[:, None]).astype(np.float32)
    sb_tri = (ii[None, :] < ii[:, None]).astype(np.float32)

    shared = dict(w_tok=w_tok, w_feat=w_feat, b_f=f32(b_f), g_v=f32(g_v), b_v=f32(b_v), w_sT=w_sT, b_sT=b_sT,
                  g_mixT=g_mixT, w_out=f32(w_out), ln1_g=f32(ln1_g), ln1_b=f32(ln1_b), ln2_g=f32(ln2_g),
                  ln2_b=f32(ln2_b), w_up=f32(w_up), w_down=f32(w_down))
    cfk_a = f32(cache_fox_k).reshape(DEPTH, 32, PAST, 384)
    cfv_a = f32(cache_fox_v).reshape(DEPTH, 32, PAST, 384)
    csk_a = f32(cache_sb_k).reshape(DEPTH, 32, PAST, 384)
    csv_a = f32(cache_sb_v).reshape(DEPTH, 32, PAST, 384)
    cfl_a = f32(cache_fox_logf)
    in_maps = []
    for c in range(8):
        b, j = c // 2, c % 2
        xin = np.zeros((NT, 128, D), np.float32)
        xin[:NPB] = x_prompt[b].reshape(32, 128, D)[j::2]
        xin[NPB:, :64] = x_sample[4 * c:4 * c + 4]
        if j == 0:
            msk = [fox_tri, zeros, sb_tri, zeros]
        else:
            msk = [ones, fox_tri, ones, sb_tri]
        cst = np.stack([ident, tri_le, sgs] + msk + [fox_tri, sb_tri]).astype(np.float32)
        sel = np.zeros((128, 2), np.float32)
        sel[:, j] = 1.0
        m = dict(shared)
        m.update(xin=xin, cfk=np.ascontiguousarray(cfk_a[:, 4 * c:4 * c + 4]), cfv=np.ascontiguousarray(cfv_a[:, 4 * c:4 * c + 4]),
                 csk=np.ascontiguousarray(csk_a[:, 4 * c:4 * c + 4]), csv=np.ascontiguousarray(csv_a[:, 4 * c:4 * c + 4]),
                 cfl=np.ascontiguousarray(cfl_a[:, 4 * c:4 * c + 4]), cst=cst, sel=sel)
        in_maps.append(m)

    res = run_bass_kernel_spmd(_get_nc(), in_maps, core_ids=list(range(8)))
    R = res.results

    y_p = np.zeros((4, 32, 128, D), np.float32)
    y_s = np.zeros((32, 64, D), np.float32)
    pk = {n: np.zeros((DEPTH, 4, 32, 128, 384), np.float32) for n in ("okf", "ovf", "oks", "ovs")}
    pl = np.zeros((DEPTH, 4, 32, 128, 6), np.float32)
    sk = {n: np.zeros((DEPTH, 32, 64, 384), np.float32) for n in ("okf", "ovf", "oks", "ovs")}
    sl = np.zeros((DEPTH, 32, 64, 6), np.float32)
    sg = np.zeros((DEPTH, 32, 64, 256), np.float32)
    for c in range(8):
        b, j = c // 2, c % 2
        r = R[c]
        y_p[b, j::2] = r["y"][:NPB]
        y_s[4 * c:4 * c + 4] = r["y"][NPB:, :64]
        for n in pk:
            pk[n][:, b, j::2] = r[n][:, :NPB]
            sk[n][:, 4 * c:4 * c + 4] = r[n][:, NPB:, :64]
        pl[:, b, j::2] = r["olf"][:, :NPB]
        sl[:, 4 * c:4 * c + 4] = r["olf"][:, NPB:, :64]
        sg[:, 4 * c:4 * c + 4] = r["ogv"][:, :, :64]
    P5 = lambda a: a.reshape(DEPTH, 4, 4096, 6, 64)
    S5 = lambda a: a.reshape(DEPTH, 32, 64, 6, 64)
    return (y_p.reshape(4, 4096, D), y_s,
            P5(pk["okf"]), P5(pk["ovf"]), pl.reshape(DEPTH, 4, 4096, 6), P5(pk["oks"]), P5(pk["ovs"]),
            S5(sk["okf"]), S5(sk["ovf"]), sl, S5(sk["oks"]), S5(sk["ovs"]), sg)
```

```python
import math
from contextlib import ExitStack

import numpy as np
import concourse.bass as bass
import concourse.mybir as mybir
from concourse.bass_utils import run_bass_kernel_spmd

F32 = mybir.dt.float32
BF16 = mybir.dt.bfloat16
AF = mybir.ActivationFunctionType
ALU = mybir.AluOpType

DEPTH = 2
D = 1024
NT = 20
NPB = 16
NS = 4
PAST = 2048
ALPHA = (2 * DEPTH) ** 0.25
LN_EPS = 1e-5
RMS_EPS = 1e-6
GC = math.sqrt(2.0 / math.pi)
WTOK = 2054
WFEAT = 1536
GROUPS = [[0, 1], [2, 3], [4, 5], [6, 7]]
FLAGS = {"c1skew": True, "c2skew": True, "nstream": 4, "na": 2, "featdrain": 0}


class Buf:
    __slots__ = ("w", "r")

    def __init__(self):
        self.w = None
        self.r = {}


class Eng:
    def __init__(self, name, h):
        self.name = name
        self.h = h
        self.sid = None
        self.n = 0
        self.epoch = 0
        self.seen = {}


class Sched:
    NDMA = 24
    LIMIT = 12000

    def __init__(self, nc, stack):
        self.nc = nc
        self.stack = stack
        self.sems = {}
        self.bufs = {}
        self.pe = Eng("pe", nc.tensor)
        self.act = Eng("act", nc.scalar)
        self.dve = Eng("dve", nc.vector)
        self.pool = Eng("pool", nc.gpsimd)
        self.sp = Eng("sp", nc.sync)
        self.engs = [self.pe, self.act, self.dve, self.pool, self.sp]
        self.last = {}
        for e in self.engs:
            self._new_epoch(e)
        self.dsem = [self._mk(f"d{i}") for i in range(self.NDMA)]
        self.duse = [0] * self.NDMA
        self.dnext = 0
        self.cc_n = 0
        self.cc_toks = []

    def _mk(self, name):
        s = self.stack.enter_context(self.nc.semaphore("s_" + name))
        self.sems[name] = s
        return s

    def _new_epoch(self, e):
        if e.sid is not None:
            self.last[e.sid] = e.n
        e.sid = f"{e.name}{e.epoch}"
        e.epoch += 1
        e.n = 0
        self._mk(e.sid)

    def B(self, *key):
        b = self.bufs.get(key)
        if b is None:
            b = Buf()
            self.bufs[key] = b
        return b

    def _wait(self, eng, tok):
        if tok is None:
            return
        sid, val = tok
        if eng.seen.get(sid, 0) >= val:
            return
        eng.h.wait_ge(self.sems[sid], val)
        eng.seen[sid] = val

    def _own(self, eng, sid):
        return sid.startswith(eng.name) and sid[len(eng.name):].isdigit()

    def _pre(self, eng, r, w):
        for b in r:
            self._wait(eng, b.w)
        for b in w:
            if b.w is not None and not self._own(eng, b.w[0]):
                self._wait(eng, b.w)
            for t in b.r.items():
                if not self._own(eng, t[0]):
                    self._wait(eng, t)

    def _post(self, tok, r, w):
        for b in r:
            if b.r.get(tok[0], 0) < tok[1]:
                b.r[tok[0]] = tok[1]
        for b in w:
            b.w = tok
            b.r = {}

    def _tick(self, eng, ins):
        if eng.n >= self.LIMIT:
            self._new_epoch(eng)
        eng.n += 1
        ins.then_inc(self.sems[eng.sid], 1)
        return (eng.sid, eng.n)

    def op(self, eng, fn, r=(), w=()):
        self._pre(eng, r, w)
        tok = self._tick(eng, fn(eng.h))
        self._post(tok, r, w)

    def group(self, eng, fns, r=(), w=()):
        self._pre(eng, r, w)
        ins = None
        for fn in fns:
            ins = fn(eng.h)
        tok = self._tick(eng, ins)
        self._post(tok, r, w)

    def dma(self, q, out, in_, r=(), w=()):
        i = self.dnext
        self.dnext = (self.dnext + 1) % self.NDMA
        sid = f"d{i}"
        if self.duse[i] > 0:
            self._wait(q, (sid, 16 * self.duse[i]))
        self._pre(q, r, w)
        q.h.dma_start(out=out, in_=in_).then_inc(self.dsem[i], 16)
        self.duse[i] += 1
        self._post((sid, 16 * self.duse[i]), r, w)

    def cc(self, ins, outs, r=(), w=()):
        q = self.pool
        self._pre(q, r, w)
        self.cc_n += 1
        name = f"cc{self.cc_n}"
        sem = self._mk(name)
        q.h.collective_compute("AllGather", ALU.bypass, replica_groups=GROUPS,
                               ins=[ins], outs=[outs]).then_inc(sem)
        tok = (name, 1)
        self._post(tok, r, w)
        self.cc_toks.append(tok)
        self._wait(q, tok)

    def barrier(self):
        toks = [(e.sid, e.n) for e in self.engs if e.n > 0]
        toks += list(self.last.items())
        toks += [(f"d{i}", 16 * self.duse[i]) for i in range(self.NDMA) if self.duse[i] > 0]
        toks += self.cc_toks
        for e in self.engs:
            for t in toks:
                if t[1] > 0 and not self._own(e, t[0]):
                    self._wait(e, t)
        for b in self.bufs.values():
            b.w = None
            b.r = {}


def build_nc():
    nc = bass.Bass("TRN2", target_bir_lowering=False)

    def din(name, shape, dt=F32):
        return nc.dram_tensor(name, list(shape), dt, kind="ExternalInput").ap()

    def dout(name, shape):
        return nc.dram_tensor(name, list(shape), F32, kind="ExternalOutput").ap()

    def dint(name, shape, dt):
        return nc.dram_tensor(name, list(shape), dt)

    xin = din("xin", [NT, 128, D])
    cfk = din("cfk", [DEPTH, NS, PAST, 384])
    cfv = din("cfv", [DEPTH, NS, PAST, 384])
    csk = din("csk", [DEPTH, NS, PAST, 384])
    csv = din("csv", [DEPTH, NS, PAST, 384])
    cfl = din("cfl", [DEPTH, NS, PAST, 6])
    w_tok = din("w_tok", [DEPTH, D, WTOK])
    w_feat = din("w_feat", [DEPTH, D, WFEAT])
    b_f = din("b_f", [DEPTH, 6])
    g_v = din("g_v", [DEPTH, 256])
    b_v = din("b_v", [DEPTH, 256])
    w_sT = din("w_sT", [DEPTH, 4, 128, 128])
    b_sT = din("b_sT", [DEPTH, 128, 4])
    g_mixT = din("g_mixT", [DEPTH, 128, 8])
    w_out = din("w_out", [DEPTH, D, D])
    ln1_g = din("ln1_g", [DEPTH, D])
    ln1_b = din("ln1_b", [DEPTH, D])
    ln2_g = din("ln2_g", [DEPTH, D])
    ln2_b = din("ln2_b", [DEPTH, D])
    w_up = din("w_up", [DEPTH, D, 4 * D])
    w_down = din("w_down", [DEPTH, 4 * D, D])
    cst = din("cst", [9, 128, 128])
    sel = din("sel", [128, 2])

    y = dout("y", [NT, 128, D])
    okf = dout("okf", [DEPTH, NT, 128, 384])
    ovf = dout("ovf", [DEPTH, NT, 128, 384])
    oks = dout("oks", [DEPTH, NT, 128, 384])
    ovs = dout("ovs", [DEPTH, NT, 128, 384])
    olf = dout("olf", [DEPTH, NT, 128, 6])
    ogv = dout("ogv", [DEPTH, NS, 128, 256])

    xmid = dint("xmid", [NT, 128, D], F32)
    xl1 = dint("xl1", [NT, 128, D], F32)
    kTf_in = [dint(f"kTf_in{l}", [384, 2048], BF16) for l in range(DEPTH)]
    kTs_in = [dint(f"kTs_in{l}", [384, 2048], BF16) for l in range(DEPTH)]
    vf_in = [dint(f"vf_in{l}", [2048, 390], BF16) for l in range(DEPTH)]
    vs_in = [dint(f"vs_in{l}", [2048, 390], BF16) for l in range(DEPTH)]
    lf_in = [dint(f"lf_in{l}", [2048, 6], F32) for l in range(DEPTH)]
    kTf_g = [dint(f"kTf_g{l}", [768, 2048], BF16) for l in range(DEPTH)]
    kTs_g = [dint(f"kTs_g{l}", [768, 2048], BF16) for l in range(DEPTH)]
    vf_g = [dint(f"vf_g{l}", [4096, 390], BF16) for l in range(DEPTH)]
    vs_g = [dint(f"vs_g{l}", [4096, 390], BF16) for l in range(DEPTH)]
    lf_g = [dint(f"lf_g{l}", [4096, 6], F32) for l in range(DEPTH)]

    with ExitStack() as top:
        S = Sched(nc, top)
        B = S.B
        PE, ACT, DVE, POOL, SP = S.pe, S.act, S.dve, S.pool, S.sp

        uniq = [0]

        def T(stack, name, shape, dt):
            uniq[0] += 1
            return stack.enter_context(nc.sbuf_tensor(f"{name}_{uniq[0]}", list(shape), dt))

        def PS(name, shape, dt):
            return top.enter_context(nc.psum_tensor(name, list(shape), dt))

        pk = [PS(f"pk{i}", [128, 512], F32) for i in range(8)]
        pA, pB, pC, pM, pO0, pO1, pT0f, pT1f = pk
        pT0 = pT0f[:].bitcast(BF16)
        pT1 = pT1f[:].bitcast(BF16)

        cstf = T(top, "cstf", [128, 9, 128], F32)
        cstb = T(top, "cstb", [128, 9, 128], BF16)
        ones512 = T(top, "ones512", [128, 514], F32)
        onesf = T(top, "onesf", [128, 128], F32)
        selt = T(top, "selt", [128, 2], F32)
        S.dma(SP, cstf[:], cst.rearrange("c p q -> p c q"), w=[B("cstf")])
        S.dma(SP, selt[:], sel[:, :], w=[B("selt")])
        S.op(POOL, lambda h: h.tensor_copy(out=cstb[:], in_=cstf[:]), r=[B("cstf")], w=[B("cstb")])
        S.op(DVE, lambda h: h.memset(ones512[:], 1.0), w=[B("ones512")])
        S.op(DVE, lambda h: h.memset(onesf[:], 1.0), w=[B("onesf")])
        identf = cstf[:, 0, :]
        identb = cstb[:, 0, :]
        Uf = cstf[:, 1, :]
        CONST_R = [B("cstf"), B("cstb"), B("ones512"), B("onesf"), B("selt")]

        rot = {}

        def nxt(key, n):
            v = rot.get(key, 0) % n
            rot[key] = (v + 1) % n
            return v

        cast_engs = [POOL, DVE, ACT]

        def cast_op(eng, out, in_, scale=None):
            if scale is not None:
                if eng is ACT:
                    return lambda h: h.activation(out=out, in_=in_, func=AF.Copy, scale=scale)
                return lambda h: h.tensor_scalar(out=out, in0=in_, scalar1=scale, scalar2=None, op0=ALU.mult)
            if eng is ACT:
                return lambda h: h.copy(out=out, in_=in_)
            return lambda h: h.tensor_copy(out=out, in_=in_)

        def load_cast(stg, dst, src, ncols, wb, scale=None, engs=None, view=None):
            i = nxt("stg", len(stg))
            sv = stg[i][:, 0:ncols]
            S.dma(SP, view(sv) if view else sv, src, w=[B("stg", i)])
            engs = engs or cast_engs
            e = engs[nxt("casteng", len(engs))]
            S.op(e, cast_op(e, dst, stg[i][:, 0:ncols], scale), r=[B("stg", i)] + CONST_R, w=[wb])

        def transposes(pt, pbuf, src_fn, nblk, rows=128, r=()):
            fns = []
            for j in range(nblk):
                fns.append((lambda j: lambda h: h.transpose(out=pt[:, j * 128:j * 128 + rows], in_=src_fn(j),
                                                            identity=identb[0:rows, 0:rows]))(j))
            S.group(PE, fns, r=list(r) + [B("cstb")], w=[pbuf])

        def rstd_from(var_ap, out_ap, tmp_ap, scale, eps, bufs_r, buf_w):
            S.op(ACT, lambda h: h.activation(out=tmp_ap, in_=var_ap, func=AF.Ln, bias=eps, scale=scale),
                 r=bufs_r, w=[buf_w])
            S.op(ACT, lambda h: h.activation(out=out_ap, in_=tmp_ap, func=AF.Exp, scale=-0.5),
                 r=[buf_w], w=[buf_w])

        def layer_norm_tile(res, outt, stats, gB, bB, rb, ob, stb, constb):
            S.op(DVE, lambda h: h.bn_stats(out=stats[:, 0:6], in_=res[:, 0:512]), r=[rb], w=[stb])
            S.op(DVE, lambda h: h.bn_stats(out=stats[:, 6:12], in_=res[:, 512:1024]), r=[rb], w=[stb])
            S.op(DVE, lambda h: h.bn_aggr(out=stats[:, 12:14], in_=stats[:, 0:12]), r=[stb], w=[stb])
            rstd_from(stats[:, 13:14], stats[:, 14:15], stats[:, 15:16], 1.0, LN_EPS, [stb], stb)
            S.op(DVE, lambda h: h.tensor_scalar(out=outt, in0=res, scalar1=stats[:, 12:13], scalar2=stats[:, 14:15],
                                                op0=ALU.subtract, op1=ALU.mult), r=[rb, stb], w=[ob])
            S.op(POOL, lambda h: h.tensor_tensor(out=outt, in0=outt, in1=gB, op=ALU.mult), r=[ob, constb], w=[ob])
            S.op(POOL, lambda h: h.tensor_tensor(out=outt, in0=outt, in1=bB, op=ALU.add), r=[ob, constb], w=[ob])

        NSLOT = 4
        SBANK = [(pA, ("pS", 0)), (pB, ("pS", 1)), (pC, ("pS", 2)), (pM, ("pM",))]
        POBANK = [(pO0, ("pO", 0)), (pO1, ("pO", 1)), (pT0f, ("pT", 0)), (pT1f, ("pT", 1))]

        def attend(slot, kind, M, qT_ap, KT_fn, V_fn, nkb, mask_f, mask_b, maskw, cq_ap, ckB, ckb_key, W, out_ap, uid):
            ps_ = slice(0, M)
            chunks = []
            hi = nkb
            first = True
            while hi > 0:
                if first and maskw == 128:
                    lo = hi - 1
                else:
                    lo = max(0, hi - 4)
                chunks.append((lo, hi))
                hi = lo
                first = False
            po = POBANK[slot][0][:, 0:65]
            pob = B(*POBANK[slot][1])
            psb, pskey = SBANK[slot]
            psB = B(*pskey)
            pt = psb[:].bitcast(BF16)[:, 0:512]
            ptB = psB
            t1 = W["t1"][slot]
            e = W["e"][slot]
            lb = W["l"][slot]
            pin = W["pin"][slot]
            a = W["a"][slot]
            aT = W["aT"][slot]
            car = W["carry"]
            kB = lambda nm: B(nm, slot)
            if kind == "sb":
                S.op(POOL, lambda h: h.memset(car[:, slot, 0:1], 0.0), w=[B("carry", slot, 0)])
                yield
                cprev = 0
            nmm = 0
            for ci, (lo, hi) in enumerate(chunks):
                nb = hi - lo
                w = nb * 128
                S.group(PE, [lambda h: h.matmul(psb[ps_, 0:w], lhsT=qT_ap, rhs=KT_fn(lo, hi), start=True, stop=True)],
                        r=[B("qT"), B("KT", uid[0])], w=[psB])
                yield
                top_chunk = ci == 0
                if kind == "fox":
                    S.op(DVE, lambda h: h.scalar_tensor_tensor(
                        out=t1[ps_, 0:w], in0=psb[ps_, 0:w], scalar=0.125, in1=ckB[ps_, lo * 128:hi * 128],
                        op0=ALU.mult, op1=ALU.subtract), r=[psB, B(*ckb_key)], w=[kB("t1")])
                    yield
                    S.op(ACT, lambda h: h.activation(out=a[ps_, 0:w], in_=t1[ps_, 0:w], func=AF.Exp, bias=cq_ap),
                         r=[kB("t1"), B("cq")], w=[kB("a")])
                    yield
                else:
                    S.op(ACT, lambda h: h.activation(out=e[ps_, 0:w], in_=psb[ps_, 0:w], func=AF.Exp, scale=0.125),
                         r=[psB], w=[kB("e")])
                    yield
                    S.op(ACT, lambda h: h.activation(out=lb[ps_, 1:w + 1], in_=e[ps_, 0:w], func=AF.Ln, bias=1.0),
                         r=[kB("e")], w=[kB("l")])
                    yield
                    if top_chunk:
                        S.op(POOL, lambda h: h.tensor_tensor(out=lb[ps_, 1 + w - maskw:1 + w], in0=lb[ps_, 1 + w - maskw:1 + w],
                                                             in1=mask_f, op=ALU.mult), r=[kB("l"), B("cstf"), B("cstf2")], w=[kB("l")])
                        yield
                    S.op(DVE, lambda h: h.tensor_tensor_scan(
                        out=pin[ps_, 0:w + 1], data0=ones512[ps_, 0:w + 1], data1=lb[ps_, 0:w + 1], initial=0.0,
                        op0=ALU.mult, op1=ALU.add), r=[kB("l"), B("ones512")], w=[kB("pin")])
                    yield
                    cnew = 1 - cprev
                    S.op(POOL, lambda h: h.tensor_tensor(out=car[ps_, slot, cnew:cnew + 1], in0=car[ps_, slot, cprev:cprev + 1],
                                                         in1=pin[ps_, w:w + 1], op=ALU.subtract),
                         r=[B("carry", slot, cprev), kB("pin")], w=[B("carry", slot, cnew)])
                    yield
                    S.op(DVE, lambda h: h.scalar_tensor_tensor(
                        out=t1[ps_, 0:w], in0=psb[ps_, 0:w], scalar=0.125, in1=pin[ps_, 0:w],
                        op0=ALU.mult, op1=ALU.add), r=[psB, kB("pin")], w=[kB("t1")])
                    yield
                    S.op(ACT, lambda h: h.activation(out=a[ps_, 0:w], in_=t1[ps_, 0:w], func=AF.Exp, bias=car[ps_, slot, cnew:cnew + 1]),
                         r=[kB("t1"), B("carry", slot, cnew)], w=[kB("a")])
                    yield
                    cprev = cnew
                if top_chunk:
                    S.op(POOL, lambda h: h.tensor_tensor(out=a[ps_, w - maskw:w], in0=a[ps_, w - maskw:w], in1=mask_b, op=ALU.mult),
                         r=[kB("a"), B("cstb"), B("cstb2")], w=[kB("a")])
                    yield
                transposes(pt, ptB, lambda j: a[ps_, j * 128:(j + 1) * 128], nb, rows=M, r=[kB("a")])
                yield
                if M == 128:
                    S.op(ACT, lambda h: h.copy(out=aT[:, 0:w], in_=pt[:, 0:w]), r=[ptB], w=[kB("aT")])
                else:
                    S.op(ACT, lambda h: h.copy(
                        out=aT[:, 0:nb * 128].rearrange("p (b q) -> p b q", q=128)[:, :, 0:M],
                        in_=pt[:, 0:nb * 128].rearrange("p (b q) -> p b q", q=128)[:, :, 0:M]), r=[ptB], w=[kB("aT")])
                yield
                fns = []
                ncol = 65 if kind == "fox" else 64
                for j in range(nb):
                    st = nmm == 0
                    sp_ = nmm == nkb - 1
                    fns.append((lambda j, st, sp_: lambda h: h.matmul(
                        po[ps_, 0:ncol], lhsT=aT[:, j * 128:j * 128 + M], rhs=V_fn(lo + j)[:, 0:ncol],
                        start=st, stop=sp_))(j, st, sp_))
                    nmm += 1
                S.group(PE, fns, r=[kB("aT"), B("V", uid[1])], w=[pob])
                yield
            on = W["on"][slot]
            sq = W["sq"][slot]
            ss = W["ss"]
            eb = kB("ep")
            if kind == "fox":
                S.op(DVE, lambda h: h.reciprocal(out=ss[ps_, slot, 0:1], in_=po[ps_, 64:65]), r=[pob], w=[eb])
                yield
                S.op(DVE, lambda h: h.tensor_scalar(out=on[ps_, :], in0=po[ps_, 0:64], scalar1=ss[ps_, slot, 0:1], scalar2=None,
                                                    op0=ALU.mult), r=[pob, eb], w=[eb])
            else:
                S.op(DVE, lambda h: h.tensor_copy(out=on[ps_, :], in_=po[ps_, 0:64]), r=[pob], w=[eb])
            yield
            S.op(POOL, lambda h: h.memset(ss[ps_, slot, 1:2], 0.0), w=[eb])
            yield
            S.op(ACT, lambda h: h.activation(out=sq[ps_, :], in_=on[ps_, :], func=AF.Square, accum_out=ss[ps_, slot, 1:2]),
                 r=[eb], w=[eb])
            yield
            S.op(ACT, lambda h: h.activation(out=ss[ps_, slot, 2:3], in_=ss[ps_, slot, 1:2], func=AF.Ln, bias=RMS_EPS, scale=1.0 / 64.0),
                 r=[eb], w=[eb])
            yield
            S.op(ACT, lambda h: h.activation(out=ss[ps_, slot, 3:4], in_=ss[ps_, slot, 2:3], func=AF.Exp, scale=-0.5), r=[eb], w=[eb])
            yield
            S.op(DVE, lambda h: h.tensor_scalar(out=out_ap, in0=on[ps_, :], scalar1=ss[ps_, slot, 3:4], scalar2=None,
                                                op0=ALU.mult), r=[eb], w=[B("oatt")])
            yield

        def run_streams(tasks, n):
            active = []
            free = list(range(n))
            i = 0
            while i < len(tasks) or active:
                while i < len(tasks) and (free or tasks[i][0] != "task"):
                    kind_, f = tasks[i]
                    if kind_ == "now":
                        f()
                    elif kind_ == "setup":
                        if active:
                            break
                        f()
                    else:
                        slot = free.pop(0)
                        active.append((slot, f(slot)))
                    i += 1
                for item in list(active):
                    try:
                        next(item[1])
                    except StopIteration:
                        active.remove(item)
                        free.append(item[0])
                        free.sort()

        def c_compute(lfT, nblk, order, W, M=128):
            n = nblk * 6
            lf2 = lfT[:].rearrange("p b h -> p (b h)")
            S.group(PE, [lambda h: h.matmul(pM[:, 0:n], lhsT=Uf, rhs=lf2, start=True, stop=True)],
                    r=[B("lfT"), B("cstf")], w=[B("pM")])
            S.op(ACT, lambda h: h.copy(out=W["cw"][:, 0:n], in_=pM[:, 0:n]), r=[B("pM")], w=[B("cw")])
            S.group(PE, [lambda h: h.matmul(pM[:, 0:n], lhsT=onesf[:], rhs=lf2, start=True, stop=True)],
                    r=[B("lfT"), B("onesf")], w=[B("pM")])
            S.op(ACT, lambda h: h.copy(out=W["tot"][:, 0:n], in_=pM[:, 0:n]), r=[B("pM")], w=[B("tot")])
            offs = W["offs"]
            S.op(DVE, lambda h: h.memset(offs[:, order[0] * 6:order[0] * 6 + 6], 0.0), w=[B("offs")])
            for gi in range(1, nblk):
                a, b_ = order[gi], order[gi - 1]
                S.op(DVE, lambda h, a=a, b_=b_: h.tensor_tensor(out=offs[:, a * 6:a * 6 + 6], in0=offs[:, b_ * 6:b_ * 6 + 6],
                                                                in1=W["tot"][:, b_ * 6:b_ * 6 + 6], op=ALU.add),
                     r=[B("offs"), B("tot")], w=[B("offs")])
            S.op(DVE, lambda h: h.tensor_tensor(out=W["cT"][:, 0:n], in0=W["cw"][:, 0:n], in1=offs[:, 0:n], op=ALU.add),
                 r=[B("cw"), B("offs")], w=[B("cT")])

        def ckB_build(W, ckB, ckb_key, h6, order, nblk, M):
            g = 0
            while g < nblk:
                nb = min(4, nblk - g)
                ci = nxt("cexp", 2)
                cx = W["cexp"][ci]
                for jj in range(nb):
                    slot = order[g + jj]
                    S.op(POOL, lambda h, cx=cx, jj=jj, slot=slot: h.tensor_tensor(
                        out=cx[:, jj * 128:(jj + 1) * 128], in0=identf,
                        in1=W["cT"][:, slot * 6 + h6:slot * 6 + h6 + 1].to_broadcast([128, 128]), op=ALU.mult),
                        r=[B("cT"), B("cstf")], w=[B("cexp", ci)])
                S.group(PE, [lambda h, cx=cx, nb=nb: h.matmul(pM[0:M, 0:nb * 128], lhsT=onesf[:, 0:M], rhs=cx[:, 0:nb * 128],
                                                               start=True, stop=True)],
                        r=[B("cexp", ci), B("onesf")], w=[B("pM")])
                S.op(ACT, lambda h, g=g, nb=nb: h.copy(out=ckB[0:M, g * 128:(g + nb) * 128], in_=pM[0:M, 0:nb * 128]),
                     r=[B("pM")], w=[B(*ckb_key)])
                g += nb

        def attn_work(stack):
            W = {}
            for nm in ["t1", "e"]:
                W[nm] = [T(stack, f"w_{nm}{i}", [128, 512], F32) for i in range(NSLOT)]
            for nm in ["l", "pin"]:
                W[nm] = [T(stack, f"w_{nm}{i}", [128, 514], F32) for i in range(NSLOT)]
            for i in range(NSLOT):
                S.op(POOL, lambda h, i=i: h.memset(W["l"][i][:, 0:1], 0.0), w=[B("l", i)])
            W["a"] = [T(stack, f"w_a{i}", [128, 512], BF16) for i in range(NSLOT)]
            W["aT"] = [T(stack, f"w_aT{i}", [128, 512], BF16) for i in range(NSLOT)]
            W["carry"] = T(stack, "w_carry", [128, NSLOT, 4], F32)
            W["on"] = [T(stack, f"w_on{i}", [128, 64], F32) for i in range(NSLOT)]
            W["sq"] = [T(stack, f"w_sq{i}", [128, 64], F32) for i in range(NSLOT)]
            W["ss"] = T(stack, "w_ss", [128, NSLOT, 4], F32)
            W["cw"] = T(stack, "w_cw", [128, 192], F32)
            W["tot"] = T(stack, "w_tot", [128, 192], F32)
            W["offs"] = T(stack, "w_offs", [128, 192], F32)
            W["cT"] = T(stack, "w_cT", [128, 192], F32)
            W["cexp"] = [T(stack, f"w_cexp{i}", [128, 512], F32) for i in range(2)]
            return W

        for l in range(DEPTH):
            xsrc = xin if l == 0 else xl1.ap()
            xdst = xl1.ap() if l == 0 else y
            with ExitStack() as L1:
                qT = T(L1, "qT", [128, 6, NT * 128], BF16)
                og = T(L1, "og", [128, NT, 256], BF16)
                svnew = T(L1, "svnew", [128, NS, 780], BF16)
                slfnew = T(L1, "slfnew", [128, NS, 6], F32)
                skTn = T(L1, "skTn", [128, 6, 512], BF16)
                S.op(POOL, lambda h: h.memset(svnew[:], 1.0), w=[B("svnew")])
                gather_r = []
                with ExitStack() as A:
                    wtok = T(A, "wtok", [128, 8, WTOK], BF16)
                    wfeat = T(A, "wfeat", [128, 8, WFEAT], BF16)
                    with ExitStack() as A0:
                        stg = [T(A0, f"stgA{i}", [128, WTOK], F32) for i in range(2)]
                        for kc in range(8):
                            load_cast(stg, wtok[:, kc, :], w_tok[l, kc * 128:(kc + 1) * 128, :], WTOK, B("wtok", kc))
                            load_cast(stg, wfeat[:, kc, :], w_feat[l, kc * 128:(kc + 1) * 128, :], WFEAT, B("wfeat", kc))
                        S.barrier()
                    NA = 2
                    xt = [T(A, f"xtA{i}", [128, D], F32) for i in range(2)]
                    xb = [T(A, f"xbA{i}", [128, D], BF16) for i in range(2)]
                    xT = [T(A, f"xTA{i}", [128, 8, 512], BF16) for i in range(3)]
                    kvout = [T(A, f"kvout{i}", [128, 1536], F32) for i in range(NA)]
                    vaug = [T(A, f"vaug{i}", [128, 780], BF16) for i in range(NA)]
                    kst = [T(A, f"kst{i}", [128, 512], BF16) for i in range(3)]
                    gxs = [T(A, f"gx{i}", [128, 512], F32) for i in range(NA)]
                    g2s = [T(A, f"g2{i}", [128, 512], F32) for i in range(NA)]
                    ges = [T(A, f"ge{i}", [128, 512], F32) for i in range(NA)]
                    gls = [T(A, f"gl{i}", [128, 512], F32) for i in range(NA)]
                    vns = [T(A, f"vn{i}", [128, 256], F32) for i in range(NA)]
                    vbs = [T(A, f"vb{i}", [128, 256], BF16) for i in range(NA)]
                    sgbs = [T(A, f"sgb{i}", [128, 256], F32) for i in range(NA)]
                    lfw = T(A, "lfw", [128, NA, 24], F32)
                    sgsts = [T(A, f"sgst{i}", [128, 16], F32) for i in range(NA)]
                    bfB = T(A, "bfB", [128, 6], F32)
                    gvB = T(A, "gvB", [128, 256], F32)
                    bvB = T(A, "bvB", [128, 256], F32)
                    wsf = T(A, "wsf", [128, 4, 128], F32)
                    WsT = T(A, "WsT", [128, 4, 128], BF16)
                    WsTs = T(A, "WsTs", [128, 4, 128], BF16)
                    bsT_t = T(A, "bsT_t", [128, 4], F32)
                    bsB = T(A, "bsB", [128, 256], F32)
                    for i in range(NA):
                        S.op(POOL, lambda h, i=i: h.memset(vaug[i][:], 1.0), w=[B("vaug", i)])
                    S.dma(SP, bfB[:], b_f[l:l + 1, :].partition_broadcast(128), w=[B("cA")])
                    S.dma(SP, gvB[:], g_v[l:l + 1, :].partition_broadcast(128), w=[B("cA")])
                    S.dma(SP, bvB[:], b_v[l:l + 1, :].partition_broadcast(128), w=[B("cA")])
                    S.dma(SP, wsf[:], w_sT[l].rearrange("g j i -> j g i"), w=[B("wsf")])
                    S.dma(SP, bsT_t[:], b_sT[l], w=[B("bsT")])
                    S.op(POOL, lambda h: h.tensor_tensor(out=WsT[:], in0=wsf[:], in1=cstf[:, 1:2, :].to_broadcast([128, 4, 128]),
                                                         op=ALU.mult), r=[B("wsf"), B("cstf")], w=[B("cA")])
                    S.op(POOL, lambda h: h.tensor_tensor(out=WsTs[:], in0=wsf[:], in1=cstf[:, 2:3, :].to_broadcast([128, 4, 128]),
                                                         op=ALU.mult), r=[B("wsf"), B("cstf")], w=[B("cA")])
                    S.op(POOL, lambda h: h.tensor_copy(out=bsB[:].rearrange("p (g c) -> p g c", c=64),
                                                       in_=bsT_t[:].unsqueeze(2).to_broadcast([128, 4, 64])),
                         r=[B("bsT")], w=[B("cA")])
                    WTOK_R = [B("wtok", kc) for kc in range(8)]
                    WFEAT_R = [B("wfeat", kc) for kc in range(8)]

                    def a_prep(t):
                        g, tl = t // 4, t % 4
                        gb = g % 3
                        bi = nxt("xtA", 2)
                        S.dma(SP, xt[bi][:], xsrc[t], w=[B("xtA", bi)])
                        S.op(POOL, lambda h: h.tensor_copy(out=xb[bi][:], in_=xt[bi][:]), r=[B("xtA", bi)], w=[B("xbA", bi)])
                        ti = nxt("pt", 2)
                        pt = [pT0, pT1][ti]
                        transposes(pt, B("pT", ti), lambda j: xb[bi][:, j * 128:(j + 1) * 128], 8, r=[B("xbA", bi)])
                        S.op(ACT, lambda h: h.copy(out=xT[gb][:, :, tl * 128:(tl + 1) * 128],
                                                   in_=pt[:].rearrange("p (k q) -> p k q", q=128)),
                             r=[B("pT", ti)], w=[B("xTA", gb, tl)])

                    def a_compute(slot, t):
                        g, tl = t // 4, t % 4
                        gb = g % 3
                        samp = t >= NPB
                        s_i = t - NPB
                        gx, g2, ge, gl = gxs[slot], g2s[slot], ges[slot], gls[slot]
                        vn, vb, sgb, sgst = vns[slot], vbs[slot], sgbs[slot], sgsts[slot]
                        kB = lambda nm: B(nm, "A", slot)
                        kvb = kB("kvout")
                        for ci, (c0, c1) in enumerate([(0, 384), (384, 768), (768, 1152), (1152, 1536), (1536, 2048), (2048, 2054)]):
                            si = nxt("psA", 3)
                            psb = [pA, pB, pC][si]
                            psB = B("pS", si)
                            wd = c1 - c0
                            S.group(PE, [(lambda kc: lambda h: h.matmul(psb[:, 0:wd], lhsT=xT[gb][:, kc, tl * 128:(tl + 1) * 128],
                                                                         rhs=wtok[:, kc, c0:c1], start=kc == 0, stop=kc == 7))(kc)
                                         for kc in range(8)], r=[B("xTA", gb, tl)] + WTOK_R, w=[psB])
                            yield
                            if ci < 4:
                                e = ACT if ci % 2 == 0 else DVE
                                S.op(e, cast_op(e, kvout[slot][:, c0:c1], psb[:, 0:wd]), r=[psB], w=[kvb])
                                yield
                            elif ci == 4:
                                S.op(ACT, lambda h: h.copy(out=gx[:], in_=psb[:, 0:512]), r=[psB], w=[kB("gx")])
                                yield
                                S.op(POOL, lambda h: h.tensor_tensor(out=g2[:], in0=gx[:], in1=gx[:], op=ALU.mult), r=[kB("gx")], w=[kB("g2")])
                                yield
                                S.op(POOL, lambda h: h.tensor_scalar(out=g2[:], in0=g2[:], scalar1=0.044715, scalar2=1.0,
                                                                     op0=ALU.mult, op1=ALU.add), r=[kB("g2")], w=[kB("g2")])
                                yield
                                S.op(POOL, lambda h: h.tensor_tensor(out=g2[:], in0=g2[:], in1=gx[:], op=ALU.mult),
                                     r=[kB("g2"), kB("gx")], w=[kB("g2")])
                                yield
                                S.op(ACT, lambda h: h.activation(out=ge[:], in_=g2[:], func=AF.Exp, scale=-2.0 * GC), r=[kB("g2")], w=[kB("ge")])
                                yield
                                S.op(DVE, lambda h: h.tensor_scalar(out=ge[:], in0=ge[:], scalar1=1.0, scalar2=None, op0=ALU.add),
                                     r=[kB("ge")], w=[kB("ge")])
                                yield
                                S.op(DVE, lambda h: h.reciprocal(out=ge[:], in_=ge[:]), r=[kB("ge")], w=[kB("ge")])
                                yield
                                S.op(POOL, lambda h: h.tensor_tensor(out=gl[:], in0=gx[:], in1=ge[:], op=ALU.mult),
                                     r=[kB("gx"), kB("ge")], w=[kB("gl")])
                                yield
                                S.op(DVE, lambda h: h.bn_stats(out=sgst[:, 0:6], in_=gl[:, 256:512]), r=[kB("gl")], w=[kB("sgst")])
                                yield
                                S.op(DVE, lambda h: h.bn_aggr(out=sgst[:, 6:8], in_=sgst[:, 0:6]), r=[kB("sgst")], w=[kB("sgst")])
                                yield
                                S.op(ACT, lambda h: h.activation(out=sgst[:, 9:10], in_=sgst[:, 7:8], func=AF.Ln, bias=LN_EPS), r=[kB("sgst")], w=[kB("sgst")])
                                yield
                                S.op(ACT, lambda h: h.activation(out=sgst[:, 8:9], in_=sgst[:, 9:10], func=AF.Exp, scale=-0.5), r=[kB("sgst")], w=[kB("sgst")])
                                yield
                                S.op(DVE, lambda h: h.tensor_scalar(out=vn[:], in0=gl[:, 256:512], scalar1=sgst[:, 6:7],
                                                                    scalar2=sgst[:, 8:9], op0=ALU.subtract, op1=ALU.mult),
                                     r=[kB("gl"), kB("sgst")], w=[kB("vn")])
                                yield
                                S.op(POOL, lambda h: h.tensor_tensor(out=vn[:], in0=vn[:], in1=gvB[:], op=ALU.mult), r=[kB("vn"), B("cA")], w=[kB("vn")])
                                yield
                                S.op(POOL, lambda h: h.tensor_tensor(out=vn[:], in0=vn[:], in1=bvB[:], op=ALU.add), r=[kB("vn"), B("cA")], w=[kB("vn")])
                                yield
                                if samp:
                                    S.dma(POOL, ogv[l, s_i], vn[:], r=[kB("vn")], w=[B("ogv", l, s_i)])
                                S.op(POOL, lambda h: h.tensor_copy(out=vb[:], in_=vn[:]), r=[kB("vn")], w=[kB("vb")])
                                yield
                                Wm = WsTs if samp else WsT
                                S.group(PE, [(lambda gg: lambda h: h.matmul(pM[:, gg * 64:(gg + 1) * 64], lhsT=Wm[:, gg, :],
                                                                             rhs=vb[:, gg * 64:(gg + 1) * 64], start=True, stop=True))(gg)
                                             for gg in range(4)], r=[kB("vb"), B("cA")], w=[B("pM")])
                                S.op(DVE, lambda h: h.tensor_tensor(out=sgb[:], in0=pM[:, 0:256], in1=bsB[:], op=ALU.add),
                                     r=[B("pM"), B("cA")], w=[kB("sgb")])
                                yield
                                S.op(POOL, lambda h: h.tensor_tensor(out=sgb[:], in0=sgb[:], in1=gl[:, 0:256], op=ALU.mult),
                                     r=[kB("sgb"), kB("gl")], w=[kB("sgb")])
                                yield
                                S.op(POOL, lambda h: h.tensor_tensor(out=g2[:, 0:256], in0=sgb[:], in1=sgb[:], op=ALU.mult),
                                     r=[kB("sgb"), kB("g2")], w=[kB("g2")])
                                yield
                                S.op(DVE, lambda h: h.reduce_sum(out=sgst[:, 10:14], in_=g2[:, 0:256].rearrange("p (g c) -> p g c", c=64),
                                                                 axis=mybir.AxisListType.X), r=[kB("g2"), kB("sgst")], w=[kB("sgst")])
                                yield
                                S.op(ACT, lambda h: h.activation(out=sgst[:, 10:14], in_=sgst[:, 10:14], func=AF.Ln, bias=RMS_EPS,
                                                                 scale=1.0 / 64.0), r=[kB("sgst")], w=[kB("sgst")])
                                yield
                                S.op(ACT, lambda h: h.activation(out=sgst[:, 10:14], in_=sgst[:, 10:14], func=AF.Exp, scale=-0.5),
                                     r=[kB("sgst")], w=[kB("sgst")])
                                yield
                                S.op(DVE, lambda h: h.tensor_tensor(out=og[:, t, :].rearrange("p (g c) -> p g c", c=64),
                                                                    in0=sgb[:].rearrange("p (g c) -> p g c", c=64),
                                                                    in1=sgst[:, 10:14].unsqueeze(2).to_broadcast([128, 4, 64]),
                                                                    op=ALU.mult), r=[kB("sgb"), kB("sgst")], w=[B("og")])
                                yield
                            else:
                                li = slot
                                lb_ = kB("lfw")
                                S.op(DVE, lambda h: h.tensor_tensor(out=lfw[:, li, 0:6], in0=psb[:, 0:6], in1=bfB[:], op=ALU.add),
                                     r=[psB, B("cA")], w=[lb_])
                                yield
                                S.op(ACT, lambda h: h.activation(out=lfw[:, li, 6:12], in_=lfw[:, li, 0:6], func=AF.Exp, scale=-1.0), r=[lb_], w=[lb_])
                                yield
                                S.op(ACT, lambda h: h.activation(out=lfw[:, li, 12:18], in_=lfw[:, li, 6:12], func=AF.Ln, bias=1.0), r=[lb_], w=[lb_])
                                yield
                                S.op(DVE, lambda h: h.tensor_scalar(out=lfw[:, li, 18:24], in0=lfw[:, li, 12:18], scalar1=-1.0,
                                                                    scalar2=None, op0=ALU.mult), r=[lb_], w=[lb_])
                                yield
                                S.dma(POOL, olf[l, t], lfw[:, li, 18:24], r=[lb_], w=[B("olf", l, t)])
                                if samp:
                                    S.op(POOL, lambda h: h.tensor_copy(out=slfnew[:, s_i, :], in_=lfw[:, li, 18:24]), r=[lb_], w=[B("slfnew")])
                                else:
                                    S.dma(POOL, lf_in[l][t * 128:(t + 1) * 128, :], lfw[:, li, 18:24], r=[lb_], w=[B("lf_in", l, t)])
                                yield
                        for oi_, (oap, c0) in enumerate([(okf, 0), (ovf, 384), (oks, 768), (ovs, 1152)]):
                            S.dma(POOL, oap[l, t], kvout[slot][:, c0:c0 + 384], r=[kvb], w=[B("okv", l, t, oi_)])
                        yield
                        if samp:
                            for hf, c0 in ((0, 384), (1, 1152)):
                                S.op(POOL, lambda h: h.tensor_copy(
                                    out=svnew[:, s_i, hf * 390:(hf + 1) * 390].rearrange("p (a c) -> p a c", c=65)[:, :, 0:64],
                                    in_=kvout[slot][:, c0:c0 + 384].rearrange("p (a c) -> p a c", c=64)), r=[kvb], w=[B("svnew")])
                                yield
                        else:
                            for hf, c0 in ((0, 384), (1, 1152)):
                                S.op(POOL, lambda h: h.tensor_copy(
                                    out=vaug[slot][:, hf * 390:(hf + 1) * 390].rearrange("p (a c) -> p a c", c=65)[:, :, 0:64],
                                    in_=kvout[slot][:, c0:c0 + 384].rearrange("p (a c) -> p a c", c=64)), r=[kvb], w=[B("vaug", slot)])
                                yield
                            for hf, dst in ((0, vf_in[l]), (1, vs_in[l])):
                                S.dma(POOL, dst[t * 128:(t + 1) * 128, :], vaug[slot][:, hf * 390:(hf + 1) * 390],
                                      r=[B("vaug", slot)], w=[B("v_in", l, t, hf)])
                            yield

                    def a_feat(slot, g):
                        gb = g % 3
                        xr = [B("xTA", gb, tl) for tl in range(4)]
                        for cc in range(12):
                            fi = nxt("poA", 2)
                            pf = [pO0, pO1][fi]
                            pfB = B("pO", fi)
                            S.group(PE, [(lambda kc: lambda h: h.matmul(pf[:, 0:512], lhsT=wfeat[:, kc, cc * 128:(cc + 1) * 128],
                                                                         rhs=xT[gb][:, kc, :], start=kc == 0, stop=kc == 7))(kc)
                                         for kc in range(8)], r=xr + WFEAT_R, w=[pfB])
                            yield
                            e = ACT if cc % 2 == 0 else DVE
                            if cc < 3 or 6 <= cc < 9:
                                pr = cc if cc < 3 else cc - 3
                                S.op(e, cast_op(e, qT[:, pr, g * 512:(g + 1) * 512], pf[:, 0:512]), r=[pfB], w=[B("qT")])
                            else:
                                pr = cc - 3 if cc < 6 else cc - 9
                                fox = cc < 6
                                if g == 4:
                                    S.op(e, cast_op(e, skTn[:, pr + (0 if fox else 3), :], pf[:, 0:512]), r=[pfB], w=[B("skTn")])
                                else:
                                    ksi = nxt("kst", 3)
                                    S.op(e, cast_op(e, kst[ksi][:], pf[:, 0:512]), r=[pfB], w=[B("kst", ksi)])
                                    dst = kTf_in[l] if fox else kTs_in[l]
                                    S.dma(POOL, dst[pr * 128:(pr + 1) * 128, g * 512:(g + 1) * 512], kst[ksi][:],
                                          r=[B("kst", ksi)], w=[B("kT_in", l, g, cc)])
                            yield

                    for tl in range(4):
                        a_prep(tl)
                    tasks = []
                    for g in range(5):
                        for tl in range(4):
                            tasks.append(("task", lambda slot, t=4 * g + tl: a_compute(slot, t)))
                            if g + 1 < 5:
                                tasks.append(("now", lambda t=4 * (g + 1) + tl: a_prep(t)))
                        if FLAGS["featdrain"]:
                            def feat_now(g=g):
                                for _ in a_feat(0, g):
                                    pass
                            tasks.append(("setup", feat_now))
                        else:
                            tasks.append(("task", lambda slot, g=g: a_feat(slot, g)))
                    run_streams(tasks, FLAGS['na'])
                    S.barrier()
                for src, dst, nm in ((kTf_in, kTf_g, "kTf"), (kTs_in, kTs_g, "kTs"), (vf_in, vf_g, "vf"),
                                     (vs_in, vs_g, "vs"), (lf_in, lf_g, "lf")):
                    S.cc(src[l].ap().opt(), dst[l].ap().opt(), r=[], w=[B("g_" + nm, l)])
                S.barrier()

                with ExitStack() as BC:
                    oatt = T(BC, "oatt", [128, NT, 768], BF16)
                    with ExitStack() as Bp:
                        W = attn_work(Bp)
                        KTp = [T(Bp, f"KTp{i}", [128, 32 * 128], BF16) for i in range(2)]
                        Vaug = T(Bp, "Vaug", [128, 32, 390], BF16)
                        ckB = T(Bp, "ckB", [128, 4096], F32)
                        lfT = T(Bp, "lfT", [128, 32, 6], F32)
                        cq = T(Bp, "cq", [128, 96], F32)
                        mAB_f = T(Bp, "mAB_f", [128, 2, 256], F32)
                        mAB_b = T(Bp, "mAB_b", [128, 2, 256], BF16)
                        for kd in range(2):
                            S.op(POOL, lambda h, kd=kd: h.tensor_copy(out=mAB_f[:, kd, :].rearrange("p (a q) -> p a q", q=128),
                                                                      in_=cstf[:, 3 + 2 * kd:5 + 2 * kd, :]), r=[B("cstf")], w=[B("cstf2")])
                        S.op(POOL, lambda h: h.tensor_copy(out=mAB_b[:], in_=mAB_f[:]), r=[B("cstf2")], w=[B("cstb2")])
                        order = [(g % 2) * 16 + g // 2 for g in range(32)]
                        S.dma(SP, lfT[:], lf_g[l].ap().rearrange("(b t) h -> t b h", t=128), r=[B("g_lf", l)], w=[B("lfT")])
                        c_compute(lfT, 32, order, W)
                        cT = W["cT"]
                        S.op(DVE, lambda h: h.tensor_scalar(out=cq[:], in0=cT[:, 0:96], scalar1=selt[:, 0:1], scalar2=None, op0=ALU.mult),
                             r=[B("cT"), B("selt")], w=[B("cq")])
                        S.op(DVE, lambda h: h.scalar_tensor_tensor(out=cq[:], in0=cT[:, 96:192], scalar=selt[:, 1:2], in1=cq[:],
                                                                   op0=ALU.mult, op1=ALU.add), r=[B("cT"), B("selt"), B("cq")], w=[B("cq")])
                        tasks = []
                        kbs = {}
                        for kind, kT_g, v_g in (("fox", kTf_g[l], vf_g[l]), ("sb", kTs_g[l], vs_g[l])):
                            kd = 0 if kind == "fox" else 1

                            def load_v(kind=kind, v_g=v_g):
                                for r_ in range(2):
                                    S.dma(SP, Vaug[:].rearrange("p (k r) c -> p k r c", r=2)[:, :, r_, :],
                                          v_g.ap()[r_ * 2048:(r_ + 1) * 2048, :].rearrange("(k t) c -> t k c", t=128),
                                          r=[B("g_vf" if kind == "fox" else "g_vs", l)], w=[B("V", "p")])
                            tasks.append(("setup", load_v))
                            for hh in range(6):
                                pair, half = hh // 2, hh % 2
                                hp = slice(half * 64, half * 64 + 64)
                                if half == 0:
                                    kb = nxt("KTp", 2)

                                    def load_k(kind=kind, kT_g=kT_g, pair=pair, kb=kb):
                                        for r_ in range(2):
                                            S.dma(SP, KTp[kb][:].rearrange("p (k r t) -> p k r t", r=2, t=128)[:, :, r_, :],
                                                  kT_g.ap()[r_ * 384 + pair * 128:r_ * 384 + (pair + 1) * 128, :].rearrange("p (k t) -> p k t", t=128),
                                                  r=[B("g_kTf" if kind == "fox" else "g_kTs", l)], w=[B("KT", ("p", kb))])
                                    tasks.append(("now", load_k))
                                if kind == "fox":
                                    tasks.append(("setup", lambda hh=hh: ckB_build(W, ckB, ("ckB",), hh, order, 32, 128)))
                                hcol = (0 if kind == "fox" else 384) + hh * 64
                                for k in range(NPB - 1, -1, -1):
                                    tasks.append(("task", lambda slot, kind=kind, hp=hp, pair=pair, kd=kd, k=k, kb=kb, hh=hh, hcol=hcol: attend(
                                        slot, kind, 128, qT[hp, pair + 3 * kd, k * 128:(k + 1) * 128],
                                        lambda lo, hi: KTp[kb][hp, lo * 128:hi * 128],
                                        lambda g_: Vaug[:, g_, hh * 65:(hh + 1) * 65],
                                        2 * k + 2, mAB_f[:, kd, :], mAB_b[:, kd, :], 256,
                                        cq[:, k * 6 + hh:k * 6 + hh + 1], ckB, ("ckB",), W, oatt[:, k, hcol:hcol + 64], (("p", kb), "p"))))
                        run_streams(tasks, FLAGS['nstream'])
                        S.barrier()
                    with ExitStack() as Bs:
                        W = attn_work(Bs)
                        sKT = T(Bs, "sKT", [128, 3, 17 * 128], BF16)
                        sV = T(Bs, "sV", [128, 17, 390], BF16)
                        ckBs = [T(Bs, f"ckBs{i}", [128, 17 * 128], F32) for i in range(2)]
                        lfS = T(Bs, "lfS", [128, 17, 6], F32)
                        cstk = [T(Bs, f"cstk{i}", [128, 2, 384], F32) for i in range(2)]
                        cstv = [T(Bs, f"cstv{i}", [128, 2, 384], F32) for i in range(2)]
                        order = list(range(17))
                        S.op(POOL, lambda h: h.memset(sV[:], 1.0), w=[B("V", "s")])
                        S.op(POOL, lambda h: h.memset(sKT[:], 0.0), w=[B("KT", "s")])
                        for s_i in range(NS):
                            t = NPB + s_i
                            for kind, ck, cv in (("fox", cfk, cfv), ("sb", csk, csv)):
                                kd = 0 if kind == "fox" else 1
                                tasks = []

                                def prep(kind=kind, ck=ck, cv=cv, kd=kd, s_i=s_i):
                                    for b2 in range(8):
                                        ci_ = nxt("cstk", 2)
                                        S.dma(SP, cstk[ci_][:], ck[l, s_i, b2 * 256:(b2 + 1) * 256, :].rearrange("(b t) c -> t b c", t=128),
                                              w=[B("cstk", ci_)])
                                        S.dma(SP, cstv[ci_][:], cv[l, s_i, b2 * 256:(b2 + 1) * 256, :].rearrange("(b t) c -> t b c", t=128),
                                              w=[B("cstv", ci_)])
                                        for bb_ in range(2):
                                            blk = b2 * 2 + bb_
                                            ri_ = nxt("ptr", 2)
                                            pt = [pA, pB][ri_]
                                            ptb_ = B("pS", ri_)
                                            S.group(PE, [(lambda j: lambda h: h.transpose(out=pt[:, j * 128:(j + 1) * 128],
                                                                                          in_=cstk[ci_][:, bb_, j * 128:(j + 1) * 128],
                                                                                          identity=identf))(j) for j in range(3)],
                                                    r=[B("cstk", ci_), B("cstf")], w=[ptb_])
                                            S.op(ACT, lambda h: h.copy(out=sKT[:, :, blk * 128:(blk + 1) * 128],
                                                                       in_=pt[:, 0:384].rearrange("p (a q) -> p a q", q=128)),
                                                 r=[ptb_], w=[B("KT", "s")])
                                        S.op(POOL, lambda h: h.tensor_copy(
                                            out=sV[:, b2 * 2:(b2 + 1) * 2, :].rearrange("p b (a c) -> p b a c", c=65)[:, :, :, 0:64],
                                            in_=cstv[ci_][:].rearrange("p b (a c) -> p b a c", c=64)), r=[B("cstv", ci_)], w=[B("V", "s")])
                                    S.op(POOL, lambda h: h.tensor_copy(out=sKT[:, :, 2048:2048 + 64],
                                                                       in_=skTn[:, 3 * kd:3 * kd + 3, s_i * 128:s_i * 128 + 64]),
                                         r=[B("skTn")], w=[B("KT", "s")])
                                    S.op(POOL, lambda h: h.memset(sV[:, 16, :], 0.0), w=[B("V", "s")])
                                    S.op(POOL, lambda h: h.tensor_copy(out=sV[0:64, 16, :], in_=svnew[0:64, s_i, kd * 390:(kd + 1) * 390]),
                                         r=[B("svnew")], w=[B("V", "s")])
                                    if kind == "fox":
                                        S.op(POOL, lambda h: h.memset(lfS[:, 16, :], 0.0), w=[B("lfT")])
                                        S.dma(SP, lfS[:, 0:16, :], cfl[l, s_i].rearrange("(b t) h -> t b h", t=128), w=[B("lfT")])
                                        S.op(POOL, lambda h: h.tensor_copy(out=lfS[0:64, 16, :], in_=slfnew[0:64, s_i, :]),
                                             r=[B("slfnew")], w=[B("lfT")])
                                        c_compute(lfS, 17, order, W)
                                tasks.append(("setup", prep))
                                for hh in range(6):
                                    pair, half = hh // 2, hh % 2
                                    hp = slice(half * 64, half * 64 + 64)
                                    hcol = (0 if kind == "fox" else 384) + hh * 64

                                    def mk(slot, kind=kind, hp=hp, pair=pair, kd=kd, hh=hh, hcol=hcol, t=t):
                                        cb = ckBs[slot % 2]
                                        key = ("ckBs", slot % 2)
                                        if kind == "fox":
                                            ckB_build(W, cb, key, hh, order, 17, 64)
                                        return attend(slot, kind, 64, qT[hp, pair + 3 * kd, t * 128:t * 128 + 64],
                                                      lambda lo, hi: sKT[hp, pair, lo * 128:hi * 128],
                                                      lambda g_: sV[:, g_, hh * 65:(hh + 1) * 65],
                                                      17, cstf[0:64, 7 + kd, :], cstb[0:64, 7 + kd, :], 128,
                                                      W["cT"][0:64, 16 * 6 + hh:16 * 6 + hh + 1], cb, key, W,
                                                      oatt[0:64, t, hcol:hcol + 64], ("s", "s"))
                                    tasks.append(("task", mk))
                                run_streams(tasks, min(FLAGS['nstream'], 2 if kind == "fox" else NSLOT))
                        S.barrier()
                    with ExitStack() as C1:
                        wout = T(C1, "wout", [128, 8, D], BF16)
                        stg = [T(C1, f"stgC{i}", [128, D], F32) for i in range(2)]
                        gmT = T(C1, "gmT", [128, 8], F32)
                        g1B = T(C1, "g1B", [128, D], F32)
                        b1B = T(C1, "b1B", [128, D], F32)
                        oT = [T(C1, f"oT{i}", [128, 8, 128], BF16) for i in range(3)]
                        xt = [T(C1, f"xtC{i}", [128, D], F32) for i in range(3)]
                        res = [T(C1, f"resC{i}", [128, D], F32) for i in range(2)]
                        xn = [T(C1, f"xnC{i}", [128, D], F32) for i in range(2)]
                        stt = [T(C1, f"stC{i}", [128, 16], F32) for i in range(2)]
                        S.dma(SP, gmT[:], g_mixT[l], w=[B("gmT")])
                        S.dma(SP, g1B[:], ln1_g[l:l + 1, :].partition_broadcast(128), w=[B("cC")])
                        S.dma(SP, b1B[:], ln1_b[l:l + 1, :].partition_broadcast(128), w=[B("cC")])
                        for kc in range(8):
                            load_cast(stg, wout[:, kc, :], w_out[l, kc * 128:(kc + 1) * 128, :], D, B("wout", kc),
                                      scale=gmT[:, kc:kc + 1], engs=[POOL, DVE])
                        WOUT_R = [B("wout", kc) for kc in range(8)] + [B("gmT")]
                        def c1_prep(t):
                            bi = t % 3
                            S.dma(SP, xt[bi][:], xsrc[t], w=[B("xtC", bi)])
                            ti = nxt("pt", 2)
                            pt = [pT0, pT1][ti]

                            def osrc(j, t=t):
                                if j < 3:
                                    return oatt[:, t, j * 128:(j + 1) * 128]
                                if j < 5:
                                    return og[:, t, (j - 3) * 128:(j - 2) * 128]
                                return oatt[:, t, 384 + (j - 5) * 128:384 + (j - 4) * 128]
                            transposes(pt, B("pT", ti), osrc, 8, r=[B("oatt"), B("og")])
                            oi_ = t % 3
                            S.op(ACT, lambda h: h.copy(out=oT[oi_][:], in_=pt[:].rearrange("p (k q) -> p k q", q=128)),
                                 r=[B("pT", ti)], w=[B("oT", oi_)])

                        def c1_banks(t):
                            return [(pA, B("pS", 0)), (pB, B("pS", 1))] if t % 2 == 0 else [(pO0, B("pO", 0)), (pO1, B("pO", 1))]

                        def c1_mix(t):
                            oi_ = t % 3
                            for n_, (pb_, pbB) in enumerate(c1_banks(t)):
                                S.group(PE, [(lambda kc: lambda h: h.matmul(pb_[:, 0:512], lhsT=oT[oi_][:, kc, :],
                                                                             rhs=wout[:, kc, n_ * 512:(n_ + 1) * 512],
                                                                             start=kc == 0, stop=kc == 7))(kc) for kc in range(8)],
                                        r=[B("oT", oi_)] + WOUT_R, w=[pbB])

                        def c1_post(t):
                            bi = t % 3
                            ri = t % 2
                            for n_, (pb_, pbB) in enumerate(c1_banks(t)):
                                S.op(DVE, lambda h: h.scalar_tensor_tensor(
                                    out=res[ri][:, n_ * 512:(n_ + 1) * 512], in0=xt[bi][:, n_ * 512:(n_ + 1) * 512], scalar=ALPHA,
                                    in1=pb_[:, 0:512], op0=ALU.mult, op1=ALU.add), r=[B("xtC", bi), pbB], w=[B("resC", ri)])
                            layer_norm_tile(res[ri][:], xn[ri][:], stt[ri], g1B[:], b1B[:], B("resC", ri), B("xnC", ri), B("stC", ri), B("cC"))
                            S.dma(POOL, xmid.ap()[t], xn[ri][:], r=[B("xnC", ri)], w=[B("xmid", t)])

                        if FLAGS["c1skew"]:
                            c1_prep(0)
                            c1_prep(1)
                            for t in range(NT):
                                c1_mix(t)
                                if t + 2 < NT:
                                    c1_prep(t + 2)
                                c1_post(t)
                        else:
                            for t in range(NT):
                                c1_prep(t)
                                c1_mix(t)
                                c1_post(t)
                        S.barrier()
            with ExitStack() as C2:
                wup = T(C2, "wup", [128, 8, 4 * D], BF16)
                wdn = T(C2, "wdn", [128, 32, D], BF16)
                g2B = T(C2, "g2B", [128, D], F32)
                b2B = T(C2, "b2B", [128, D], F32)
                S.dma(SP, g2B[:], ln2_g[l:l + 1, :].partition_broadcast(128), w=[B("cD")])
                S.dma(SP, b2B[:], ln2_b[l:l + 1, :].partition_broadcast(128), w=[B("cD")])
                with ExitStack() as C2a:
                    stg = [T(C2a, f"stgD{i}", [128, 2048], F32) for i in range(3)]
                    for kc in range(8):
                        for hf in range(2):
                            load_cast(stg, wup[:, kc, hf * 2048:(hf + 1) * 2048], w_up[l, kc * 128:(kc + 1) * 128, hf * 2048:(hf + 1) * 2048],
                                      2048, B("wup", kc))
                    for fc2 in range(16):
                        load_cast(stg, wdn[:, 2 * fc2:2 * fc2 + 2, :].rearrange("p a c -> p (a c)"),
                                  w_down[l, fc2 * 256:(fc2 + 1) * 256, :].rearrange("(a p) c -> p a c", p=128), 2048, B("wdn", fc2),
                                  view=lambda a: a.rearrange("p (a c) -> p a c", a=2))
                    S.barrier()
                with ExitStack() as C2b:
                    xt = [T(C2b, f"xtD{i}", [128, D], F32) for i in range(4)]
                    xb = [T(C2b, f"xbD{i}", [128, D], BF16) for i in range(2)]
                    x1T = [T(C2b, f"x1T{i}", [128, 8, 256], BF16) for i in range(2)]
                    hT = [T(C2b, f"hT{i}", [128, 256], BF16) for i in range(4)]
                    hr = [T(C2b, f"hr{i}", [128, 256], F32) for i in range(3)]
                    res = [T(C2b, f"resD{i}", [128, D], F32) for i in range(2)]
                    xn = [T(C2b, f"xnD{i}", [128, D], F32) for i in range(2)]
                    stt = [T(C2b, f"stD{i}", [128, 16], F32) for i in range(2)]
                    accs = [[(pA, B("pS", 0)), (pB, B("pS", 1))], [(pO0, B("pO", 0)), (pO1, B("pO", 1))]]
                    WUP_R = [B("wup", kc) for kc in range(8)]

                    def c2_prep(gp):
                        xg = gp % 2
                        for tl in range(2):
                            t = 2 * gp + tl
                            bi = (2 * gp + tl) % 4
                            S.dma(SP, xt[bi][:], xmid.ap()[t], r=[B("xmid", t)], w=[B("xtD", bi)])
                            ci_ = nxt("xbD", 2)
                            S.op(POOL, lambda h: h.tensor_copy(out=xb[ci_][:], in_=xt[bi][:]), r=[B("xtD", bi)], w=[B("xbD", ci_)])
                            ti = nxt("pt", 2)
                            pt = [pT0, pT1][ti]
                            transposes(pt, B("pT", ti), lambda j: xb[ci_][:, j * 128:(j + 1) * 128], 8, r=[B("xbD", ci_)])
                            S.op(ACT, lambda h: h.copy(out=x1T[xg][:, :, tl * 128:(tl + 1) * 128],
                                                       in_=pt[:].rearrange("p (k q) -> p k q", q=128)),
                                 r=[B("pT", ti)], w=[B("x1T", xg)])

                    def c2_up(gp, fc):
                        xg = gp % 2
                        ph, phB = [(pC, B("pS", 2)), (pM, B("pM"))][fc % 2]
                        S.group(PE, [(lambda kc: lambda h: h.matmul(ph[:, 0:256], lhsT=wup[:, kc, fc * 128:(fc + 1) * 128],
                                                                     rhs=x1T[xg][:, kc, :], start=kc == 0, stop=kc == 7))(kc)
                                     for kc in range(8)], r=[B("x1T", xg)] + WUP_R, w=[phB])
                        hj = fc % 4
                        rj = fc % 3
                        S.op(ACT, lambda h: h.activation(out=hr[rj][:], in_=ph[:, 0:256], func=AF.Relu), r=[phB], w=[B("hr", rj)])
                        S.op(DVE if fc % 2 == 0 else POOL, lambda h: h.tensor_tensor(out=hT[hj][:], in0=hr[rj][:], in1=hr[rj][:], op=ALU.mult),
                             r=[B("hr", rj)], w=[B("hT", hj)])

                    def c2_down(gp, fc):
                        hj = fc % 4
                        for tl in range(2):
                            for n_ in range(2):
                                pb_, pbB = accs[tl][n_]
                                S.group(PE, [lambda h: h.matmul(pb_[:, 0:512], lhsT=hT[hj][:, tl * 128:(tl + 1) * 128],
                                                                rhs=wdn[:, fc, n_ * 512:(n_ + 1) * 512], start=fc == 0, stop=fc == 31)],
                                        r=[B("hT", hj), B("wdn", fc // 2)], w=[pbB])

                    def c2_post(gp):
                        for tl in range(2):
                            t = 2 * gp + tl
                            bi = (2 * gp + tl) % 4
                            ri = tl
                            for n_ in range(2):
                                pb_, pbB = accs[tl][n_]
                                S.op(DVE, lambda h: h.scalar_tensor_tensor(
                                    out=res[ri][:, n_ * 512:(n_ + 1) * 512], in0=xt[bi][:, n_ * 512:(n_ + 1) * 512], scalar=ALPHA,
                                    in1=pb_[:, 0:512], op0=ALU.mult, op1=ALU.add), r=[B("xtD", bi), pbB], w=[B("resD", ri)])
                            layer_norm_tile(res[ri][:], xn[ri][:], stt[ri], g2B[:], b2B[:], B("resD", ri), B("xnD", ri), B("stD", ri), B("cD"))
                            S.dma(POOL, xdst[t], xn[ri][:], r=[B("xnD", ri)], w=[B("xdst", l, t)])

                    NG = NT // 2
                    if FLAGS["c2skew"]:
                        c2_prep(0)
                        for gp in range(NG):
                            c2_up(gp, 0)
                            c2_up(gp, 1)
                            if gp + 1 < NG:
                                c2_prep(gp + 1)
                            for fc in range(32):
                                if fc + 2 < 32:
                                    c2_up(gp, fc + 2)
                                c2_down(gp, fc)
                            c2_post(gp)
                    else:
                        for gp in range(NG):
                            c2_prep(gp)
                            for fc in range(32):
                                c2_up(gp, fc)
                                c2_down(gp, fc)
                            c2_post(gp)
                    S.barrier()
        S.barrier()
    return nc


_NC = None


def _get_nc():
    global _NC
    if _NC is None:
        _NC = build_nc()
    return _NC


def kernel(x_prompt, x_sample, cache_fox_k, cache_fox_v, cache_fox_logf, cache_sb_k, cache_sb_v,
           w_in, b_f, g_v, b_v, w_s, b_s, g_mix, w_out, ln1_g, ln1_b, w_up, w_down, ln2_g, ln2_b):
    f32 = lambda a: np.ascontiguousarray(np.asarray(a), dtype=np.float32)
    x_prompt, x_sample = f32(x_prompt), f32(x_sample)
    w_in = f32(w_in)
    sp = np.cumsum([384, 384, 384, 6, 256, 256, 384, 384, 384])
    seg = lambda i: slice(0 if i == 0 else sp[i - 1], sp[i])
    qf, kf, vf, fl, ug, vg, qs, ks, vs = [w_in[:, :, seg(i)] for i in range(9)]
    w_tok = np.ascontiguousarray(np.concatenate([kf, vf, ks, vs, ug, vg, fl], axis=2))
    w_feat = np.ascontiguousarray(np.concatenate([qf, kf, qs, ks], axis=2))
    w_sT = np.ascontiguousarray(np.transpose(f32(w_s), (0, 1, 3, 2)))
    b_sT = np.ascontiguousarray(np.transpose(f32(b_s), (0, 2, 1)))
    g_mixT = np.ascontiguousarray(np.transpose(f32(g_mix).reshape(DEPTH, 8, 128), (0, 2, 1)))

    ii = np.arange(128)
    tri_le = (ii[:, None] <= ii[None, :]).astype(np.float32)
    ident = np.eye(128, dtype=np.float32)
    ones = np.ones((128, 128), np.float32)
    zeros = np.zeros((128, 128), np.float32)
    sgs = tri_le * (ii[:, None] < 64) * (ii[None, :] < 64)
    fox_tri = (ii[None, :] <= ii[:, None]).astype(np.float32)
    sb_tri = (ii[None, :] < ii[:, None]).astype(np.float32)

    shared = dict(w_tok=w_tok, w_feat=w_feat, b_f=f32(b_f), g_v=f32(g_v), b_v=f32(b_v), w_sT=w_sT, b_sT=b_sT,
                  g_mixT=g_mixT, w_out=f32(w_out), ln1_g=f32(ln1_g), ln1_b=f32(ln1_b), ln2_g=f32(ln2_g),
                  ln2_b=f32(ln2_b), w_up=f32(w_up), w_down=f32(w_down))
    cfk_a = f32(cache_fox_k).reshape(DEPTH, 32, PAST, 384)
    cfv_a = f32(cache_fox_v).reshape(DEPTH, 32, PAST, 384)
    csk_a = f32(cache_sb_k).reshape(DEPTH, 32, PAST, 384)
    csv_a = f32(cache_sb_v).reshape(DEPTH, 32, PAST, 384)
    cfl_a = f32(cache_fox_logf)
    in_maps = []
    for c in range(8):
        b, j = c // 2, c % 2
        xin = np.zeros((NT, 128, D), np.float32)
        xin[:NPB] = x_prompt[b].reshape(32, 128, D)[j::2]
        xin[NPB:, :64] = x_sample[4 * c:4 * c + 4]
        if j == 0:
            msk = [fox_tri, zeros, sb_tri, zeros]
        else:
            msk = [ones, fox_tri, ones, sb_tri]
        cst = np.stack([ident, tri_le, sgs] + msk + [fox_tri, sb_tri]).astype(np.float32)
        sel = np.zeros((128, 2), np.float32)
        sel[:, j] = 1.0
        m = dict(shared)
        m.update(xin=xin, cfk=np.ascontiguousarray(cfk_a[:, 4 * c:4 * c + 4]), cfv=np.ascontiguousarray(cfv_a[:, 4 * c:4 * c + 4]),
                 csk=np.ascontiguousarray(csk_a[:, 4 * c:4 * c + 4]), csv=np.ascontiguousarray(csv_a[:, 4 * c:4 * c + 4]),
                 cfl=np.ascontiguousarray(cfl_a[:, 4 * c:4 * c + 4]), cst=cst, sel=sel)
        in_maps.append(m)

    res = run_bass_kernel_spmd(_get_nc(), in_maps, core_ids=list(range(8)))
    R = res.results

    y_p = np.zeros((4, 32, 128, D), np.float32)
    y_s = np.zeros((32, 64, D), np.float32)
    pk = {n: np.zeros((DEPTH, 4, 32, 128, 384), np.float32) for n in ("okf", "ovf", "oks", "ovs")}
    pl = np.zeros((DEPTH, 4, 32, 128, 6), np.float32)
    sk = {n: np.zeros((DEPTH, 32, 64, 384), np.float32) for n in ("okf", "ovf", "oks", "ovs")}
    sl = np.zeros((DEPTH, 32, 64, 6), np.float32)
    sg = np.zeros((DEPTH, 32, 64, 256), np.float32)
    for c in range(8):
        b, j = c // 2, c % 2
        r = R[c]
        y_p[b, j::2] = r["y"][:NPB]
        y_s[4 * c:4 * c + 4] = r["y"][NPB:, :64]
        for n in pk:
            pk[n][:, b, j::2] = r[n][:, :NPB]
            sk[n][:, 4 * c:4 * c + 4] = r[n][:, NPB:, :64]
        pl[:, b, j::2] = r["olf"][:, :NPB]
        sl[:, 4 * c:4 * c + 4] = r["olf"][:, NPB:, :64]
        sg[:, 4 * c:4 * c + 4] = r["ogv"][:, :, :64]
    P5 = lambda a: a.reshape(DEPTH, 4, 4096, 6, 64)
    S5 = lambda a: a.reshape(DEPTH, 32, 64, 6, 64)
    return (y_p.reshape(4, 4096, D), y_s,
            P5(pk["okf"]), P5(pk["ovf"]), pl.reshape(DEPTH, 4, 4096, 6), P5(pk["oks"]), P5(pk["ovs"]),
            S5(sk["okf"]), S5(sk["ovf"]), sl, S5(sk["oks"]), S5(sk["ovs"]), sg)
```

```python
import math
from contextlib import ExitStack

import numpy as np
import concourse.bass as bass
import concourse.mybir as mybir
from concourse.bass_utils import run_bass_kernel_spmd

F32 = mybir.dt.float32
BF16 = mybir.dt.bfloat16
AF = mybir.ActivationFunctionType
ALU = mybir.AluOpType

DEPTH = 2
D = 1024
NT = 20
NPB = 16
NS = 4
PAST = 2048
ALPHA = (2 * DEPTH) ** 0.25
LN_EPS = 1e-5
RMS_EPS = 1e-6
GC = math.sqrt(2.0 / math.pi)
WTOK = 2054
WFEAT = 1536
GROUPS = [[0, 1], [2, 3], [4, 5], [6, 7]]
FLAGS = {"c1skew": True, "c2skew": True, "nstream": 4, "na": 2, "featdrain": 0}


class Buf:
    __slots__ = ("w", "r")

    def __init__(self):
        self.w = None
        self.r = {}


class Eng:
    def __init__(self, name, h):
        self.name = name
        self.h = h
        self.sid = None
        self.n = 0
        self.epoch = 0
        self.seen = {}


class Sched:
    NDMA = 24
    LIMIT = 12000

    def __init__(self, nc, stack):
        self.nc = nc
        self.stack = stack
        self.sems = {}
        self.bufs = {}
        self.pe = Eng("pe", nc.tensor)
        self.act = Eng("act", nc.scalar)
        self.dve = Eng("dve", nc.vector)
        self.pool = Eng("pool", nc.gpsimd)
        self.sp = Eng("sp", nc.sync)
        self.engs = [self.pe, self.act, self.dve, self.pool, self.sp]
        self.last = {}
        for e in self.engs:
            self._new_epoch(e)
        self.dsem = [self._mk(f"d{i}") for i in range(self.NDMA)]
        self.duse = [0] * self.NDMA
        self.dnext = 0
        self.cc_n = 0
        self.cc_toks = []

    def _mk(self, name):
        s = self.stack.enter_context(self.nc.semaphore("s_" + name))
        self.sems[name] = s
        return s

    def _new_epoch(self, e):
        if e.sid is not None:
            self.last[e.sid] = e.n
        e.sid = f"{e.name}{e.epoch}"
        e.epoch += 1
        e.n = 0
        self._mk(e.sid)

    def B(self, *key):
        b = self.bufs.get(key)
        if b is None:
            b = Buf()
            self.bufs[key] = b
        return b

    def _wait(self, eng, tok):
        if tok is None:
            return
        sid, val = tok
        if eng.seen.get(sid, 0) >= val:
            return
        eng.h.wait_ge(self.sems[sid], val)
        eng.seen[sid] = val

    def _own(self, eng, sid):
        return sid.startswith(eng.name) and sid[len(eng.name):].isdigit()

    def _pre(self, eng, r, w):
        for b in r:
            self._wait(eng, b.w)
        for b in w:
            if b.w is not None and not self._own(eng, b.w[0]):
                self._wait(eng, b.w)
            for t in b.r.items():
                if not self._own(eng, t[0]):
                    self._wait(eng, t)

    def _post(self, tok, r, w):
        for b in r:
            if b.r.get(tok[0], 0) < tok[1]:
                b.r[tok[0]] = tok[1]
        for b in w:
            b.w = tok
            b.r = {}

    def _tick(self, eng, ins):
        if eng.n >= self.LIMIT:
            self._new_epoch(eng)
        eng.n += 1
        ins.then_inc(self.sems[eng.sid], 1)
        return (eng.sid, eng.n)

    def op(self, eng, fn, r=(), w=()):
        self._pre(eng, r, w)
        tok = self._tick(eng, fn(eng.h))
        self._post(tok, r, w)

    def group(self, eng, fns, r=(), w=()):
        self._pre(eng, r, w)
        ins = None
        for fn in fns:
            ins = fn(eng.h)
        tok = self._tick(eng, ins)
        self._post(tok, r, w)

    def dma(self, q, out, in_, r=(), w=()):
        i = self.dnext
        self.dnext = (self.dnext + 1) % self.NDMA
        sid = f"d{i}"
        if self.duse[i] > 0:
            self._wait(q, (sid, 16 * self.duse[i]))
        self._pre(q, r, w)
        q.h.dma_start(out=out, in_=in_).then_inc(self.dsem[i], 16)
        self.duse[i] += 1
        self._post((sid, 16 * self.duse[i]), r, w)

    def cc(self, ins, outs, r=(), w=()):
        q = self.pool
        self._pre(q, r, w)
        self.cc_n += 1
        name = f"cc{self.cc_n}"
        sem = self._mk(name)
        q.h.collective_compute("AllGather", ALU.bypass, replica_groups=GROUPS,
                               ins=[ins], outs=[outs]).then_inc(sem)
        tok = (name, 1)
        self._post(tok, r, w)
        self.cc_toks.append(tok)
        self._wait(q, tok)

    def barrier(self):
        toks = [(e.sid, e.n) for e in self.engs if e.n > 0]
        toks += list(self.last.items())
        toks += [(f"d{i}", 16 * self.duse[i]) for i in range(self.NDMA) if self.duse[i] > 0]
        toks += self.cc_toks
        for e in self.engs:
            for t in toks:
                if t[1] > 0 and not self._own(e, t[0]):
                    self._wait(e, t)
        for b in self.bufs.values():
            b.w = None
            b.r = {}


def build_nc():
    nc = bass.Bass("TRN2", target_bir_lowering=False)

    def din(name, shape, dt=F32):
        return nc.dram_tensor(name, list(shape), dt, kind="ExternalInput").ap()

    def dout(name, shape):
        return nc.dram_tensor(name, list(shape), F32, kind="ExternalOutput").ap()

    def dint(name, shape, dt):
        return nc.dram_tensor(name, list(shape), dt)

    xin = din("xin", [NT, 128, D])
    cfk = din("cfk", [DEPTH, NS, PAST, 384])
    cfv = din("cfv", [DEPTH, NS, PAST, 384])
    csk = din("csk", [DEPTH, NS, PAST, 384])
    csv = din("csv", [DEPTH, NS, PAST, 384])
    cfl = din("cfl", [DEPTH, NS, PAST, 6])
    w_tok = din("w_tok", [DEPTH, D, WTOK])
    w_feat = din("w_feat", [DEPTH, D, WFEAT])
    b_f = din("b_f", [DEPTH, 6])
    g_v = din("g_v", [DEPTH, 256])
    b_v = din("b_v", [DEPTH, 256])
    w_sT = din("w_sT", [DEPTH, 4, 128, 128])
    b_sT = din("b_sT", [DEPTH, 128, 4])
    g_mixT = din("g_mixT", [DEPTH, 128, 8])
    w_out = din("w_out", [DEPTH, D, D])
    ln1_g = din("ln1_g", [DEPTH, D])
    ln1_b = din("ln1_b", [DEPTH, D])
    ln2_g = din("ln2_g", [DEPTH, D])
    ln2_b = din("ln2_b", [DEPTH, D])
    w_up = din("w_up", [DEPTH, D, 4 * D])
    w_down = din("w_down", [DEPTH, 4 * D, D])
    cst = din("cst", [9, 128, 128])
    sel = din("sel", [128, 2])

    y = dout("y", [NT, 128, D])
    okf = dout("okf", [DEPTH, NT, 128, 384])
    ovf = dout("ovf", [DEPTH, NT, 128, 384])
    oks = dout("oks", [DEPTH, NT, 128, 384])
    ovs = dout("ovs", [DEPTH, NT, 128, 384])
    olf = dout("olf", [DEPTH, NT, 128, 6])
    ogv = dout("ogv", [DEPTH, NS, 128, 256])

    xmid = dint("xmid", [NT, 128, D], F32)
    xl1 = dint("xl1", [NT, 128, D], F32)
    kTf_in = [dint(f"kTf_in{l}", [384, 2048], BF16) for l in range(DEPTH)]
    kTs_in = [dint(f"kTs_in{l}", [384, 2048], BF16) for l in range(DEPTH)]
    vf_in = [dint(f"vf_in{l}", [2048, 390], BF16) for l in range(DEPTH)]
    vs_in = [dint(f"vs_in{l}", [2048, 390], BF16) for l in range(DEPTH)]
    lf_in = [dint(f"lf_in{l}", [2048, 6], F32) for l in range(DEPTH)]
    kTf_g = [dint(f"kTf_g{l}", [768, 2048], BF16) for l in range(DEPTH)]
    kTs_g = [dint(f"kTs_g{l}", [768, 2048], BF16) for l in range(DEPTH)]
    vf_g = [dint(f"vf_g{l}", [4096, 390], BF16) for l in range(DEPTH)]
    vs_g = [dint(f"vs_g{l}", [4096, 390], BF16) for l in range(DEPTH)]
    lf_g = [dint(f"lf_g{l}", [4096, 6], F32) for l in range(DEPTH)]

    with ExitStack() as top:
        S = Sched(nc, top)
        B = S.B
        PE, ACT, DVE, POOL, SP = S.pe, S.act, S.dve, S.pool, S.sp

        uniq = [0]

        def T(stack, name, shape, dt):
            uniq[0] += 1
            return stack.enter_context(nc.sbuf_tensor(f"{name}_{uniq[0]}", list(shape), dt))

        def PS(name, shape, dt):
            return top.enter_context(nc.psum_tensor(name, list(shape), dt))

        pk = [PS(f"pk{i}", [128, 512], F32) for i in range(8)]
        pA, pB, pC, pM, pO0, pO1, pT0f, pT1f = pk
        pT0 = pT0f[:].bitcast(BF16)
        pT1 = pT1f[:].bitcast(BF16)

        cstf = T(top, "cstf", [128, 9, 128], F32)
        cstb = T(top, "cstb", [128, 9, 128], BF16)
        ones512 = T(top, "ones512", [128, 514], F32)
        onesf = T(top, "onesf", [128, 128], F32)
        selt = T(top, "selt", [128, 2], F32)
        S.dma(SP, cstf[:], cst.rearrange("c p q -> p c q"), w=[B("cstf")])
        S.dma(SP, selt[:], sel[:, :], w=[B("selt")])
        S.op(POOL, lambda h: h.tensor_copy(out=cstb[:], in_=cstf[:]), r=[B("cstf")], w=[B("cstb")])
        S.op(DVE, lambda h: h.memset(ones512[:], 1.0), w=[B("ones512")])
        S.op(DVE, lambda h: h.memset(onesf[:], 1.0), w=[B("onesf")])
        identf = cstf[:, 0, :]
        identb = cstb[:, 0, :]
        Uf = cstf[:, 1, :]
        CONST_R = [B("cstf"), B("cstb"), B("ones512"), B("onesf"), B("selt")]

        rot = {}

        def nxt(key, n):
            v = rot.get(key, 0) % n
            rot[key] = (v + 1) % n
            return v

        cast_engs = [POOL, DVE, ACT]

        def cast_op(eng, out, in_, scale=None):
            if scale is not None:
                if eng is ACT:
                    return lambda h: h.activation(out=out, in_=in_, func=AF.Copy, scale=scale)
                return lambda h: h.tensor_scalar(out=out, in0=in_, scalar1=scale, scalar2=None, op0=ALU.mult)
            if eng is ACT:
                return lambda h: h.copy(out=out, in_=in_)
            return lambda h: h.tensor_copy(out=out, in_=in_)

        def load_cast(stg, dst, src, ncols, wb, scale=None, engs=None, view=None):
            i = nxt("stg", len(stg))
            sv = stg[i][:, 0:ncols]
            S.dma(SP, view(sv) if view else sv, src, w=[B("stg", i)])
            engs = engs or cast_engs
            e = engs[nxt("casteng", len(engs))]
            S.op(e, cast_op(e, dst, stg[i][:, 0:ncols], scale), r=[B("stg", i)] + CONST_R, w=[wb])

        def transposes(pt, pbuf, src_fn, nblk, rows=128, r=()):
            fns = []
            for j in range(nblk):
                fns.append((lambda j: lambda h: h.transpose(out=pt[:, j * 128:j * 128 + rows], in_=src_fn(j),
                                                            identity=identb[0:rows, 0:rows]))(j))
            S.group(PE, fns, r=list(r) + [B("cstb")], w=[pbuf])

        def rstd_from(var_ap, out_ap, tmp_ap, scale, eps, bufs_r, buf_w):
            S.op(ACT, lambda h: h.activation(out=tmp_ap, in_=var_ap, func=AF.Ln, bias=eps, scale=scale),
                 r=bufs_r, w=[buf_w])
            S.op(ACT, lambda h: h.activation(out=out_ap, in_=tmp_ap, func=AF.Exp, scale=-0.5),
                 r=[buf_w], w=[buf_w])

        def layer_norm_tile(res, outt, stats, gB, bB, rb, ob, stb, constb):
            S.op(DVE, lambda h: h.bn_stats(out=stats[:, 0:6], in_=res[:, 0:512]), r=[rb], w=[stb])
            S.op(DVE, lambda h: h.bn_stats(out=stats[:, 6:12], in_=res[:, 512:1024]), r=[rb], w=[stb])
            S.op(DVE, lambda h: h.bn_aggr(out=stats[:, 12:14], in_=stats[:, 0:12]), r=[stb], w=[stb])
            rstd_from(stats[:, 13:14], stats[:, 14:15], stats[:, 15:16], 1.0, LN_EPS, [stb], stb)
            S.op(DVE, lambda h: h.scalar_tensor_tensor(out=stats[:, 15:16], in0=stats[:, 12:13], scalar=-1.0, in1=stats[:, 14:15],
                                                       op0=ALU.mult, op1=ALU.mult), r=[stb], w=[stb])
            S.op(ACT, lambda h: h.activation(out=outt, in_=res, func=AF.Identity, scale=stats[:, 14:15], bias=stats[:, 15:16]),
                 r=[rb, stb], w=[ob])
            S.op(DVE, lambda h: h.tensor_tensor(out=outt, in0=outt, in1=gB, op=ALU.mult), r=[ob, constb], w=[ob])
            S.op(POOL, lambda h: h.tensor_tensor(out=outt, in0=outt, in1=bB, op=ALU.add), r=[ob, constb], w=[ob])

        NSLOT = 4
        SBANK = [(pA, ("pS", 0)), (pB, ("pS", 1)), (pC, ("pS", 2)), (pM, ("pM",))]
        POBANK = [(pO0, ("pO", 0)), (pO1, ("pO", 1)), (pT0f, ("pT", 0)), (pT1f, ("pT", 1))]

        def attend(slot, kind, M, qT_ap, KT_fn, V_fn, nkb, mask_f, mask_b, maskw, cq_ap, ckB, ckb_key, W, out_ap, uid):
            ps_ = slice(0, M)
            chunks = []
            hi = nkb
            first = True
            while hi > 0:
                if first and maskw == 128:
                    lo = hi - 1
                else:
                    lo = max(0, hi - 4)
                chunks.append((lo, hi))
                hi = lo
                first = False
            po = POBANK[slot][0][:, 0:65]
            pob = B(*POBANK[slot][1])
            psb, pskey = SBANK[slot]
            psB = B(*pskey)
            pt = psb[:].bitcast(BF16)[:, 0:512]
            ptB = psB
            t1 = W["t1"][slot]
            e = W["e"][slot]
            lb = W["l"][slot]
            pin = W["pin"][slot]
            a = W["a"][slot]
            aT = W["aT"][slot]
            car = W["carry"]
            kB = lambda nm: B(nm, slot)
            if kind == "sb":
                S.op(POOL, lambda h: h.memset(car[:, slot, 0:1], 0.0), w=[B("carry", slot, 0)])
                yield
                cprev = 0
            nmm = 0
            for ci, (lo, hi) in enumerate(chunks):
                nb = hi - lo
                w = nb * 128
                S.group(PE, [lambda h: h.matmul(psb[ps_, 0:w], lhsT=qT_ap, rhs=KT_fn(lo, hi), start=True, stop=True)],
                        r=[B("qT"), B("KT", uid[0])], w=[psB])
                yield
                top_chunk = ci == 0
                if kind == "fox":
                    S.op(DVE, lambda h: h.scalar_tensor_tensor(
                        out=t1[ps_, 0:w], in0=psb[ps_, 0:w], scalar=0.125, in1=ckB[ps_, lo * 128:hi * 128],
                        op0=ALU.mult, op1=ALU.subtract), r=[psB, B(*ckb_key)], w=[kB("t1")])
                    yield
                    S.op(ACT, lambda h: h.activation(out=a[ps_, 0:w], in_=t1[ps_, 0:w], func=AF.Exp, bias=cq_ap),
                         r=[kB("t1"), B("cq")], w=[kB("a")])
                    yield
                else:
                    S.op(ACT, lambda h: h.activation(out=e[ps_, 0:w], in_=psb[ps_, 0:w], func=AF.Exp, scale=0.125),
                         r=[psB], w=[kB("e")])
                    yield
                    S.op(ACT, lambda h: h.activation(out=lb[ps_, 1:w + 1], in_=e[ps_, 0:w], func=AF.Ln, bias=1.0),
                         r=[kB("e")], w=[kB("l")])
                    yield
                    if top_chunk:
                        S.op(POOL, lambda h: h.tensor_tensor(out=lb[ps_, 1 + w - maskw:1 + w], in0=lb[ps_, 1 + w - maskw:1 + w],
                                                             in1=mask_f, op=ALU.mult), r=[kB("l"), B("cstf"), B("cstf2")], w=[kB("l")])
                        yield
                    S.op(DVE, lambda h: h.tensor_tensor_scan(
                        out=pin[ps_, 0:w + 1], data0=ones512[ps_, 0:w + 1], data1=lb[ps_, 0:w + 1], initial=0.0,
                        op0=ALU.mult, op1=ALU.add), r=[kB("l"), B("ones512")], w=[kB("pin")])
                    yield
                    cnew = 1 - cprev
                    S.op(POOL, lambda h: h.tensor_tensor(out=car[ps_, slot, cnew:cnew + 1], in0=car[ps_, slot, cprev:cprev + 1],
                                                         in1=pin[ps_, w:w + 1], op=ALU.subtract),
                         r=[B("carry", slot, cprev), kB("pin")], w=[B("carry", slot, cnew)])
                    yield
                    S.op(DVE, lambda h: h.scalar_tensor_tensor(
                        out=t1[ps_, 0:w], in0=psb[ps_, 0:w], scalar=0.125, in1=pin[ps_, 0:w],
                        op0=ALU.mult, op1=ALU.add), r=[psB, kB("pin")], w=[kB("t1")])
                    yield
                    S.op(ACT, lambda h: h.activation(out=a[ps_, 0:w], in_=t1[ps_, 0:w], func=AF.Exp, bias=car[ps_, slot, cnew:cnew + 1]),
                         r=[kB("t1"), B("carry", slot, cnew)], w=[kB("a")])
                    yield
                    cprev = cnew
                if top_chunk:
                    S.op(POOL, lambda h: h.tensor_tensor(out=a[ps_, w - maskw:w], in0=a[ps_, w - maskw:w], in1=mask_b, op=ALU.mult),
                         r=[kB("a"), B("cstb"), B("cstb2")], w=[kB("a")])
                    yield
                transposes(pt, ptB, lambda j: a[ps_, j * 128:(j + 1) * 128], nb, rows=M, r=[kB("a")])
                yield
                if M == 128:
                    S.op(ACT, lambda h: h.copy(out=aT[:, 0:w], in_=pt[:, 0:w]), r=[ptB], w=[kB("aT")])
                else:
                    S.op(ACT, lambda h: h.copy(
                        out=aT[:, 0:nb * 128].rearrange("p (b q) -> p b q", q=128)[:, :, 0:M],
                        in_=pt[:, 0:nb * 128].rearrange("p (b q) -> p b q", q=128)[:, :, 0:M]), r=[ptB], w=[kB("aT")])
                yield
                fns = []
                ncol = 65 if kind == "fox" else 64
                for j in range(nb):
                    st = nmm == 0
                    sp_ = nmm == nkb - 1
                    fns.append((lambda j, st, sp_: lambda h: h.matmul(
                        po[ps_, 0:ncol], lhsT=aT[:, j * 128:j * 128 + M], rhs=V_fn(lo + j)[:, 0:ncol],
                        start=st, stop=sp_))(j, st, sp_))
                    nmm += 1
                S.group(PE, fns, r=[kB("aT"), B("V", uid[1])], w=[pob])
                yield
            on = W["on"][slot]
            sq = W["sq"][slot]
            ss = W["ss"]
            eb = kB("ep")
            if kind == "fox":
                S.op(DVE, lambda h: h.reciprocal(out=ss[ps_, slot, 0:1], in_=po[ps_, 64:65]), r=[pob], w=[eb])
                yield
                S.op(DVE, lambda h: h.tensor_scalar(out=on[ps_, :], in0=po[ps_, 0:64], scalar1=ss[ps_, slot, 0:1], scalar2=None,
                                                    op0=ALU.mult), r=[pob, eb], w=[eb])
            else:
                S.op(DVE, lambda h: h.tensor_copy(out=on[ps_, :], in_=po[ps_, 0:64]), r=[pob], w=[eb])
            yield
            S.op(POOL, lambda h: h.memset(ss[ps_, slot, 1:2], 0.0), w=[eb])
            yield
            S.op(ACT, lambda h: h.activation(out=sq[ps_, :], in_=on[ps_, :], func=AF.Square, accum_out=ss[ps_, slot, 1:2]),
                 r=[eb], w=[eb])
            yield
            S.op(ACT, lambda h: h.activation(out=ss[ps_, slot, 2:3], in_=ss[ps_, slot, 1:2], func=AF.Ln, bias=RMS_EPS, scale=1.0 / 64.0),
                 r=[eb], w=[eb])
            yield
            S.op(ACT, lambda h: h.activation(out=ss[ps_, slot, 3:4], in_=ss[ps_, slot, 2:3], func=AF.Exp, scale=-0.5), r=[eb], w=[eb])
            yield
            S.op(DVE, lambda h: h.tensor_scalar(out=out_ap, in0=on[ps_, :], scalar1=ss[ps_, slot, 3:4], scalar2=None,
                                                op0=ALU.mult), r=[eb], w=[B("oatt")])
            yield

        def run_streams(tasks, n):
            active = []
            free = list(range(n))
            i = 0
            while i < len(tasks) or active:
                while i < len(tasks) and (free or tasks[i][0] != "task"):
                    kind_, f = tasks[i]
                    if kind_ == "now":
                        f()
                    elif kind_ == "setup":
                        if active:
                            break
                        f()
                    else:
                        slot = free.pop(0)
                        active.append((slot, f(slot)))
                    i += 1
                for item in list(active):
                    try:
                        next(item[1])
                    except StopIteration:
                        active.remove(item)
                        free.append(item[0])
                        free.sort()

        def c_compute(lfT, nblk, order, W, M=128):
            n = nblk * 6
            lf2 = lfT[:].rearrange("p b h -> p (b h)")
            S.group(PE, [lambda h: h.matmul(pM[:, 0:n], lhsT=Uf, rhs=lf2, start=True, stop=True)],
                    r=[B("lfT"), B("cstf")], w=[B("pM")])
            S.op(ACT, lambda h: h.copy(out=W["cw"][:, 0:n], in_=pM[:, 0:n]), r=[B("pM")], w=[B("cw")])
            S.group(PE, [lambda h: h.matmul(pM[:, 0:n], lhsT=onesf[:], rhs=lf2, start=True, stop=True)],
                    r=[B("lfT"), B("onesf")], w=[B("pM")])
            S.op(ACT, lambda h: h.copy(out=W["tot"][:, 0:n], in_=pM[:, 0:n]), r=[B("pM")], w=[B("tot")])
            offs = W["offs"]
            S.op(DVE, lambda h: h.memset(offs[:, order[0] * 6:order[0] * 6 + 6], 0.0), w=[B("offs")])
            for gi in range(1, nblk):
                a, b_ = order[gi], order[gi - 1]
                S.op(DVE, lambda h, a=a, b_=b_: h.tensor_tensor(out=offs[:, a * 6:a * 6 + 6], in0=offs[:, b_ * 6:b_ * 6 + 6],
                                                                in1=W["tot"][:, b_ * 6:b_ * 6 + 6], op=ALU.add),
                     r=[B("offs"), B("tot")], w=[B("offs")])
            S.op(DVE, lambda h: h.tensor_tensor(out=W["cT"][:, 0:n], in0=W["cw"][:, 0:n], in1=offs[:, 0:n], op=ALU.add),
                 r=[B("cw"), B("offs")], w=[B("cT")])

        def ckB_build(W, ckB, ckb_key, h6, order, nblk, M):
            g = 0
            while g < nblk:
                nb = min(4, nblk - g)
                ci = nxt("cexp", 2)
                cx = W["cexp"][ci]
                for jj in range(nb):
                    slot = order[g + jj]
                    S.op(POOL, lambda h, cx=cx, jj=jj, slot=slot: h.tensor_tensor(
                        out=cx[:, jj * 128:(jj + 1) * 128], in0=identf,
                        in1=W["cT"][:, slot * 6 + h6:slot * 6 + h6 + 1].to_broadcast([128, 128]), op=ALU.mult),
                        r=[B("cT"), B("cstf")], w=[B("cexp", ci)])
                S.group(PE, [lambda h, cx=cx, nb=nb: h.matmul(pM[0:M, 0:nb * 128], lhsT=onesf[:, 0:M], rhs=cx[:, 0:nb * 128],
                                                               start=True, stop=True)],
                        r=[B("cexp", ci), B("onesf")], w=[B("pM")])
                S.op(ACT, lambda h, g=g, nb=nb: h.copy(out=ckB[0:M, g * 128:(g + nb) * 128], in_=pM[0:M, 0:nb * 128]),
                     r=[B("pM")], w=[B(*ckb_key)])
                g += nb

        def attn_work(stack):
            W = {}
            for nm in ["t1", "e"]:
                W[nm] = [T(stack, f"w_{nm}{i}", [128, 512], F32) for i in range(NSLOT)]
            for nm in ["l", "pin"]:
                W[nm] = [T(stack, f"w_{nm}{i}", [128, 514], F32) for i in range(NSLOT)]
            for i in range(NSLOT):
                S.op(POOL, lambda h, i=i: h.memset(W["l"][i][:, 0:1], 0.0), w=[B("l", i)])
            W["a"] = [T(stack, f"w_a{i}", [128, 512], BF16) for i in range(NSLOT)]
            W["aT"] = [T(stack, f"w_aT{i}", [128, 512], BF16) for i in range(NSLOT)]
            W["carry"] = T(stack, "w_carry", [128, NSLOT, 4], F32)
            W["on"] = [T(stack, f"w_on{i}", [128, 64], F32) for i in range(NSLOT)]
            W["sq"] = [T(stack, f"w_sq{i}", [128, 64], F32) for i in range(NSLOT)]
            W["ss"] = T(stack, "w_ss", [128, NSLOT, 4], F32)
            W["cw"] = T(stack, "w_cw", [128, 192], F32)
            W["tot"] = T(stack, "w_tot", [128, 192], F32)
            W["offs"] = T(stack, "w_offs", [128, 192], F32)
            W["cT"] = T(stack, "w_cT", [128, 192], F32)
            W["cexp"] = [T(stack, f"w_cexp{i}", [128, 512], F32) for i in range(2)]
            return W

        for l in range(DEPTH):
            xsrc = xin if l == 0 else xl1.ap()
            xdst = xl1.ap() if l == 0 else y
            with ExitStack() as L1:
                qT = T(L1, "qT", [128, 6, NT * 128], BF16)
                og = T(L1, "og", [128, NT, 256], BF16)
                svnew = T(L1, "svnew", [128, NS, 780], BF16)
                slfnew = T(L1, "slfnew", [128, NS, 6], F32)
                skTn = T(L1, "skTn", [128, 6, 512], BF16)
                S.op(POOL, lambda h: h.memset(svnew[:], 1.0), w=[B("svnew")])
                gather_r = []
                with ExitStack() as A:
                    wtok = T(A, "wtok", [128, 8, WTOK], BF16)
                    wfeat = T(A, "wfeat", [128, 8, WFEAT], BF16)
                    with ExitStack() as A0:
                        stg = [T(A0, f"stgA{i}", [128, WTOK], F32) for i in range(2)]
                        for kc in range(8):
                            load_cast(stg, wtok[:, kc, :], w_tok[l, kc * 128:(kc + 1) * 128, :], WTOK, B("wtok", kc))
                            load_cast(stg, wfeat[:, kc, :], w_feat[l, kc * 128:(kc + 1) * 128, :], WFEAT, B("wfeat", kc))
                        S.barrier()
                    NA = 2
                    xt = [T(A, f"xtA{i}", [128, D], F32) for i in range(2)]
                    xb = [T(A, f"xbA{i}", [128, D], BF16) for i in range(2)]
                    xT = [T(A, f"xTA{i}", [128, 8, 512], BF16) for i in range(3)]
                    kvout = [T(A, f"kvout{i}", [128, 1536], F32) for i in range(NA)]
                    vaug = [T(A, f"vaug{i}", [128, 780], BF16) for i in range(NA)]
                    kst = [T(A, f"kst{i}", [128, 512], BF16) for i in range(3)]
                    gxs = [T(A, f"gx{i}", [128, 512], F32) for i in range(NA)]
                    g2s = [T(A, f"g2{i}", [128, 512], F32) for i in range(NA)]
                    ges = [T(A, f"ge{i}", [128, 512], F32) for i in range(NA)]
                    gls = [T(A, f"gl{i}", [128, 512], F32) for i in range(NA)]
                    vns = [T(A, f"vn{i}", [128, 256], F32) for i in range(NA)]
                    vbs = [T(A, f"vb{i}", [128, 256], BF16) for i in range(NA)]
                    sgbs = [T(A, f"sgb{i}", [128, 256], F32) for i in range(NA)]
                    lfw = T(A, "lfw", [128, NA, 24], F32)
                    sgsts = [T(A, f"sgst{i}", [128, 16], F32) for i in range(NA)]
                    bfB = T(A, "bfB", [128, 6], F32)
                    gvB = T(A, "gvB", [128, 256], F32)
                    bvB = T(A, "bvB", [128, 256], F32)
                    wsf = T(A, "wsf", [128, 4, 128], F32)
                    WsT = T(A, "WsT", [128, 4, 128], BF16)
                    WsTs = T(A, "WsTs", [128, 4, 128], BF16)
                    bsT_t = T(A, "bsT_t", [128, 4], F32)
                    bsB = T(A, "bsB", [128, 256], F32)
                    for i in range(NA):
                        S.op(POOL, lambda h, i=i: h.memset(vaug[i][:], 1.0), w=[B("vaug", i)])
                    S.dma(SP, bfB[:], b_f[l:l + 1, :].partition_broadcast(128), w=[B("cA")])
                    S.dma(SP, gvB[:], g_v[l:l + 1, :].partition_broadcast(128), w=[B("cA")])
                    S.dma(SP, bvB[:], b_v[l:l + 1, :].partition_broadcast(128), w=[B("cA")])
                    S.dma(SP, wsf[:], w_sT[l].rearrange("g j i -> j g i"), w=[B("wsf")])
                    S.dma(SP, bsT_t[:], b_sT[l], w=[B("bsT")])
                    S.op(POOL, lambda h: h.tensor_tensor(out=WsT[:], in0=wsf[:], in1=cstf[:, 1:2, :].to_broadcast([128, 4, 128]),
                                                         op=ALU.mult), r=[B("wsf"), B("cstf")], w=[B("cA")])
                    S.op(POOL, lambda h: h.tensor_tensor(out=WsTs[:], in0=wsf[:], in1=cstf[:, 2:3, :].to_broadcast([128, 4, 128]),
                                                         op=ALU.mult), r=[B("wsf"), B("cstf")], w=[B("cA")])
                    S.op(POOL, lambda h: h.tensor_copy(out=bsB[:].rearrange("p (g c) -> p g c", c=64),
                                                       in_=bsT_t[:].unsqueeze(2).to_broadcast([128, 4, 64])),
                         r=[B("bsT")], w=[B("cA")])
                    WTOK_R = [B("wtok", kc) for kc in range(8)]
                    WFEAT_R = [B("wfeat", kc) for kc in range(8)]

                    def a_prep(t):
                        g, tl = t // 4, t % 4
                        gb = g % 3
                        bi = nxt("xtA", 2)
                        S.dma(SP, xt[bi][:], xsrc[t], w=[B("xtA", bi)])
                        S.op(ACT, lambda h: h.copy(out=xb[bi][:], in_=xt[bi][:]), r=[B("xtA", bi)], w=[B("xbA", bi)])
                        ti = nxt("pt", 2)
                        pt = [pT0, pT1][ti]
                        transposes(pt, B("pT", ti), lambda j: xb[bi][:, j * 128:(j + 1) * 128], 8, r=[B("xbA", bi)])
                        S.op(ACT, lambda h: h.copy(out=xT[gb][:, :, tl * 128:(tl + 1) * 128],
                                                   in_=pt[:].rearrange("p (k q) -> p k q", q=128)),
                             r=[B("pT", ti)], w=[B("xTA", gb, tl)])

                    def a_compute(slot, t):
                        g, tl = t // 4, t % 4
                        gb = g % 3
                        samp = t >= NPB
                        s_i = t - NPB
                        gx, g2, ge, gl = gxs[slot], g2s[slot], ges[slot], gls[slot]
                        vn, vb, sgb, sgst = vns[slot], vbs[slot], sgbs[slot], sgsts[slot]
                        kB = lambda nm: B(nm, "A", slot)
                        kvb = kB("kvout")
                        for ci, (c0, c1) in enumerate([(0, 384), (384, 768), (768, 1152), (1152, 1536), (1536, 2048), (2048, 2054)]):
                            si = nxt("psA", 3)
                            psb = [pA, pB, pC][si]
                            psB = B("pS", si)
                            wd = c1 - c0
                            S.group(PE, [(lambda kc: lambda h: h.matmul(psb[:, 0:wd], lhsT=xT[gb][:, kc, tl * 128:(tl + 1) * 128],
                                                                         rhs=wtok[:, kc, c0:c1], start=kc == 0, stop=kc == 7))(kc)
                                         for kc in range(8)], r=[B("xTA", gb, tl)] + WTOK_R, w=[psB])
                            yield
                            if ci < 4:
                                e = ACT if ci % 2 == 0 else DVE
                                S.op(e, cast_op(e, kvout[slot][:, c0:c1], psb[:, 0:wd]), r=[psB], w=[kvb])
                                yield
                            elif ci == 4:
                                S.op(ACT, lambda h: h.copy(out=gx[:], in_=psb[:, 0:512]), r=[psB], w=[kB("gx")])
                                yield
                                S.op(DVE, lambda h: h.scalar_tensor_tensor(out=g2[:], in0=gx[:], scalar=0.044715, in1=gx[:],
                                                                           op0=ALU.mult, op1=ALU.mult), r=[kB("gx")], w=[kB("g2")])
                                yield
                                S.op(DVE, lambda h: h.scalar_tensor_tensor(out=g2[:], in0=g2[:], scalar=1.0, in1=gx[:],
                                                                           op0=ALU.add, op1=ALU.mult), r=[kB("g2"), kB("gx")], w=[kB("g2")])
                                yield
                                S.op(ACT, lambda h: h.activation(out=ge[:], in_=g2[:], func=AF.Exp, scale=-2.0 * GC), r=[kB("g2")], w=[kB("ge")])
                                yield
                                S.op(DVE, lambda h: h.tensor_scalar(out=ge[:], in0=ge[:], scalar1=1.0, scalar2=None, op0=ALU.add),
                                     r=[kB("ge")], w=[kB("ge")])
                                yield
                                S.op(DVE, lambda h: h.reciprocal(out=ge[:], in_=ge[:]), r=[kB("ge")], w=[kB("ge")])
                                yield
                                S.op(POOL, lambda h: h.tensor_tensor(out=gl[:], in0=gx[:], in1=ge[:], op=ALU.mult),
                                     r=[kB("gx"), kB("ge")], w=[kB("gl")])
                                yield
                                S.op(DVE, lambda h: h.bn_stats(out=sgst[:, 0:6], in_=gl[:, 256:512]), r=[kB("gl")], w=[kB("sgst")])
                                yield
                                S.op(DVE, lambda h: h.bn_aggr(out=sgst[:, 6:8], in_=sgst[:, 0:6]), r=[kB("sgst")], w=[kB("sgst")])
                                yield
                                S.op(ACT, lambda h: h.activation(out=sgst[:, 9:10], in_=sgst[:, 7:8], func=AF.Ln, bias=LN_EPS), r=[kB("sgst")], w=[kB("sgst")])
                                yield
                                S.op(ACT, lambda h: h.activation(out=sgst[:, 8:9], in_=sgst[:, 9:10], func=AF.Exp, scale=-0.5), r=[kB("sgst")], w=[kB("sgst")])
                                yield
                                S.op(DVE, lambda h: h.tensor_scalar(out=vn[:], in0=gl[:, 256:512], scalar1=sgst[:, 6:7],
                                                                    scalar2=sgst[:, 8:9], op0=ALU.subtract, op1=ALU.mult),
                                     r=[kB("gl"), kB("sgst")], w=[kB("vn")])
                                yield
                                S.op(POOL, lambda h: h.tensor_tensor(out=vn[:], in0=vn[:], in1=gvB[:], op=ALU.mult), r=[kB("vn"), B("cA")], w=[kB("vn")])
                                yield
                                S.op(POOL, lambda h: h.tensor_tensor(out=vn[:], in0=vn[:], in1=bvB[:], op=ALU.add), r=[kB("vn"), B("cA")], w=[kB("vn")])
                                yield
                                if samp:
                                    S.dma(POOL, ogv[l, s_i], vn[:], r=[kB("vn")], w=[B("ogv", l, s_i)])
                                S.op(POOL, lambda h: h.tensor_copy(out=vb[:], in_=vn[:]), r=[kB("vn")], w=[kB("vb")])
                                yield
                                Wm = WsTs if samp else WsT
                                S.group(PE, [(lambda gg: lambda h: h.matmul(pM[:, gg * 64:(gg + 1) * 64], lhsT=Wm[:, gg, :],
                                                                             rhs=vb[:, gg * 64:(gg + 1) * 64], start=True, stop=True))(gg)
                                             for gg in range(4)], r=[kB("vb"), B("cA")], w=[B("pM")])
                                S.op(DVE, lambda h: h.tensor_tensor(out=sgb[:], in0=pM[:, 0:256], in1=bsB[:], op=ALU.add),
                                     r=[B("pM"), B("cA")], w=[kB("sgb")])
                                yield
                                S.op(POOL, lambda h: h.tensor_tensor(out=sgb[:], in0=sgb[:], in1=gl[:, 0:256], op=ALU.mult),
                                     r=[kB("sgb"), kB("gl")], w=[kB("sgb")])
                                yield
                                S.op(POOL, lambda h: h.tensor_tensor(out=g2[:, 0:256], in0=sgb[:], in1=sgb[:], op=ALU.mult),
                                     r=[kB("sgb"), kB("g2")], w=[kB("g2")])
                                yield
                                S.op(DVE, lambda h: h.reduce_sum(out=sgst[:, 10:14], in_=g2[:, 0:256].rearrange("p (g c) -> p g c", c=64),
                                                                 axis=mybir.AxisListType.X), r=[kB("g2"), kB("sgst")], w=[kB("sgst")])
                                yield
                                S.op(ACT, lambda h: h.activation(out=sgst[:, 10:14], in_=sgst[:, 10:14], func=AF.Ln, bias=RMS_EPS,
                                                                 scale=1.0 / 64.0), r=[kB("sgst")], w=[kB("sgst")])
                                yield
                                S.op(ACT, lambda h: h.activation(out=sgst[:, 10:14], in_=sgst[:, 10:14], func=AF.Exp, scale=-0.5),
                                     r=[kB("sgst")], w=[kB("sgst")])
                                yield
                                S.op(DVE, lambda h: h.tensor_tensor(out=og[:, t, :].rearrange("p (g c) -> p g c", c=64),
                                                                    in0=sgb[:].rearrange("p (g c) -> p g c", c=64),
                                                                    in1=sgst[:, 10:14].unsqueeze(2).to_broadcast([128, 4, 64]),
                                                                    op=ALU.mult), r=[kB("sgb"), kB("sgst")], w=[B("og")])
                                yield
                            else:
                                li = slot
                                lb_ = kB("lfw")
                                S.op(DVE, lambda h: h.tensor_tensor(out=lfw[:, li, 0:6], in0=psb[:, 0:6], in1=bfB[:], op=ALU.add),
                                     r=[psB, B("cA")], w=[lb_])
                                yield
                                S.op(ACT, lambda h: h.activation(out=lfw[:, li, 6:12], in_=lfw[:, li, 0:6], func=AF.Exp, scale=-1.0), r=[lb_], w=[lb_])
                                yield
                                S.op(ACT, lambda h: h.activation(out=lfw[:, li, 12:18], in_=lfw[:, li, 6:12], func=AF.Ln, bias=1.0), r=[lb_], w=[lb_])
                                yield
                                S.op(DVE, lambda h: h.tensor_scalar(out=lfw[:, li, 18:24], in0=lfw[:, li, 12:18], scalar1=-1.0,
                                                                    scalar2=None, op0=ALU.mult), r=[lb_], w=[lb_])
                                yield
                                S.dma(POOL, olf[l, t], lfw[:, li, 18:24], r=[lb_], w=[B("olf", l, t)])
                                if samp:
                                    S.op(POOL, lambda h: h.tensor_copy(out=slfnew[:, s_i, :], in_=lfw[:, li, 18:24]), r=[lb_], w=[B("slfnew")])
                                else:
                                    S.dma(POOL, lf_in[l][t * 128:(t + 1) * 128, :], lfw[:, li, 18:24], r=[lb_], w=[B("lf_in", l, t)])
                                yield
                        for oi_, (oap, c0) in enumerate([(okf, 0), (ovf, 384), (oks, 768), (ovs, 1152)]):
                            S.dma(POOL, oap[l, t], kvout[slot][:, c0:c0 + 384], r=[kvb], w=[B("okv", l, t, oi_)])
                        yield
                        if samp:
                            for hf, c0 in ((0, 384), (1, 1152)):
                                S.op(POOL, lambda h: h.tensor_copy(
                                    out=svnew[:, s_i, hf * 390:(hf + 1) * 390].rearrange("p (a c) -> p a c", c=65)[:, :, 0:64],
                                    in_=kvout[slot][:, c0:c0 + 384].rearrange("p (a c) -> p a c", c=64)), r=[kvb], w=[B("svnew")])
                                yield
                        else:
                            for hf, c0 in ((0, 384), (1, 1152)):
                                S.op(POOL, lambda h: h.tensor_copy(
                                    out=vaug[slot][:, hf * 390:(hf + 1) * 390].rearrange("p (a c) -> p a c", c=65)[:, :, 0:64],
                                    in_=kvout[slot][:, c0:c0 + 384].rearrange("p (a c) -> p a c", c=64)), r=[kvb], w=[B("vaug", slot)])
                                yield
                            for hf, dst in ((0, vf_in[l]), (1, vs_in[l])):
                                S.dma(POOL, dst[t * 128:(t + 1) * 128, :], vaug[slot][:, hf * 390:(hf + 1) * 390],
                                      r=[B("vaug", slot)], w=[B("v_in", l, t, hf)])
                            yield

                    def a_feat(slot, g):
                        gb = g % 3
                        xr = [B("xTA", gb, tl) for tl in range(4)]
                        for cc in range(12):
                            fi = nxt("poA", 2)
                            pf = [pO0, pO1][fi]
                            pfB = B("pO", fi)
                            S.group(PE, [(lambda kc: lambda h: h.matmul(pf[:, 0:512], lhsT=wfeat[:, kc, cc * 128:(cc + 1) * 128],
                                                                         rhs=xT[gb][:, kc, :], start=kc == 0, stop=kc == 7))(kc)
                                         for kc in range(8)], r=xr + WFEAT_R, w=[pfB])
                            yield
                            e = ACT if cc % 2 == 0 else DVE
                            if cc < 3 or 6 <= cc < 9:
                                pr = cc if cc < 3 else cc - 3
                                S.op(e, cast_op(e, qT[:, pr, g * 512:(g + 1) * 512], pf[:, 0:512]), r=[pfB], w=[B("qT")])
                            else:
                                pr = cc - 3 if cc < 6 else cc - 9
                                fox = cc < 6
                                if g == 4:
                                    S.op(e, cast_op(e, skTn[:, pr + (0 if fox else 3), :], pf[:, 0:512]), r=[pfB], w=[B("skTn")])
                                else:
                                    ksi = nxt("kst", 3)
                                    S.op(e, cast_op(e, kst[ksi][:], pf[:, 0:512]), r=[pfB], w=[B("kst", ksi)])
                                    dst = kTf_in[l] if fox else kTs_in[l]
                                    S.dma(POOL, dst[pr * 128:(pr + 1) * 128, g * 512:(g + 1) * 512], kst[ksi][:],
                                          r=[B("kst", ksi)], w=[B("kT_in", l, g, cc)])
                            yield

                    for tl in range(4):
                        a_prep(tl)
                    tasks = []
                    for g in range(5):
                        for tl in range(4):
                            tasks.append(("task", lambda slot, t=4 * g + tl: a_compute(slot, t)))
                            if g + 1 < 5:
                                tasks.append(("now", lambda t=4 * (g + 1) + tl: a_prep(t)))
                        if FLAGS["featdrain"]:
                            def feat_now(g=g):
                                for _ in a_feat(0, g):
                                    pass
                            tasks.append(("setup", feat_now))
                        else:
                            tasks.append(("task", lambda slot, g=g: a_feat(slot, g)))
                    run_streams(tasks, FLAGS['na'])
                    S.barrier()
                for src, dst, nm in ((kTf_in, kTf_g, "kTf"), (kTs_in, kTs_g, "kTs"), (vf_in, vf_g, "vf"),
                                     (vs_in, vs_g, "vs"), (lf_in, lf_g, "lf")):
                    S.cc(src[l].ap().opt(), dst[l].ap().opt(), r=[], w=[B("g_" + nm, l)])
                S.barrier()

                with ExitStack() as BC:
                    oatt = T(BC, "oatt", [128, NT, 768], BF16)
                    with ExitStack() as Bp:
                        W = attn_work(Bp)
                        KTp = [T(Bp, f"KTp{i}", [128, 32 * 128], BF16) for i in range(2)]
                        Vaug = T(Bp, "Vaug", [128, 32, 390], BF16)
                        ckB = T(Bp, "ckB", [128, 4096], F32)
                        lfT = T(Bp, "lfT", [128, 32, 6], F32)
                        cq = T(Bp, "cq", [128, 96], F32)
                        mAB_f = T(Bp, "mAB_f", [128, 2, 256], F32)
                        mAB_b = T(Bp, "mAB_b", [128, 2, 256], BF16)
                        for kd in range(2):
                            S.op(POOL, lambda h, kd=kd: h.tensor_copy(out=mAB_f[:, kd, :].rearrange("p (a q) -> p a q", q=128),
                                                                      in_=cstf[:, 3 + 2 * kd:5 + 2 * kd, :]), r=[B("cstf")], w=[B("cstf2")])
                        S.op(POOL, lambda h: h.tensor_copy(out=mAB_b[:], in_=mAB_f[:]), r=[B("cstf2")], w=[B("cstb2")])
                        order = [(g % 2) * 16 + g // 2 for g in range(32)]
                        S.dma(SP, lfT[:], lf_g[l].ap().rearrange("(b t) h -> t b h", t=128), r=[B("g_lf", l)], w=[B("lfT")])
                        c_compute(lfT, 32, order, W)
                        cT = W["cT"]
                        S.op(DVE, lambda h: h.tensor_scalar(out=cq[:], in0=cT[:, 0:96], scalar1=selt[:, 0:1], scalar2=None, op0=ALU.mult),
                             r=[B("cT"), B("selt")], w=[B("cq")])
                        S.op(DVE, lambda h: h.scalar_tensor_tensor(out=cq[:], in0=cT[:, 96:192], scalar=selt[:, 1:2], in1=cq[:],
                                                                   op0=ALU.mult, op1=ALU.add), r=[B("cT"), B("selt"), B("cq")], w=[B("cq")])
                        tasks = []
                        kbs = {}
                        for kind, kT_g, v_g in (("fox", kTf_g[l], vf_g[l]), ("sb", kTs_g[l], vs_g[l])):
                            kd = 0 if kind == "fox" else 1

                            def load_v(kind=kind, v_g=v_g):
                                for r_ in range(2):
                                    S.dma(SP, Vaug[:].rearrange("p (k r) c -> p k r c", r=2)[:, :, r_, :],
                                          v_g.ap()[r_ * 2048:(r_ + 1) * 2048, :].rearrange("(k t) c -> t k c", t=128),
                                          r=[B("g_vf" if kind == "fox" else "g_vs", l)], w=[B("V", "p")])
                            tasks.append(("setup", load_v))
                            for hh in range(6):
                                pair, half = hh // 2, hh % 2
                                hp = slice(half * 64, half * 64 + 64)
                                if half == 0:
                                    kb = nxt("KTp", 2)

                                    def load_k(kind=kind, kT_g=kT_g, pair=pair, kb=kb):
                                        for r_ in range(2):
                                            S.dma(SP, KTp[kb][:].rearrange("p (k r t) -> p k r t", r=2, t=128)[:, :, r_, :],
                                                  kT_g.ap()[r_ * 384 + pair * 128:r_ * 384 + (pair + 1) * 128, :].rearrange("p (k t) -> p k t", t=128),
                                                  r=[B("g_kTf" if kind == "fox" else "g_kTs", l)], w=[B("KT", ("p", kb))])
                                    tasks.append(("now", load_k))
                                if kind == "fox":
                                    tasks.append(("setup", lambda hh=hh: ckB_build(W, ckB, ("ckB",), hh, order, 32, 128)))
                                hcol = (0 if kind == "fox" else 384) + hh * 64
                                for k in range(NPB - 1, -1, -1):
                                    tasks.append(("task", lambda slot, kind=kind, hp=hp, pair=pair, kd=kd, k=k, kb=kb, hh=hh, hcol=hcol: attend(
                                        slot, kind, 128, qT[hp, pair + 3 * kd, k * 128:(k + 1) * 128],
                                        lambda lo, hi: KTp[kb][hp, lo * 128:hi * 128],
                                        lambda g_: Vaug[:, g_, hh * 65:(hh + 1) * 65],
                                        2 * k + 2, mAB_f[:, kd, :], mAB_b[:, kd, :], 256,
                                        cq[:, k * 6 + hh:k * 6 + hh + 1], ckB, ("ckB",), W, oatt[:, k, hcol:hcol + 64], (("p", kb), "p"))))
                        run_streams(tasks, FLAGS['nstream'])
                        S.barrier()
                    with ExitStack() as Bs:
                        W = attn_work(Bs)
                        sKT = T(Bs, "sKT", [128, 3, 17 * 128], BF16)
                        sV = T(Bs, "sV", [128, 17, 390], BF16)
                        ckBs = [T(Bs, f"ckBs{i}", [128, 17 * 128], F32) for i in range(2)]
                        lfS = T(Bs, "lfS", [128, 17, 6], F32)
                        cstk = [T(Bs, f"cstk{i}", [128, 2, 384], F32) for i in range(2)]
                        cstv = [T(Bs, f"cstv{i}", [128, 2, 384], F32) for i in range(2)]
                        order = list(range(17))
                        S.op(POOL, lambda h: h.memset(sV[:], 1.0), w=[B("V", "s")])
                        S.op(POOL, lambda h: h.memset(sKT[:], 0.0), w=[B("KT", "s")])
                        for s_i in range(NS):
                            t = NPB + s_i
                            for kind, ck, cv in (("fox", cfk, cfv), ("sb", csk, csv)):
                                kd = 0 if kind == "fox" else 1
                                tasks = []

                                def prep(kind=kind, ck=ck, cv=cv, kd=kd, s_i=s_i):
                                    for b2 in range(8):
                                        ci_ = nxt("cstk", 2)
                                        S.dma(SP, cstk[ci_][:], ck[l, s_i, b2 * 256:(b2 + 1) * 256, :].rearrange("(b t) c -> t b c", t=128),
                                              w=[B("cstk", ci_)])
                                        S.dma(SP, cstv[ci_][:], cv[l, s_i, b2 * 256:(b2 + 1) * 256, :].rearrange("(b t) c -> t b c", t=128),
                                              w=[B("cstv", ci_)])
                                        for bb_ in range(2):
                                            blk = b2 * 2 + bb_
                                            ri_ = nxt("ptr", 2)
                                            pt = [pA, pB][ri_]
                                            ptb_ = B("pS", ri_)
                                            S.group(PE, [(lambda j: lambda h: h.transpose(out=pt[:, j * 128:(j + 1) * 128],
                                                                                          in_=cstk[ci_][:, bb_, j * 128:(j + 1) * 128],
                                                                                          identity=identf))(j) for j in range(3)],
                                                    r=[B("cstk", ci_), B("cstf")], w=[ptb_])
                                            S.op(ACT, lambda h: h.copy(out=sKT[:, :, blk * 128:(blk + 1) * 128],
                                                                       in_=pt[:, 0:384].rearrange("p (a q) -> p a q", q=128)),
                                                 r=[ptb_], w=[B("KT", "s")])
                                        S.op(POOL, lambda h: h.tensor_copy(
                                            out=sV[:, b2 * 2:(b2 + 1) * 2, :].rearrange("p b (a c) -> p b a c", c=65)[:, :, :, 0:64],
                                            in_=cstv[ci_][:].rearrange("p b (a c) -> p b a c", c=64)), r=[B("cstv", ci_)], w=[B("V", "s")])
                                    S.op(POOL, lambda h: h.tensor_copy(out=sKT[:, :, 2048:2048 + 64],
                                                                       in_=skTn[:, 3 * kd:3 * kd + 3, s_i * 128:s_i * 128 + 64]),
                                         r=[B("skTn")], w=[B("KT", "s")])
                                    S.op(POOL, lambda h: h.memset(sV[:, 16, :], 0.0), w=[B("V", "s")])
                                    S.op(POOL, lambda h: h.tensor_copy(out=sV[0:64, 16, :], in_=svnew[0:64, s_i, kd * 390:(kd + 1) * 390]),
                                         r=[B("svnew")], w=[B("V", "s")])
                                    if kind == "fox":
                                        S.op(POOL, lambda h: h.memset(lfS[:, 16, :], 0.0), w=[B("lfT")])
                                        S.dma(SP, lfS[:, 0:16, :], cfl[l, s_i].rearrange("(b t) h -> t b h", t=128), w=[B("lfT")])
                                        S.op(POOL, lambda h: h.tensor_copy(out=lfS[0:64, 16, :], in_=slfnew[0:64, s_i, :]),
                                             r=[B("slfnew")], w=[B("lfT")])
                                        c_compute(lfS, 17, order, W)
                                tasks.append(("setup", prep))
                                for hh in range(6):
                                    pair, half = hh // 2, hh % 2
                                    hp = slice(half * 64, half * 64 + 64)
                                    hcol = (0 if kind == "fox" else 384) + hh * 64

                                    def mk(slot, kind=kind, hp=hp, pair=pair, kd=kd, hh=hh, hcol=hcol, t=t):
                                        cb = ckBs[slot % 2]
                                        key = ("ckBs", slot % 2)
                                        if kind == "fox":
                                            ckB_build(W, cb, key, hh, order, 17, 64)
                                        return attend(slot, kind, 64, qT[hp, pair + 3 * kd, t * 128:t * 128 + 64],
                                                      lambda lo, hi: sKT[hp, pair, lo * 128:hi * 128],
                                                      lambda g_: sV[:, g_, hh * 65:(hh + 1) * 65],
                                                      17, cstf[0:64, 7 + kd, :], cstb[0:64, 7 + kd, :], 128,
                                                      W["cT"][0:64, 16 * 6 + hh:16 * 6 + hh + 1], cb, key, W,
                                                      oatt[0:64, t, hcol:hcol + 64], ("s", "s"))
                                    tasks.append(("task", mk))
                                run_streams(tasks, min(FLAGS['nstream'], 2 if kind == "fox" else NSLOT))
                        S.barrier()
                    with ExitStack() as C1:
                        wout = T(C1, "wout", [128, 8, D], BF16)
                        stg = [T(C1, f"stgC{i}", [128, D], F32) for i in range(2)]
                        gmT = T(C1, "gmT", [128, 8], F32)
                        g1B = T(C1, "g1B", [128, D], F32)
                        b1B = T(C1, "b1B", [128, D], F32)
                        oT = [T(C1, f"oT{i}", [128, 8, 128], BF16) for i in range(3)]
                        xt = [T(C1, f"xtC{i}", [128, D], F32) for i in range(3)]
                        res = [T(C1, f"resC{i}", [128, D], F32) for i in range(2)]
                        xn = [T(C1, f"xnC{i}", [128, D], F32) for i in range(2)]
                        stt = [T(C1, f"stC{i}", [128, 16], F32) for i in range(2)]
                        S.dma(SP, gmT[:], g_mixT[l], w=[B("gmT")])
                        S.dma(SP, g1B[:], ln1_g[l:l + 1, :].partition_broadcast(128), w=[B("cC")])
                        S.dma(SP, b1B[:], ln1_b[l:l + 1, :].partition_broadcast(128), w=[B("cC")])
                        for kc in range(8):
                            load_cast(stg, wout[:, kc, :], w_out[l, kc * 128:(kc + 1) * 128, :], D, B("wout", kc),
                                      scale=gmT[:, kc:kc + 1], engs=[POOL, DVE])
                        WOUT_R = [B("wout", kc) for kc in range(8)] + [B("gmT")]
                        def c1_prep(t):
                            bi = t % 3
                            S.dma(SP, xt[bi][:], xsrc[t], w=[B("xtC", bi)])
                            ti = nxt("pt", 2)
                            pt = [pT0, pT1][ti]

                            def osrc(j, t=t):
                                if j < 3:
                                    return oatt[:, t, j * 128:(j + 1) * 128]
                                if j < 5:
                                    return og[:, t, (j - 3) * 128:(j - 2) * 128]
                                return oatt[:, t, 384 + (j - 5) * 128:384 + (j - 4) * 128]
                            transposes(pt, B("pT", ti), osrc, 8, r=[B("oatt"), B("og")])
                            oi_ = t % 3
                            S.op(ACT, lambda h: h.copy(out=oT[oi_][:], in_=pt[:].rearrange("p (k q) -> p k q", q=128)),
                                 r=[B("pT", ti)], w=[B("oT", oi_)])

                        def c1_banks(t):
                            return [(pA, B("pS", 0)), (pB, B("pS", 1))] if t % 2 == 0 else [(pO0, B("pO", 0)), (pO1, B("pO", 1))]

                        def c1_mix(t):
                            oi_ = t % 3
                            for n_, (pb_, pbB) in enumerate(c1_banks(t)):
                                S.group(PE, [(lambda kc: lambda h: h.matmul(pb_[:, 0:512], lhsT=oT[oi_][:, kc, :],
                                                                             rhs=wout[:, kc, n_ * 512:(n_ + 1) * 512],
                                                                             start=kc == 0, stop=kc == 7))(kc) for kc in range(8)],
                                        r=[B("oT", oi_)] + WOUT_R, w=[pbB])

                        def c1_post(t):
                            bi = t % 3
                            ri = t % 2
                            for n_, (pb_, pbB) in enumerate(c1_banks(t)):
                                S.op(DVE, lambda h: h.scalar_tensor_tensor(
                                    out=res[ri][:, n_ * 512:(n_ + 1) * 512], in0=xt[bi][:, n_ * 512:(n_ + 1) * 512], scalar=ALPHA,
                                    in1=pb_[:, 0:512], op0=ALU.mult, op1=ALU.add), r=[B("xtC", bi), pbB], w=[B("resC", ri)])
                            layer_norm_tile(res[ri][:], xn[ri][:], stt[ri], g1B[:], b1B[:], B("resC", ri), B("xnC", ri), B("stC", ri), B("cC"))
                            S.dma(POOL, xmid.ap()[t], xn[ri][:], r=[B("xnC", ri)], w=[B("xmid", t)])

                        if FLAGS["c1skew"]:
                            c1_prep(0)
                            c1_prep(1)
                            for t in range(NT):
                                c1_mix(t)
                                if t + 2 < NT:
                                    c1_prep(t + 2)
                                c1_post(t)
                        else:
                            for t in range(NT):
                                c1_prep(t)
                                c1_mix(t)
                                c1_post(t)
                        S.barrier()
            with ExitStack() as C2:
                wup = T(C2, "wup", [128, 8, 4 * D], BF16)
                wdn = T(C2, "wdn", [128, 32, D], BF16)
                g2B = T(C2, "g2B", [128, D], F32)
                b2B = T(C2, "b2B", [128, D], F32)
                S.dma(SP, g2B[:], ln2_g[l:l + 1, :].partition_broadcast(128), w=[B("cD")])
                S.dma(SP, b2B[:], ln2_b[l:l + 1, :].partition_broadcast(128), w=[B("cD")])
                with ExitStack() as C2a:
                    stg = [T(C2a, f"stgD{i}", [128, 2048], F32) for i in range(3)]
                    for kc in range(8):
                        for hf in range(2):
                            load_cast(stg, wup[:, kc, hf * 2048:(hf + 1) * 2048], w_up[l, kc * 128:(kc + 1) * 128, hf * 2048:(hf + 1) * 2048],
                                      2048, B("wup", kc))
                    for fc2 in range(16):
                        load_cast(stg, wdn[:, 2 * fc2:2 * fc2 + 2, :].rearrange("p a c -> p (a c)"),
                                  w_down[l, fc2 * 256:(fc2 + 1) * 256, :].rearrange("(a p) c -> p a c", p=128), 2048, B("wdn", fc2),
                                  view=lambda a: a.rearrange("p (a c) -> p a c", a=2))
                    S.barrier()
                with ExitStack() as C2b:
                    xt = [T(C2b, f"xtD{i}", [128, D], F32) for i in range(4)]
                    xb = [T(C2b, f"xbD{i}", [128, D], BF16) for i in range(2)]
                    x1T = [T(C2b, f"x1T{i}", [128, 8, 256], BF16) for i in range(2)]
                    hT = [T(C2b, f"hT{i}", [128, 256], BF16) for i in range(4)]
                    hr = [T(C2b, f"hr{i}", [128, 256], F32) for i in range(3)]
                    res = [T(C2b, f"resD{i}", [128, D], F32) for i in range(2)]
                    xn = [T(C2b, f"xnD{i}", [128, D], F32) for i in range(2)]
                    stt = [T(C2b, f"stD{i}", [128, 16], F32) for i in range(2)]
                    accs = [[(pA, B("pS", 0)), (pB, B("pS", 1))], [(pO0, B("pO", 0)), (pO1, B("pO", 1))]]
                    WUP_R = [B("wup", kc) for kc in range(8)]

                    def c2_prep(gp):
                        xg = gp % 2
                        for tl in range(2):
                            t = 2 * gp + tl
                            bi = (2 * gp + tl) % 4
                            S.dma(SP, xt[bi][:], xmid.ap()[t], r=[B("xmid", t)], w=[B("xtD", bi)])
                            ci_ = nxt("xbD", 2)
                            S.op(POOL, lambda h: h.tensor_copy(out=xb[ci_][:], in_=xt[bi][:]), r=[B("xtD", bi)], w=[B("xbD", ci_)])
                            ti = nxt("pt", 2)
                            pt = [pT0, pT1][ti]
                            transposes(pt, B("pT", ti), lambda j: xb[ci_][:, j * 128:(j + 1) * 128], 8, r=[B("xbD", ci_)])
                            S.op(ACT, lambda h: h.copy(out=x1T[xg][:, :, tl * 128:(tl + 1) * 128],
                                                       in_=pt[:].rearrange("p (k q) -> p k q", q=128)),
                                 r=[B("pT", ti)], w=[B("x1T", xg)])

                    def c2_up(gp, fc):
                        xg = gp % 2
                        ph, phB = [(pC, B("pS", 2)), (pM, B("pM"))][fc % 2]
                        S.group(PE, [(lambda kc: lambda h: h.matmul(ph[:, 0:256], lhsT=wup[:, kc, fc * 128:(fc + 1) * 128],
                                                                     rhs=x1T[xg][:, kc, :], start=kc == 0, stop=kc == 7))(kc)
                                     for kc in range(8)], r=[B("x1T", xg)] + WUP_R, w=[phB])
                        hj = fc % 4
                        rj = fc % 3
                        S.op(ACT, lambda h: h.activation(out=hr[rj][:], in_=ph[:, 0:256], func=AF.Relu), r=[phB], w=[B("hr", rj)])
                        S.op(DVE if fc % 2 == 0 else POOL, lambda h: h.tensor_tensor(out=hT[hj][:], in0=hr[rj][:], in1=hr[rj][:], op=ALU.mult),
                             r=[B("hr", rj)], w=[B("hT", hj)])

                    def c2_down(gp, fc):
                        hj = fc % 4
                        for tl in range(2):
                            for n_ in range(2):
                                pb_, pbB = accs[tl][n_]
                                S.group(PE, [lambda h: h.matmul(pb_[:, 0:512], lhsT=hT[hj][:, tl * 128:(tl + 1) * 128],
                                                                rhs=wdn[:, fc, n_ * 512:(n_ + 1) * 512], start=fc == 0, stop=fc == 31)],
                                        r=[B("hT", hj), B("wdn", fc // 2)], w=[pbB])

                    def c2_post(gp):
                        for tl in range(2):
                            t = 2 * gp + tl
                            bi = (2 * gp + tl) % 4
                            ri = tl
                            for n_ in range(2):
                                pb_, pbB = accs[tl][n_]
                                S.op(DVE, lambda h: h.scalar_tensor_tensor(
                                    out=res[ri][:, n_ * 512:(n_ + 1) * 512], in0=xt[bi][:, n_ * 512:(n_ + 1) * 512], scalar=ALPHA,
                                    in1=pb_[:, 0:512], op0=ALU.mult, op1=ALU.add), r=[B("xtD", bi), pbB], w=[B("resD", ri)])
                            layer_norm_tile(res[ri][:], xn[ri][:], stt[ri], g2B[:], b2B[:], B("resD", ri), B("xnD", ri), B("stD", ri), B("cD"))
                            S.dma(POOL, xdst[t], xn[ri][:], r=[B("xnD", ri)], w=[B("xdst", l, t)])

                    NG = NT // 2
                    if FLAGS["c2skew"]:
                        c2_prep(0)
                        for gp in range(NG):
                            c2_up(gp, 0)
                            c2_up(gp, 1)
                            if gp + 1 < NG:
                                c2_prep(gp + 1)
                            for fc in range(32):
                                if fc + 2 < 32:
                                    c2_up(gp, fc + 2)
                                c2_down(gp, fc)
                            c2_post(gp)
                    else:
                        for gp in range(NG):
                            c2_prep(gp)
                            for fc in range(32):
                                c2_up(gp, fc)
                                c2_down(gp, fc)
                            c2_post(gp)
                    S.barrier()
        S.barrier()
    return nc


_NC = None


def _get_nc():
    global _NC
    if _NC is None:
        _NC = build_nc()
    return _NC


def kernel(x_prompt, x_sample, cache_fox_k, cache_fox_v, cache_fox_logf, cache_sb_k, cache_sb_v,
           w_in, b_f, g_v, b_v, w_s, b_s, g_mix, w_out, ln1_g, ln1_b, w_up, w_down, ln2_g, ln2_b):
    f32 = lambda a: np.ascontiguousarray(np.asarray(a), dtype=np.float32)
    x_prompt, x_sample = f32(x_prompt), f32(x_sample)
    w_in = f32(w_in)
    sp = np.cumsum([384, 384, 384, 6, 256, 256, 384, 384, 384])
    seg = lambda i: slice(0 if i == 0 else sp[i - 1], sp[i])
    qf, kf, vf, fl, ug, vg, qs, ks, vs = [w_in[:, :, seg(i)] for i in range(9)]
    w_tok = np.ascontiguousarray(np.concatenate([kf, vf, ks, vs, ug, vg, fl], axis=2))
    w_feat = np.ascontiguousarray(np.concatenate([qf, kf, qs, ks], axis=2))
    w_sT = np.ascontiguousarray(np.transpose(f32(w_s), (0, 1, 3, 2)))
    b_sT = np.ascontiguousarray(np.transpose(f32(b_s), (0, 2, 1)))
    g_mixT = np.ascontiguousarray(np.transpose(f32(g_mix).reshape(DEPTH, 8, 128), (0, 2, 1)))

    ii = np.arange(128)
    tri_le = (ii[:, None] <= ii[None, :]).astype(np.float32)
    ident = np.eye(128, dtype=np.float32)
    ones = np.ones((128, 128), np.float32)
    zeros = np.zeros((128, 128), np.float32)
    sgs = tri_le * (ii[:, None] < 64) * (ii[None, :] < 64)
    fox_tri = (ii[None, :] <= ii[:, None]).astype(np.float32)
    sb_tri = (ii[None, :] < ii[:, None]).astype(np.float32)

    shared = dict(w_tok=w_tok, w_feat=w_feat, b_f=f32(b_f), g_v=f32(g_v), b_v=f32(b_v), w_sT=w_sT, b_sT=b_sT,
                  g_mixT=g_mixT, w_out=f32(w_out), ln1_g=f32(ln1_g), ln1_b=f32(ln1_b), ln2_g=f32(ln2_g),
                  ln2_b=f32(ln2_b), w_up=f32(w_up), w_down=f32(w_down))
    cfk_a = f32(cache_fox_k).reshape(DEPTH, 32, PAST, 384)
    cfv_a = f32(cache_fox_v).reshape(DEPTH, 32, PAST, 384)
    csk_a = f32(cache_sb_k).reshape(DEPTH, 32, PAST, 384)
    csv_a = f32(cache_sb_v).reshape(DEPTH, 32, PAST, 384)
    cfl_a = f32(cache_fox_logf)
    in_maps = []
    for c in range(8):
        b, j = c // 2, c % 2
        xin = np.zeros((NT, 128, D), np.float32)
        xin[:NPB] = x_prompt[b].reshape(32, 128, D)[j::2]
        xin[NPB:, :64] = x_sample[4 * c:4 * c + 4]
        if j == 0:
            msk = [fox_tri, zeros, sb_tri, zeros]
        else:
            msk = [ones, fox_tri, ones, sb_tri]
        cst = np.stack([ident, tri_le, sgs] + msk + [fox_tri, sb_tri]).astype(np.float32)
        sel = np.zeros((128, 2), np.float32)
        sel[:, j] = 1.0
        m = dict(shared)
        m.update(xin=xin, cfk=np.ascontiguousarray(cfk_a[:, 4 * c:4 * c + 4]), cfv=np.ascontiguousarray(cfv_a[:, 4 * c:4 * c + 4]),
                 csk=np.ascontiguousarray(csk_a[:, 4 * c:4 * c + 4]), csv=np.ascontiguousarray(csv_a[:, 4 * c:4 * c + 4]),
                 cfl=np.ascontiguousarray(cfl_a[:, 4 * c:4 * c + 4]), cst=cst, sel=sel)
        in_maps.append(m)

    res = run_bass_kernel_spmd(_get_nc(), in_maps, core_ids=list(range(8)))
    R = res.results

    y_p = np.zeros((4, 32, 128, D), np.float32)
    y_s = np.zeros((32, 64, D), np.float32)
    pk = {n: np.zeros((DEPTH, 4, 32, 128, 384), np.float32) for n in ("okf", "ovf", "oks", "ovs")}
    pl = np.zeros((DEPTH, 4, 32, 128, 6), np.float32)
    sk = {n: np.zeros((DEPTH, 32, 64, 384), np.float32) for n in ("okf", "ovf", "oks", "ovs")}
    sl = np.zeros((DEPTH, 32, 64, 6), np.float32)
    sg = np.zeros((DEPTH, 32, 64, 256), np.float32)
    for c in range(8):
        b, j = c // 2, c % 2
        r = R[c]
        y_p[b, j::2] = r["y"][:NPB]
        y_s[4 * c:4 * c + 4] = r["y"][NPB:, :64]
        for n in pk:
            pk[n][:, b, j::2] = r[n][:, :NPB]
            sk[n][:, 4 * c:4 * c + 4] = r[n][:, NPB:, :64]
        pl[:, b, j::2] = r["olf"][:, :NPB]
        sl[:, 4 * c:4 * c + 4] = r["olf"][:, NPB:, :64]
        sg[:, 4 * c:4 * c + 4] = r["ogv"][:, :, :64]
    P5 = lambda a: a.reshape(DEPTH, 4, 4096, 6, 64)
    S5 = lambda a: a.reshape(DEPTH, 32, 64, 6, 64)
    return (y_p.reshape(4, 4096, D), y_s,
            P5(pk["okf"]), P5(pk["ovf"]), pl.reshape(DEPTH, 4, 4096, 6), P5(pk["oks"]), P5(pk["ovs"]),
            S5(sk["okf"]), S5(sk["ovf"]), sl, S5(sk["oks"]), S5(sk["ovs"]), sg)
```

```python
import math
from contextlib import ExitStack

import numpy as np
import concourse.bass as bass
import concourse.mybir as mybir
from concourse.bass_utils import run_bass_kernel_spmd

F32 = mybir.dt.float32
BF16 = mybir.dt.bfloat16
AF = mybir.ActivationFunctionType
ALU = mybir.AluOpType

DEPTH = 2
D = 1024
NT = 20
NPB = 16
NS = 4
PAST = 2048
ALPHA = (2 * DEPTH) ** 0.25
LN_EPS = 1e-5
RMS_EPS = 1e-6
GC = math.sqrt(2.0 / math.pi)
WTOK = 2054
WFEAT = 1536
GROUPS = [[0, 1], [2, 3], [4, 5], [6, 7]]
FLAGS = {"c1skew": True, "c2skew": True, "nstream": 4, "na": 2, "featdrain": 0}


class Buf:
    __slots__ = ("w", "r")

    def __init__(self):
        self.w = None
        self.r = {}


class Eng:
    def __init__(self, name, h):
        self.name = name
        self.h = h
        self.sid = None
        self.n = 0
        self.epoch = 0
        self.seen = {}


class Sched:
    NDMA = 24
    LIMIT = 12000

    def __init__(self, nc, stack):
        self.nc = nc
        self.stack = stack
        self.sems = {}
        self.bufs = {}
        self.pe = Eng("pe", nc.tensor)
        self.act = Eng("act", nc.scalar)
        self.dve = Eng("dve", nc.vector)
        self.pool = Eng("pool", nc.gpsimd)
        self.sp = Eng("sp", nc.sync)
        self.engs = [self.pe, self.act, self.dve, self.pool, self.sp]
        self.last = {}
        for e in self.engs:
            self._new_epoch(e)
        self.dsem = [self._mk(f"d{i}") for i in range(self.NDMA)]
        self.duse = [0] * self.NDMA
        self.dnext = 0
        self.cc_n = 0
        self.cc_toks = []

    def _mk(self, name):
        s = self.stack.enter_context(self.nc.semaphore("s_" + name))
        self.sems[name] = s
        return s

    def _new_epoch(self, e):
        if e.sid is not None:
            self.last[e.sid] = e.n
        e.sid = f"{e.name}{e.epoch}"
        e.epoch += 1
        e.n = 0
        self._mk(e.sid)

    def B(self, *key):
        b = self.bufs.get(key)
        if b is None:
            b = Buf()
            self.bufs[key] = b
        return b

    def _wait(self, eng, tok):
        if tok is None:
            return
        sid, val = tok
        if eng.seen.get(sid, 0) >= val:
            return
        eng.h.wait_ge(self.sems[sid], val)
        eng.seen[sid] = val

    def _own(self, eng, sid):
        return sid.startswith(eng.name) and sid[len(eng.name):].isdigit()

    def _pre(self, eng, r, w):
        for b in r:
            self._wait(eng, b.w)
        for b in w:
            if b.w is not None and not self._own(eng, b.w[0]):
                self._wait(eng, b.w)
            for t in b.r.items():
                if not self._own(eng, t[0]):
                    self._wait(eng, t)

    def _post(self, tok, r, w):
        for b in r:
            if b.r.get(tok[0], 0) < tok[1]:
                b.r[tok[0]] = tok[1]
        for b in w:
            b.w = tok
            b.r = {}

    def _tick(self, eng, ins):
        if eng.n >= self.LIMIT:
            self._new_epoch(eng)
        eng.n += 1
        ins.then_inc(self.sems[eng.sid], 1)
        return (eng.sid, eng.n)

    def op(self, eng, fn, r=(), w=()):
        self._pre(eng, r, w)
        tok = self._tick(eng, fn(eng.h))
        self._post(tok, r, w)

    def group(self, eng, fns, r=(), w=()):
        self._pre(eng, r, w)
        ins = None
        for fn in fns:
            ins = fn(eng.h)
        tok = self._tick(eng, ins)
        self._post(tok, r, w)

    def dma(self, q, out, in_, r=(), w=()):
        i = self.dnext
        self.dnext = (self.dnext + 1) % self.NDMA
        sid = f"d{i}"
        if self.duse[i] > 0:
            self._wait(q, (sid, 16 * self.duse[i]))
        self._pre(q, r, w)
        q.h.dma_start(out=out, in_=in_).then_inc(self.dsem[i], 16)
        self.duse[i] += 1
        self._post((sid, 16 * self.duse[i]), r, w)

    def cc(self, ins, outs, r=(), w=()):
        q = self.pool
        self._pre(q, r, w)
        self.cc_n += 1
        name = f"cc{self.cc_n}"
        sem = self._mk(name)
        q.h.collective_compute("AllGather", ALU.bypass, replica_groups=GROUPS,
                               ins=[ins], outs=[outs]).then_inc(sem)
        tok = (name, 1)
        self._post(tok, r, w)
        self.cc_toks.append(tok)
        self._wait(q, tok)

    def barrier(self):
        toks = [(e.sid, e.n) for e in self.engs if e.n > 0]
        toks += list(self.last.items())
        toks += [(f"d{i}", 16 * self.duse[i]) for i in range(self.NDMA) if self.duse[i] > 0]
        toks += self.cc_toks
        for e in self.engs:
            for t in toks:
                if t[1] > 0 and not self._own(e, t[0]):
                    self._wait(e, t)
        for b in self.bufs.values():
            b.w = None
            b.r = {}


def build_nc():
    nc = bass.Bass("TRN2", target_bir_lowering=False)

    def din(name, shape, dt=F32):
        return nc.dram_tensor(name, list(shape), dt, kind="ExternalInput").ap()

    def dout(name, shape):
        return nc.dram_tensor(name, list(shape), F32, kind="ExternalOutput").ap()

    def dint(name, shape, dt):
        return nc.dram_tensor(name, list(shape), dt)

    xin = din("xin", [NT, 128, D])
    cfk = din("cfk", [DEPTH, NS, PAST, 384])
    cfv = din("cfv", [DEPTH, NS, PAST, 384])
    csk = din("csk", [DEPTH, NS, PAST, 384])
    csv = din("csv", [DEPTH, NS, PAST, 384])
    cfl = din("cfl", [DEPTH, NS, PAST, 6])
    w_tok = din("w_tok", [DEPTH, D, WTOK])
    w_feat = din("w_feat", [DEPTH, D, WFEAT])
    b_f = din("b_f", [DEPTH, 6])
    g_v = din("g_v", [DEPTH, 256])
    b_v = din("b_v", [DEPTH, 256])
    w_sT = din("w_sT", [DEPTH, 4, 128, 128])
    b_sT = din("b_sT", [DEPTH, 128, 4])
    g_mixT = din("g_mixT", [DEPTH, 128, 8])
    w_out = din("w_out", [DEPTH, D, D])
    ln1_g = din("ln1_g", [DEPTH, D])
    ln1_b = din("ln1_b", [DEPTH, D])
    ln2_g = din("ln2_g", [DEPTH, D])
    ln2_b = din("ln2_b", [DEPTH, D])
    w_up = din("w_up", [DEPTH, D, 4 * D])
    w_down = din("w_down", [DEPTH, 4 * D, D])
    cst = din("cst", [9, 128, 128])
    sel = din("sel", [128, 2])

    y = dout("y", [NT, 128, D])
    okf = dout("okf", [DEPTH, NT, 128, 384])
    ovf = dout("ovf", [DEPTH, NT, 128, 384])
    oks = dout("oks", [DEPTH, NT, 128, 384])
    ovs = dout("ovs", [DEPTH, NT, 128, 384])
    olf = dout("olf", [DEPTH, NT, 128, 6])
    ogv = dout("ogv", [DEPTH, NS, 128, 256])

    xmid = dint("xmid", [NT, 128, D], F32)
    xl1 = dint("xl1", [NT, 128, D], F32)
    kTf_in = [dint(f"kTf_in{l}", [384, 2048], BF16) for l in range(DEPTH)]
    kTs_in = [dint(f"kTs_in{l}", [384, 2048], BF16) for l in range(DEPTH)]
    vf_in = [dint(f"vf_in{l}", [2048, 390], BF16) for l in range(DEPTH)]
    vs_in = [dint(f"vs_in{l}", [2048, 390], BF16) for l in range(DEPTH)]
    lf_in = [dint(f"lf_in{l}", [2048, 6], F32) for l in range(DEPTH)]
    kTf_g = [dint(f"kTf_g{l}", [768, 2048], BF16) for l in range(DEPTH)]
    kTs_g = [dint(f"kTs_g{l}", [768, 2048], BF16) for l in range(DEPTH)]
    vf_g = [dint(f"vf_g{l}", [4096, 390], BF16) for l in range(DEPTH)]
    vs_g = [dint(f"vs_g{l}", [4096, 390], BF16) for l in range(DEPTH)]
    lf_g = [dint(f"lf_g{l}", [4096, 6], F32) for l in range(DEPTH)]

    with ExitStack() as top:
        S = Sched(nc, top)
        B = S.B
        PE, ACT, DVE, POOL, SP = S.pe, S.act, S.dve, S.pool, S.sp

        uniq = [0]

        def T(stack, name, shape, dt):
            uniq[0] += 1
            return stack.enter_context(nc.sbuf_tensor(f"{name}_{uniq[0]}", list(shape), dt))

        def PS(name, shape, dt):
            return top.enter_context(nc.psum_tensor(name, list(shape), dt))

        pk = [PS(f"pk{i}", [128, 512], F32) for i in range(8)]
        pA, pB, pC, pM, pO0, pO1, pT0f, pT1f = pk
        pT0 = pT0f[:].bitcast(BF16)
        pT1 = pT1f[:].bitcast(BF16)

        cstf = T(top, "cstf", [128, 9, 128], F32)
        cstb = T(top, "cstb", [128, 9, 128], BF16)
        ones512 = T(top, "ones512", [128, 514], F32)
        onesf = T(top, "onesf", [128, 128], F32)
        selt = T(top, "selt", [128, 2], F32)
        S.dma(SP, cstf[:], cst.rearrange("c p q -> p c q"), w=[B("cstf")])
        S.dma(SP, selt[:], sel[:, :], w=[B("selt")])
        S.op(POOL, lambda h: h.tensor_copy(out=cstb[:], in_=cstf[:]), r=[B("cstf")], w=[B("cstb")])
        S.op(DVE, lambda h: h.memset(ones512[:], 1.0), w=[B("ones512")])
        S.op(DVE, lambda h: h.memset(onesf[:], 1.0), w=[B("onesf")])
        identf = cstf[:, 0, :]
        identb = cstb[:, 0, :]
        Uf = cstf[:, 1, :]
        CONST_R = [B("cstf"), B("cstb"), B("ones512"), B("onesf"), B("selt")]

        rot = {}

        def nxt(key, n):
            v = rot.get(key, 0) % n
            rot[key] = (v + 1) % n
            return v

        cast_engs = [POOL, DVE, ACT]

        def cast_op(eng, out, in_, scale=None):
            if scale is not None:
                if eng is ACT:
                    return lambda h: h.activation(out=out, in_=in_, func=AF.Copy, scale=scale)
                return lambda h: h.tensor_scalar(out=out, in0=in_, scalar1=scale, scalar2=None, op0=ALU.mult)
            if eng is ACT:
                return lambda h: h.copy(out=out, in_=in_)
            return lambda h: h.tensor_copy(out=out, in_=in_)

        def load_cast(stg, dst, src, ncols, wb, scale=None, engs=None, view=None):
            i = nxt("stg", len(stg))
            sv = stg[i][:, 0:ncols]
            S.dma(SP, view(sv) if view else sv, src, w=[B("stg", i)])
            engs = engs or cast_engs
            e = engs[nxt("casteng", len(engs))]
            S.op(e, cast_op(e, dst, stg[i][:, 0:ncols], scale), r=[B("stg", i)] + CONST_R, w=[wb])

        def transposes(pt, pbuf, src_fn, nblk, rows=128, r=()):
            fns = []
            for j in range(nblk):
                fns.append((lambda j: lambda h: h.transpose(out=pt[:, j * 128:j * 128 + rows], in_=src_fn(j),
                                                            identity=identb[0:rows, 0:rows]))(j))
            S.group(PE, fns, r=list(r) + [B("cstb")], w=[pbuf])

        def rstd_from(var_ap, out_ap, tmp_ap, scale, eps, bufs_r, buf_w):
            S.op(ACT, lambda h: h.activation(out=tmp_ap, in_=var_ap, func=AF.Ln, bias=eps, scale=scale),
                 r=bufs_r, w=[buf_w])
            S.op(ACT, lambda h: h.activation(out=out_ap, in_=tmp_ap, func=AF.Exp, scale=-0.5),
                 r=[buf_w], w=[buf_w])

        def layer_norm_tile(res, outt, stats, gB, bB, rb, ob, stb, constb):
            S.op(DVE, lambda h: h.bn_stats(out=stats[:, 0:6], in_=res[:, 0:512]), r=[rb], w=[stb])
            S.op(DVE, lambda h: h.bn_stats(out=stats[:, 6:12], in_=res[:, 512:1024]), r=[rb], w=[stb])
            S.op(DVE, lambda h: h.bn_aggr(out=stats[:, 12:14], in_=stats[:, 0:12]), r=[stb], w=[stb])
            rstd_from(stats[:, 13:14], stats[:, 14:15], stats[:, 15:16], 1.0, LN_EPS, [stb], stb)
            S.op(DVE, lambda h: h.scalar_tensor_tensor(out=stats[:, 15:16], in0=stats[:, 12:13], scalar=-1.0, in1=stats[:, 14:15],
                                                       op0=ALU.mult, op1=ALU.mult), r=[stb], w=[stb])
            S.op(ACT, lambda h: h.activation(out=outt, in_=res, func=AF.Identity, scale=stats[:, 14:15], bias=stats[:, 15:16]),
                 r=[rb, stb], w=[ob])
            S.op(DVE, lambda h: h.tensor_tensor(out=outt, in0=outt, in1=gB, op=ALU.mult), r=[ob, constb], w=[ob])
            S.op(POOL, lambda h: h.tensor_tensor(out=outt, in0=outt, in1=bB, op=ALU.add), r=[ob, constb], w=[ob])

        NSLOT = 4
        SBANK = [(pA, ("pS", 0)), (pB, ("pS", 1)), (pC, ("pS", 2)), (pM, ("pM",))]
        POBANK = [(pO0, ("pO", 0)), (pO1, ("pO", 1)), (pT0f, ("pT", 0)), (pT1f, ("pT", 1))]

        def attend(slot, kind, M, qT_ap, KT_fn, V_fn, nkb, mask_f, mask_b, maskw, cq_ap, ckB, ckb_key, W, out_ap, uid):
            ps_ = slice(0, M)
            chunks = []
            hi = nkb
            first = True
            while hi > 0:
                if first and maskw == 128:
                    lo = hi - 1
                else:
                    lo = max(0, hi - 4)
                chunks.append((lo, hi))
                hi = lo
                first = False
            po = POBANK[slot][0][:, 0:65]
            pob = B(*POBANK[slot][1])
            psb, pskey = SBANK[slot]
            psB = B(*pskey)
            pt = psb[:].bitcast(BF16)[:, 0:512]
            ptB = psB
            t1 = W["t1"][slot]
            e = W["e"][slot]
            lb = W["l"][slot]
            pin = W["pin"][slot]
            a = W["a"][slot]
            aT = W["aT"][slot]
            car = W["carry"]
            kB = lambda nm: B(nm, slot)
            if kind == "sb":
                S.op(POOL, lambda h: h.memset(car[:, slot, 0:1], 0.0), w=[B("carry", slot, 0)])
                yield
                cprev = 0
            nmm = 0
            for ci, (lo, hi) in enumerate(chunks):
                nb = hi - lo
                w = nb * 128
                S.group(PE, [lambda h: h.matmul(psb[ps_, 0:w], lhsT=qT_ap, rhs=KT_fn(lo, hi), start=True, stop=True)],
                        r=[B("qT"), B("KT", uid[0])], w=[psB])
                yield
                top_chunk = ci == 0
                if kind == "fox":
                    S.op(DVE, lambda h: h.scalar_tensor_tensor(
                        out=t1[ps_, 0:w], in0=psb[ps_, 0:w], scalar=0.125, in1=ckB[ps_, lo * 128:hi * 128],
                        op0=ALU.mult, op1=ALU.subtract), r=[psB, B(*ckb_key)], w=[kB("t1")])
                    yield
                    S.op(ACT, lambda h: h.activation(out=a[ps_, 0:w], in_=t1[ps_, 0:w], func=AF.Exp, bias=cq_ap),
                         r=[kB("t1"), B("cq")], w=[kB("a")])
                    yield
                else:
                    S.op(ACT, lambda h: h.activation(out=e[ps_, 0:w], in_=psb[ps_, 0:w], func=AF.Exp, scale=0.125),
                         r=[psB], w=[kB("t1")])
                    yield
                    S.op(ACT, lambda h: h.activation(out=lb[ps_, 1:w + 1], in_=e[ps_, 0:w], func=AF.Ln, bias=1.0),
                         r=[kB("t1")], w=[kB("l")])
                    yield
                    if top_chunk:
                        S.op(POOL, lambda h: h.tensor_tensor(out=lb[ps_, 1 + w - maskw:1 + w], in0=lb[ps_, 1 + w - maskw:1 + w],
                                                             in1=mask_f, op=ALU.mult), r=[kB("l"), B("cstf"), B("cstf2")], w=[kB("l")])
                        yield
                    S.op(DVE, lambda h: h.tensor_tensor_scan(
                        out=pin[ps_, 0:w + 1], data0=ones512[ps_, 0:w + 1], data1=lb[ps_, 0:w + 1], initial=0.0,
                        op0=ALU.mult, op1=ALU.add), r=[kB("l"), B("ones512")], w=[kB("pin")])
                    yield
                    cnew = 1 - cprev
                    S.op(POOL, lambda h: h.tensor_tensor(out=car[ps_, slot, cnew:cnew + 1], in0=car[ps_, slot, cprev:cprev + 1],
                                                         in1=pin[ps_, w:w + 1], op=ALU.subtract),
                         r=[B("carry", slot, cprev), kB("pin")], w=[B("carry", slot, cnew)])
                    yield
                    S.op(DVE, lambda h: h.scalar_tensor_tensor(
                        out=t1[ps_, 0:w], in0=psb[ps_, 0:w], scalar=0.125, in1=pin[ps_, 0:w],
                        op0=ALU.mult, op1=ALU.add), r=[psB, kB("pin")], w=[kB("t1")])
                    yield
                    S.op(ACT, lambda h: h.activation(out=a[ps_, 0:w], in_=t1[ps_, 0:w], func=AF.Exp, bias=car[ps_, slot, cnew:cnew + 1]),
                         r=[kB("t1"), B("carry", slot, cnew)], w=[kB("a")])
                    yield
                    cprev = cnew
                if top_chunk:
                    S.op(POOL, lambda h: h.tensor_tensor(out=a[ps_, w - maskw:w], in0=a[ps_, w - maskw:w], in1=mask_b, op=ALU.mult),
                         r=[kB("a"), B("cstb"), B("cstb2")], w=[kB("a")])
                    yield
                transposes(pt, ptB, lambda j: a[ps_, j * 128:(j + 1) * 128], nb, rows=M, r=[kB("a")])
                yield
                if M == 128:
                    S.op(ACT, lambda h: h.copy(out=aT[:, 0:w], in_=pt[:, 0:w]), r=[ptB], w=[kB("aT")])
                else:
                    S.op(ACT, lambda h: h.copy(
                        out=aT[:, 0:nb * 128].rearrange("p (b q) -> p b q", q=128)[:, :, 0:M],
                        in_=pt[:, 0:nb * 128].rearrange("p (b q) -> p b q", q=128)[:, :, 0:M]), r=[ptB], w=[kB("aT")])
                yield
                fns = []
                ncol = 65 if kind == "fox" else 64
                for j in range(nb):
                    st = nmm == 0
                    sp_ = nmm == nkb - 1
                    fns.append((lambda j, st, sp_: lambda h: h.matmul(
                        po[ps_, 0:ncol], lhsT=aT[:, j * 128:j * 128 + M], rhs=V_fn(lo + j)[:, 0:ncol],
                        start=st, stop=sp_))(j, st, sp_))
                    nmm += 1
                S.group(PE, fns, r=[kB("aT"), B("V", uid[1])], w=[pob])
                yield
            on = W["on"][slot]
            sq = W["sq"][slot]
            ss = W["ss"]
            eb = kB("ep")
            if kind == "fox":
                S.op(DVE, lambda h: h.reciprocal(out=ss[ps_, slot, 0:1], in_=po[ps_, 64:65]), r=[pob], w=[eb])
                yield
                S.op(DVE, lambda h: h.tensor_scalar(out=on[ps_, :], in0=po[ps_, 0:64], scalar1=ss[ps_, slot, 0:1], scalar2=None,
                                                    op0=ALU.mult), r=[pob, eb], w=[eb])
            else:
                S.op(DVE, lambda h: h.tensor_copy(out=on[ps_, :], in_=po[ps_, 0:64]), r=[pob], w=[eb])
            yield
            S.op(POOL, lambda h: h.memset(ss[ps_, slot, 1:2], 0.0), w=[eb])
            yield
            S.op(ACT, lambda h: h.activation(out=sq[ps_, :], in_=on[ps_, :], func=AF.Square, accum_out=ss[ps_, slot, 1:2]),
                 r=[eb], w=[eb])
            yield
            S.op(ACT, lambda h: h.activation(out=ss[ps_, slot, 2:3], in_=ss[ps_, slot, 1:2], func=AF.Ln, bias=RMS_EPS, scale=1.0 / 64.0),
                 r=[eb], w=[eb])
            yield
            S.op(ACT, lambda h: h.activation(out=ss[ps_, slot, 3:4], in_=ss[ps_, slot, 2:3], func=AF.Exp, scale=-0.5), r=[eb], w=[eb])
            yield
            S.op(DVE, lambda h: h.tensor_scalar(out=out_ap, in0=on[ps_, :], scalar1=ss[ps_, slot, 3:4], scalar2=None,
                                                op0=ALU.mult), r=[eb], w=[B("oatt")])
            yield

        def run_streams(tasks, n):
            active = []
            free = list(range(n))
            i = 0
            while i < len(tasks) or active:
                while i < len(tasks) and (free or tasks[i][0] != "task"):
                    kind_, f = tasks[i]
                    if kind_ == "now":
                        f()
                    elif kind_ == "setup":
                        if active:
                            break
                        f()
                    else:
                        slot = free.pop(0)
                        active.append((slot, f(slot)))
                    i += 1
                for item in list(active):
                    try:
                        next(item[1])
                    except StopIteration:
                        active.remove(item)
                        free.append(item[0])
                        free.sort()

        def c_compute(lfT, nblk, order, W, M=128):
            n = nblk * 6
            lf2 = lfT[:].rearrange("p b h -> p (b h)")
            S.group(PE, [lambda h: h.matmul(pM[:, 0:n], lhsT=Uf, rhs=lf2, start=True, stop=True)],
                    r=[B("lfT"), B("cstf")], w=[B("pM")])
            S.op(ACT, lambda h: h.copy(out=W["cw"][:, 0:n], in_=pM[:, 0:n]), r=[B("pM")], w=[B("cw")])
            S.group(PE, [lambda h: h.matmul(pM[:, 0:n], lhsT=onesf[:], rhs=lf2, start=True, stop=True)],
                    r=[B("lfT"), B("onesf")], w=[B("pM")])
            S.op(ACT, lambda h: h.copy(out=W["tot"][:, 0:n], in_=pM[:, 0:n]), r=[B("pM")], w=[B("tot")])
            offs = W["offs"]
            S.op(DVE, lambda h: h.memset(offs[:, order[0] * 6:order[0] * 6 + 6], 0.0), w=[B("offs")])
            for gi in range(1, nblk):
                a, b_ = order[gi], order[gi - 1]
                S.op(DVE, lambda h, a=a, b_=b_: h.tensor_tensor(out=offs[:, a * 6:a * 6 + 6], in0=offs[:, b_ * 6:b_ * 6 + 6],
                                                                in1=W["tot"][:, b_ * 6:b_ * 6 + 6], op=ALU.add),
                     r=[B("offs"), B("tot")], w=[B("offs")])
            S.op(DVE, lambda h: h.tensor_tensor(out=W["cT"][:, 0:n], in0=W["cw"][:, 0:n], in1=offs[:, 0:n], op=ALU.add),
                 r=[B("cw"), B("offs")], w=[B("cT")])

        def ckB_build(W, ckB, ckb_key, h6, order, nblk, M):
            g = 0
            while g < nblk:
                nb = min(4, nblk - g)
                ci = nxt("cexp", 2)
                cx = W["cexp"][ci]
                for jj in range(nb):
                    slot = order[g + jj]
                    S.op(POOL, lambda h, cx=cx, jj=jj, slot=slot: h.tensor_tensor(
                        out=cx[:, jj * 128:(jj + 1) * 128], in0=identf,
                        in1=W["cT"][:, slot * 6 + h6:slot * 6 + h6 + 1].to_broadcast([128, 128]), op=ALU.mult),
                        r=[B("cT"), B("cstf")], w=[B("cexp", ci)])
                S.group(PE, [lambda h, cx=cx, nb=nb: h.matmul(pM[0:M, 0:nb * 128], lhsT=onesf[:, 0:M], rhs=cx[:, 0:nb * 128],
                                                               start=True, stop=True)],
                        r=[B("cexp", ci), B("onesf")], w=[B("pM")])
                S.op(ACT, lambda h, g=g, nb=nb: h.copy(out=ckB[0:M, g * 128:(g + nb) * 128], in_=pM[0:M, 0:nb * 128]),
                     r=[B("pM")], w=[B(*ckb_key)])
                g += nb

        def attn_work(stack):
            W = {}
            W["t1"] = [T(stack, f"w_t1{i}", [128, 512], F32) for i in range(NSLOT)]
            W["e"] = W["t1"]
            for nm in ["l", "pin"]:
                W[nm] = [T(stack, f"w_{nm}{i}", [128, 514], F32) for i in range(NSLOT)]
            for i in range(NSLOT):
                S.op(POOL, lambda h, i=i: h.memset(W["l"][i][:, 0:1], 0.0), w=[B("l", i)])
            W["a"] = [T(stack, f"w_a{i}", [128, 512], BF16) for i in range(NSLOT)]
            W["aT"] = [T(stack, f"w_aT{i}", [128, 512], BF16) for i in range(NSLOT)]
            W["carry"] = T(stack, "w_carry", [128, NSLOT, 4], F32)
            W["on"] = [T(stack, f"w_on{i}", [128, 64], F32) for i in range(NSLOT)]
            W["sq"] = [T(stack, f"w_sq{i}", [128, 64], F32) for i in range(NSLOT)]
            W["ss"] = T(stack, "w_ss", [128, NSLOT, 4], F32)
            W["cw"] = T(stack, "w_cw", [128, 192], F32)
            W["tot"] = T(stack, "w_tot", [128, 192], F32)
            W["offs"] = T(stack, "w_offs", [128, 192], F32)
            W["cT"] = T(stack, "w_cT", [128, 192], F32)
            W["cexp"] = [T(stack, f"w_cexp{i}", [128, 512], F32) for i in range(2)]
            return W

        for l in range(DEPTH):
            xsrc = xin if l == 0 else xl1.ap()
            xdst = xl1.ap() if l == 0 else y
            with ExitStack() as L1:
                qT = T(L1, "qT", [128, 6, NT * 128], BF16)
                og = T(L1, "og", [128, NT, 256], BF16)
                svnew = T(L1, "svnew", [128, NS, 780], BF16)
                slfnew = T(L1, "slfnew", [128, NS, 6], F32)
                skTn = T(L1, "skTn", [128, 6, 512], BF16)
                S.op(POOL, lambda h: h.memset(svnew[:], 1.0), w=[B("svnew")])
                gather_r = []
                with ExitStack() as A:
                    wtok = T(A, "wtok", [128, 8, WTOK], BF16)
                    wfeat = T(A, "wfeat", [128, 8, WFEAT], BF16)
                    with ExitStack() as A0:
                        stg = [T(A0, f"stgA{i}", [128, WTOK], F32) for i in range(2)]
                        for kc in range(8):
                            load_cast(stg, wtok[:, kc, :], w_tok[l, kc * 128:(kc + 1) * 128, :], WTOK, B("wtok", kc))
                            load_cast(stg, wfeat[:, kc, :], w_feat[l, kc * 128:(kc + 1) * 128, :], WFEAT, B("wfeat", kc))
                        S.barrier()
                    NA = 2
                    xt = [T(A, f"xtA{i}", [128, D], F32) for i in range(2)]
                    xb = [T(A, f"xbA{i}", [128, D], BF16) for i in range(2)]
                    xT = [T(A, f"xTA{i}", [128, 8, 512], BF16) for i in range(3)]
                    kvout = [T(A, f"kvout{i}", [128, 1536], F32) for i in range(NA)]
                    vaug = [T(A, f"vaug{i}", [128, 780], BF16) for i in range(NA)]
                    kst = [T(A, f"kst{i}", [128, 512], BF16) for i in range(3)]
                    gxs = [T(A, f"gx{i}", [128, 512], F32) for i in range(NA)]
                    g2s = [T(A, f"g2{i}", [128, 512], F32) for i in range(NA)]
                    ges = [T(A, f"ge{i}", [128, 512], F32) for i in range(NA)]
                    gls = [T(A, f"gl{i}", [128, 512], F32) for i in range(NA)]
                    vns = [T(A, f"vn{i}", [128, 256], F32) for i in range(NA)]
                    vbs = [T(A, f"vb{i}", [128, 256], BF16) for i in range(NA)]
                    sgbs = [T(A, f"sgb{i}", [128, 256], F32) for i in range(NA)]
                    lfw = T(A, "lfw", [128, NA, 24], F32)
                    sgsts = [T(A, f"sgst{i}", [128, 16], F32) for i in range(NA)]
                    bfB = T(A, "bfB", [128, 6], F32)
                    gvB = T(A, "gvB", [128, 256], F32)
                    bvB = T(A, "bvB", [128, 256], F32)
                    wsf = T(A, "wsf", [128, 4, 128], F32)
                    WsT = T(A, "WsT", [128, 4, 128], BF16)
                    WsTs = T(A, "WsTs", [128, 4, 128], BF16)
                    bsT_t = T(A, "bsT_t", [128, 4], F32)
                    bsB = T(A, "bsB", [128, 256], F32)
                    for i in range(NA):
                        S.op(POOL, lambda h, i=i: h.memset(vaug[i][:], 1.0), w=[B("vaug", i)])
                    S.dma(SP, bfB[:], b_f[l:l + 1, :].partition_broadcast(128), w=[B("cA")])
                    S.dma(SP, gvB[:], g_v[l:l + 1, :].partition_broadcast(128), w=[B("cA")])
                    S.dma(SP, bvB[:], b_v[l:l + 1, :].partition_broadcast(128), w=[B("cA")])
                    S.dma(SP, wsf[:], w_sT[l].rearrange("g j i -> j g i"), w=[B("wsf")])
                    S.dma(SP, bsT_t[:], b_sT[l], w=[B("bsT")])
                    S.op(POOL, lambda h: h.tensor_tensor(out=WsT[:], in0=wsf[:], in1=cstf[:, 1:2, :].to_broadcast([128, 4, 128]),
                                                         op=ALU.mult), r=[B("wsf"), B("cstf")], w=[B("cA")])
                    S.op(POOL, lambda h: h.tensor_tensor(out=WsTs[:], in0=wsf[:], in1=cstf[:, 2:3, :].to_broadcast([128, 4, 128]),
                                                         op=ALU.mult), r=[B("wsf"), B("cstf")], w=[B("cA")])
                    S.op(POOL, lambda h: h.tensor_copy(out=bsB[:].rearrange("p (g c) -> p g c", c=64),
                                                       in_=bsT_t[:].unsqueeze(2).to_broadcast([128, 4, 64])),
                         r=[B("bsT")], w=[B("cA")])
                    WTOK_R = [B("wtok", kc) for kc in range(8)]
                    WFEAT_R = [B("wfeat", kc) for kc in range(8)]

                    def a_prep(t):
                        g, tl = t // 4, t % 4
                        gb = g % 3
                        bi = nxt("xtA", 2)
                        S.dma(SP, xt[bi][:], xsrc[t], w=[B("xtA", bi)])
                        S.op(ACT, lambda h: h.copy(out=xb[bi][:], in_=xt[bi][:]), r=[B("xtA", bi)], w=[B("xbA", bi)])
                        ti = nxt("pt", 2)
                        pt = [pT0, pT1][ti]
                        transposes(pt, B("pT", ti), lambda j: xb[bi][:, j * 128:(j + 1) * 128], 8, r=[B("xbA", bi)])
                        S.op(ACT, lambda h: h.copy(out=xT[gb][:, :, tl * 128:(tl + 1) * 128],
                                                   in_=pt[:].rearrange("p (k q) -> p k q", q=128)),
                             r=[B("pT", ti)], w=[B("xTA", gb, tl)])

                    def a_compute(slot, t):
                        g, tl = t // 4, t % 4
                        gb = g % 3
                        samp = t >= NPB
                        s_i = t - NPB
                        gx, g2, ge, gl = gxs[slot], g2s[slot], ges[slot], gls[slot]
                        vn, vb, sgb, sgst = vns[slot], vbs[slot], sgbs[slot], sgsts[slot]
                        kB = lambda nm: B(nm, "A", slot)
                        kvb = kB("kvout")
                        for ci, (c0, c1) in enumerate([(0, 384), (384, 768), (768, 1152), (1152, 1536), (1536, 2048), (2048, 2054)]):
                            si = nxt("psA", 3)
                            psb = [pA, pB, pC][si]
                            psB = B("pS", si)
                            wd = c1 - c0
                            S.group(PE, [(lambda kc: lambda h: h.matmul(psb[:, 0:wd], lhsT=xT[gb][:, kc, tl * 128:(tl + 1) * 128],
                                                                         rhs=wtok[:, kc, c0:c1], start=kc == 0, stop=kc == 7))(kc)
                                         for kc in range(8)], r=[B("xTA", gb, tl)] + WTOK_R, w=[psB])
                            yield
                            if ci < 4:
                                e = ACT if ci % 2 == 0 else DVE
                                S.op(e, cast_op(e, kvout[slot][:, c0:c1], psb[:, 0:wd]), r=[psB], w=[kvb])
                                yield
                            elif ci == 4:
                                S.op(ACT, lambda h: h.copy(out=gx[:], in_=psb[:, 0:512]), r=[psB], w=[kB("gx")])
                                yield
                                S.op(DVE, lambda h: h.scalar_tensor_tensor(out=g2[:], in0=gx[:], scalar=0.044715, in1=gx[:],
                                                                           op0=ALU.mult, op1=ALU.mult), r=[kB("gx")], w=[kB("g2")])
                                yield
                                S.op(DVE, lambda h: h.scalar_tensor_tensor(out=g2[:], in0=g2[:], scalar=1.0, in1=gx[:],
                                                                           op0=ALU.add, op1=ALU.mult), r=[kB("g2"), kB("gx")], w=[kB("g2")])
                                yield
                                S.op(ACT, lambda h: h.activation(out=ge[:], in_=g2[:], func=AF.Exp, scale=-2.0 * GC), r=[kB("g2")], w=[kB("ge")])
                                yield
                                S.op(DVE, lambda h: h.tensor_scalar(out=ge[:], in0=ge[:], scalar1=1.0, scalar2=None, op0=ALU.add),
                                     r=[kB("ge")], w=[kB("ge")])
                                yield
                                S.op(DVE, lambda h: h.reciprocal(out=ge[:], in_=ge[:]), r=[kB("ge")], w=[kB("ge")])
                                yield
                                S.op(POOL, lambda h: h.tensor_tensor(out=gl[:], in0=gx[:], in1=ge[:], op=ALU.mult),
                                     r=[kB("gx"), kB("ge")], w=[kB("gl")])
                                yield
                                S.op(DVE, lambda h: h.bn_stats(out=sgst[:, 0:6], in_=gl[:, 256:512]), r=[kB("gl")], w=[kB("sgst")])
                                yield
                                S.op(DVE, lambda h: h.bn_aggr(out=sgst[:, 6:8], in_=sgst[:, 0:6]), r=[kB("sgst")], w=[kB("sgst")])
                                yield
                                S.op(ACT, lambda h: h.activation(out=sgst[:, 9:10], in_=sgst[:, 7:8], func=AF.Ln, bias=LN_EPS), r=[kB("sgst")], w=[kB("sgst")])
                                yield
                                S.op(ACT, lambda h: h.activation(out=sgst[:, 8:9], in_=sgst[:, 9:10], func=AF.Exp, scale=-0.5), r=[kB("sgst")], w=[kB("sgst")])
                                yield
                                S.op(DVE, lambda h: h.tensor_scalar(out=vn[:], in0=gl[:, 256:512], scalar1=sgst[:, 6:7],
                                                                    scalar2=sgst[:, 8:9], op0=ALU.subtract, op1=ALU.mult),
                                     r=[kB("gl"), kB("sgst")], w=[kB("vn")])
                                yield
                                S.op(POOL, lambda h: h.tensor_tensor(out=vn[:], in0=vn[:], in1=gvB[:], op=ALU.mult), r=[kB("vn"), B("cA")], w=[kB("vn")])
                                yield
                                S.op(POOL, lambda h: h.tensor_tensor(out=vn[:], in0=vn[:], in1=bvB[:], op=ALU.add), r=[kB("vn"), B("cA")], w=[kB("vn")])
                                yield
                                if samp:
                                    S.dma(POOL, ogv[l, s_i], vn[:], r=[kB("vn")], w=[B("ogv", l, s_i)])
                                S.op(POOL, lambda h: h.tensor_copy(out=vb[:], in_=vn[:]), r=[kB("vn")], w=[kB("vb")])
                                yield
                                Wm = WsTs if samp else WsT
                                S.group(PE, [(lambda gg: lambda h: h.matmul(pM[:, gg * 64:(gg + 1) * 64], lhsT=Wm[:, gg, :],
                                                                             rhs=vb[:, gg * 64:(gg + 1) * 64], start=True, stop=True))(gg)
                                             for gg in range(4)], r=[kB("vb"), B("cA")], w=[B("pM")])
                                S.op(DVE, lambda h: h.tensor_tensor(out=sgb[:], in0=pM[:, 0:256], in1=bsB[:], op=ALU.add),
                                     r=[B("pM"), B("cA")], w=[kB("sgb")])
                                yield
                                S.op(POOL, lambda h: h.tensor_tensor(out=sgb[:], in0=sgb[:], in1=gl[:, 0:256], op=ALU.mult),
                                     r=[kB("sgb"), kB("gl")], w=[kB("sgb")])
                                yield
                                S.op(POOL, lambda h: h.tensor_tensor(out=g2[:, 0:256], in0=sgb[:], in1=sgb[:], op=ALU.mult),
                                     r=[kB("sgb"), kB("g2")], w=[kB("g2")])
                                yield
                                S.op(DVE, lambda h: h.reduce_sum(out=sgst[:, 10:14], in_=g2[:, 0:256].rearrange("p (g c) -> p g c", c=64),
                                                                 axis=mybir.AxisListType.X), r=[kB("g2"), kB("sgst")], w=[kB("sgst")])
                                yield
                                S.op(ACT, lambda h: h.activation(out=sgst[:, 10:14], in_=sgst[:, 10:14], func=AF.Ln, bias=RMS_EPS,
                                                                 scale=1.0 / 64.0), r=[kB("sgst")], w=[kB("sgst")])
                                yield
                                S.op(ACT, lambda h: h.activation(out=sgst[:, 10:14], in_=sgst[:, 10:14], func=AF.Exp, scale=-0.5),
                                     r=[kB("sgst")], w=[kB("sgst")])
                                yield
                                S.op(DVE, lambda h: h.tensor_tensor(out=og[:, t, :].rearrange("p (g c) -> p g c", c=64),
                                                                    in0=sgb[:].rearrange("p (g c) -> p g c", c=64),
                                                                    in1=sgst[:, 10:14].unsqueeze(2).to_broadcast([128, 4, 64]),
                                                                    op=ALU.mult), r=[kB("sgb"), kB("sgst")], w=[B("og")])
                                yield
                            else:
                                li = slot
                                lb_ = kB("lfw")
                                S.op(DVE, lambda h: h.tensor_tensor(out=lfw[:, li, 0:6], in0=psb[:, 0:6], in1=bfB[:], op=ALU.add),
                                     r=[psB, B("cA")], w=[lb_])
                                yield
                                S.op(ACT, lambda h: h.activation(out=lfw[:, li, 6:12], in_=lfw[:, li, 0:6], func=AF.Exp, scale=-1.0), r=[lb_], w=[lb_])
                                yield
                                S.op(ACT, lambda h: h.activation(out=lfw[:, li, 12:18], in_=lfw[:, li, 6:12], func=AF.Ln, bias=1.0), r=[lb_], w=[lb_])
                                yield
                                S.op(DVE, lambda h: h.tensor_scalar(out=lfw[:, li, 18:24], in0=lfw[:, li, 12:18], scalar1=-1.0,
                                                                    scalar2=None, op0=ALU.mult), r=[lb_], w=[lb_])
                                yield
                                S.dma(POOL, olf[l, t], lfw[:, li, 18:24], r=[lb_], w=[B("olf", l, t)])
                                if samp:
                                    S.op(POOL, lambda h: h.tensor_copy(out=slfnew[:, s_i, :], in_=lfw[:, li, 18:24]), r=[lb_], w=[B("slfnew")])
                                else:
                                    S.dma(POOL, lf_in[l][t * 128:(t + 1) * 128, :], lfw[:, li, 18:24], r=[lb_], w=[B("lf_in", l, t)])
                                yield
                        for oi_, (oap, c0) in enumerate([(okf, 0), (ovf, 384), (oks, 768), (ovs, 1152)]):
                            S.dma(POOL, oap[l, t], kvout[slot][:, c0:c0 + 384], r=[kvb], w=[B("okv", l, t, oi_)])
                        yield
                        if samp:
                            for hf, c0 in ((0, 384), (1, 1152)):
                                S.op(POOL, lambda h: h.tensor_copy(
                                    out=svnew[:, s_i, hf * 390:(hf + 1) * 390].rearrange("p (a c) -> p a c", c=65)[:, :, 0:64],
                                    in_=kvout[slot][:, c0:c0 + 384].rearrange("p (a c) -> p a c", c=64)), r=[kvb], w=[B("svnew")])
                                yield
                        else:
                            for hf, c0 in ((0, 384), (1, 1152)):
                                S.op(POOL, lambda h: h.tensor_copy(
                                    out=vaug[slot][:, hf * 390:(hf + 1) * 390].rearrange("p (a c) -> p a c", c=65)[:, :, 0:64],
                                    in_=kvout[slot][:, c0:c0 + 384].rearrange("p (a c) -> p a c", c=64)), r=[kvb], w=[B("vaug", slot)])
                                yield
                            for hf, dst in ((0, vf_in[l]), (1, vs_in[l])):
                                S.dma(POOL, dst[t * 128:(t + 1) * 128, :], vaug[slot][:, hf * 390:(hf + 1) * 390],
                                      r=[B("vaug", slot)], w=[B("v_in", l, t, hf)])
                            yield

                    def a_feat(slot, g):
                        gb = g % 3
                        xr = [B("xTA", gb, tl) for tl in range(4)]
                        for cc in range(12):
                            fi = nxt("poA", 2)
                            pf = [pO0, pO1][fi]
                            pfB = B("pO", fi)
                            S.group(PE, [(lambda kc: lambda h: h.matmul(pf[:, 0:512], lhsT=wfeat[:, kc, cc * 128:(cc + 1) * 128],
                                                                         rhs=xT[gb][:, kc, :], start=kc == 0, stop=kc == 7))(kc)
                                         for kc in range(8)], r=xr + WFEAT_R, w=[pfB])
                            yield
                            e = ACT if cc % 2 == 0 else DVE
                            if cc < 3 or 6 <= cc < 9:
                                pr = cc if cc < 3 else cc - 3
                                S.op(e, cast_op(e, qT[:, pr, g * 512:(g + 1) * 512], pf[:, 0:512]), r=[pfB], w=[B("qT")])
                            else:
                                pr = cc - 3 if cc < 6 else cc - 9
                                fox = cc < 6
                                if g == 4:
                                    S.op(e, cast_op(e, skTn[:, pr + (0 if fox else 3), :], pf[:, 0:512]), r=[pfB], w=[B("skTn")])
                                else:
                                    ksi = nxt("kst", 3)
                                    S.op(e, cast_op(e, kst[ksi][:], pf[:, 0:512]), r=[pfB], w=[B("kst", ksi)])
                                    dst = kTf_in[l] if fox else kTs_in[l]
                                    S.dma(POOL, dst[pr * 128:(pr + 1) * 128, g * 512:(g + 1) * 512], kst[ksi][:],
                                          r=[B("kst", ksi)], w=[B("kT_in", l, g, cc)])
                            yield

                    for tl in range(4):
                        a_prep(tl)
                    tasks = []
                    for g in range(5):
                        for tl in range(4):
                            tasks.append(("task", lambda slot, t=4 * g + tl: a_compute(slot, t)))
                            if g + 1 < 5:
                                tasks.append(("now", lambda t=4 * (g + 1) + tl: a_prep(t)))
                        if FLAGS["featdrain"]:
                            def feat_now(g=g):
                                for _ in a_feat(0, g):
                                    pass
                            tasks.append(("setup", feat_now))
                        else:
                            tasks.append(("task", lambda slot, g=g: a_feat(slot, g)))
                    run_streams(tasks, FLAGS['na'])
                    S.barrier()
                for src, dst, nm in ((kTf_in, kTf_g, "kTf"), (kTs_in, kTs_g, "kTs"), (vf_in, vf_g, "vf"),
                                     (vs_in, vs_g, "vs"), (lf_in, lf_g, "lf")):
                    S.cc(src[l].ap().opt(), dst[l].ap().opt(), r=[], w=[B("g_" + nm, l)])
                S.barrier()

                with ExitStack() as BC:
                    oatt = T(BC, "oatt", [128, NT, 768], BF16)
                    with ExitStack() as Bp:
                        W = attn_work(Bp)
                        KTp = [T(Bp, f"KTp{i}", [128, 32 * 128], BF16) for i in range(2)]
                        Vaug = T(Bp, "Vaug", [128, 32, 390], BF16)
                        ckB = T(Bp, "ckB", [128, 4096], F32)
                        lfT = T(Bp, "lfT", [128, 32, 6], F32)
                        cq = T(Bp, "cq", [128, 96], F32)
                        mAB_f = T(Bp, "mAB_f", [128, 2, 256], F32)
                        mAB_b = T(Bp, "mAB_b", [128, 2, 256], BF16)
                        for kd in range(2):
                            S.op(POOL, lambda h, kd=kd: h.tensor_copy(out=mAB_f[:, kd, :].rearrange("p (a q) -> p a q", q=128),
                                                                      in_=cstf[:, 3 + 2 * kd:5 + 2 * kd, :]), r=[B("cstf")], w=[B("cstf2")])
                        S.op(POOL, lambda h: h.tensor_copy(out=mAB_b[:], in_=mAB_f[:]), r=[B("cstf2")], w=[B("cstb2")])
                        order = [(g % 2) * 16 + g // 2 for g in range(32)]
                        S.dma(SP, lfT[:], lf_g[l].ap().rearrange("(b t) h -> t b h", t=128), r=[B("g_lf", l)], w=[B("lfT")])
                        c_compute(lfT, 32, order, W)
                        cT = W["cT"]
                        S.op(DVE, lambda h: h.tensor_scalar(out=cq[:], in0=cT[:, 0:96], scalar1=selt[:, 0:1], scalar2=None, op0=ALU.mult),
                             r=[B("cT"), B("selt")], w=[B("cq")])
                        S.op(DVE, lambda h: h.scalar_tensor_tensor(out=cq[:], in0=cT[:, 96:192], scalar=selt[:, 1:2], in1=cq[:],
                                                                   op0=ALU.mult, op1=ALU.add), r=[B("cT"), B("selt"), B("cq")], w=[B("cq")])
                        tasks = []
                        kbs = {}
                        for kind, kT_g, v_g in (("fox", kTf_g[l], vf_g[l]), ("sb", kTs_g[l], vs_g[l])):
                            kd = 0 if kind == "fox" else 1

                            def load_v(kind=kind, v_g=v_g):
                                for r_ in range(2):
                                    S.dma(SP, Vaug[:].rearrange("p (k r) c -> p k r c", r=2)[:, :, r_, :],
                                          v_g.ap()[r_ * 2048:(r_ + 1) * 2048, :].rearrange("(k t) c -> t k c", t=128),
                                          r=[B("g_vf" if kind == "fox" else "g_vs", l)], w=[B("V", "p")])
                            tasks.append(("setup", load_v))
                            for hh in range(6):
                                pair, half = hh // 2, hh % 2
                                hp = slice(half * 64, half * 64 + 64)
                                if half == 0:
                                    kb = nxt("KTp", 2)

                                    def load_k(kind=kind, kT_g=kT_g, pair=pair, kb=kb):
                                        for r_ in range(2):
                                            S.dma(SP, KTp[kb][:].rearrange("p (k r t) -> p k r t", r=2, t=128)[:, :, r_, :],
                                                  kT_g.ap()[r_ * 384 + pair * 128:r_ * 384 + (pair + 1) * 128, :].rearrange("p (k t) -> p k t", t=128),
                                                  r=[B("g_kTf" if kind == "fox" else "g_kTs", l)], w=[B("KT", ("p", kb))])
                                    tasks.append(("now", load_k))
                                if kind == "fox":
                                    tasks.append(("setup", lambda hh=hh: ckB_build(W, ckB, ("ckB",), hh, order, 32, 128)))
                                hcol = (0 if kind == "fox" else 384) + hh * 64
                                for k in range(NPB - 1, -1, -1):
                                    tasks.append(("task", lambda slot, kind=kind, hp=hp, pair=pair, kd=kd, k=k, kb=kb, hh=hh, hcol=hcol: attend(
                                        slot, kind, 128, qT[hp, pair + 3 * kd, k * 128:(k + 1) * 128],
                                        lambda lo, hi: KTp[kb][hp, lo * 128:hi * 128],
                                        lambda g_: Vaug[:, g_, hh * 65:(hh + 1) * 65],
                                        2 * k + 2, mAB_f[:, kd, :], mAB_b[:, kd, :], 256,
                                        cq[:, k * 6 + hh:k * 6 + hh + 1], ckB, ("ckB",), W, oatt[:, k, hcol:hcol + 64], (("p", kb), "p"))))
                        run_streams(tasks, FLAGS['nstream'])
                        S.barrier()
                    with ExitStack() as Bs:
                        W = attn_work(Bs)
                        sKT = T(Bs, "sKT", [128, 3, 17 * 128], BF16)
                        sV = T(Bs, "sV", [128, 17, 390], BF16)
                        ckBs = [T(Bs, f"ckBs{i}", [128, 17 * 128], F32) for i in range(3)]
                        lfS = T(Bs, "lfS", [128, 17, 6], F32)
                        cstk = [T(Bs, f"cstk{i}", [128, 2, 384], F32) for i in range(2)]
                        cstv = [T(Bs, f"cstv{i}", [128, 2, 384], F32) for i in range(2)]
                        order = list(range(17))
                        S.op(POOL, lambda h: h.memset(sV[:], 1.0), w=[B("V", "s")])
                        S.op(POOL, lambda h: h.memset(sKT[:], 0.0), w=[B("KT", "s")])
                        for s_i in range(NS):
                            t = NPB + s_i
                            for kind, ck, cv in (("fox", cfk, cfv), ("sb", csk, csv)):
                                kd = 0 if kind == "fox" else 1
                                tasks = []

                                def prep(kind=kind, ck=ck, cv=cv, kd=kd, s_i=s_i):
                                    for b2 in range(8):
                                        ci_ = nxt("cstk", 2)
                                        S.dma(SP, cstk[ci_][:], ck[l, s_i, b2 * 256:(b2 + 1) * 256, :].rearrange("(b t) c -> t b c", t=128),
                                              w=[B("cstk", ci_)])
                                        S.dma(SP, cstv[ci_][:], cv[l, s_i, b2 * 256:(b2 + 1) * 256, :].rearrange("(b t) c -> t b c", t=128),
                                              w=[B("cstv", ci_)])
                                        for bb_ in range(2):
                                            blk = b2 * 2 + bb_
                                            ri_ = nxt("ptr", 2)
                                            pt = [pA, pB][ri_]
                                            ptb_ = B("pS", ri_)
                                            S.group(PE, [(lambda j: lambda h: h.transpose(out=pt[:, j * 128:(j + 1) * 128],
                                                                                          in_=cstk[ci_][:, bb_, j * 128:(j + 1) * 128],
                                                                                          identity=identf))(j) for j in range(3)],
                                                    r=[B("cstk", ci_), B("cstf")], w=[ptb_])
                                            S.op(ACT, lambda h: h.copy(out=sKT[:, :, blk * 128:(blk + 1) * 128],
                                                                       in_=pt[:, 0:384].rearrange("p (a q) -> p a q", q=128)),
                                                 r=[ptb_], w=[B("KT", "s")])
                                        S.op(POOL, lambda h: h.tensor_copy(
                                            out=sV[:, b2 * 2:(b2 + 1) * 2, :].rearrange("p b (a c) -> p b a c", c=65)[:, :, :, 0:64],
                                            in_=cstv[ci_][:].rearrange("p b (a c) -> p b a c", c=64)), r=[B("cstv", ci_)], w=[B("V", "s")])
                                    S.op(POOL, lambda h: h.tensor_copy(out=sKT[:, :, 2048:2048 + 64],
                                                                       in_=skTn[:, 3 * kd:3 * kd + 3, s_i * 128:s_i * 128 + 64]),
                                         r=[B("skTn")], w=[B("KT", "s")])
                                    S.op(POOL, lambda h: h.memset(sV[:, 16, :], 0.0), w=[B("V", "s")])
                                    S.op(POOL, lambda h: h.tensor_copy(out=sV[0:64, 16, :], in_=svnew[0:64, s_i, kd * 390:(kd + 1) * 390]),
                                         r=[B("svnew")], w=[B("V", "s")])
                                    if kind == "fox":
                                        S.op(POOL, lambda h: h.memset(lfS[:, 16, :], 0.0), w=[B("lfT")])
                                        S.dma(SP, lfS[:, 0:16, :], cfl[l, s_i].rearrange("(b t) h -> t b h", t=128), w=[B("lfT")])
                                        S.op(POOL, lambda h: h.tensor_copy(out=lfS[0:64, 16, :], in_=slfnew[0:64, s_i, :]),
                                             r=[B("slfnew")], w=[B("lfT")])
                                        c_compute(lfS, 17, order, W)
                                tasks.append(("setup", prep))
                                for hh in range(6):
                                    pair, half = hh // 2, hh % 2
                                    hp = slice(half * 64, half * 64 + 64)
                                    hcol = (0 if kind == "fox" else 384) + hh * 64

                                    def mk(slot, kind=kind, hp=hp, pair=pair, kd=kd, hh=hh, hcol=hcol, t=t):
                                        cb = ckBs[slot % 3]
                                        key = ("ckBs", slot % 3)
                                        if kind == "fox":
                                            ckB_build(W, cb, key, hh, order, 17, 64)
                                        return attend(slot, kind, 64, qT[hp, pair + 3 * kd, t * 128:t * 128 + 64],
                                                      lambda lo, hi: sKT[hp, pair, lo * 128:hi * 128],
                                                      lambda g_: sV[:, g_, hh * 65:(hh + 1) * 65],
                                                      17, cstf[0:64, 7 + kd, :], cstb[0:64, 7 + kd, :], 128,
                                                      W["cT"][0:64, 16 * 6 + hh:16 * 6 + hh + 1], cb, key, W,
                                                      oatt[0:64, t, hcol:hcol + 64], ("s", "s"))
                                    tasks.append(("task", mk))
                                run_streams(tasks, min(FLAGS['nstream'], 3 if kind == "fox" else NSLOT))
                        S.barrier()
                    with ExitStack() as C1:
                        wout = T(C1, "wout", [128, 8, D], BF16)
                        stg = [T(C1, f"stgC{i}", [128, D], F32) for i in range(2)]
                        gmT = T(C1, "gmT", [128, 8], F32)
                        g1B = T(C1, "g1B", [128, D], F32)
                        b1B = T(C1, "b1B", [128, D], F32)
                        oT = [T(C1, f"oT{i}", [128, 8, 128], BF16) for i in range(3)]
                        xt = [T(C1, f"xtC{i}", [128, D], F32) for i in range(3)]
                        res = [T(C1, f"resC{i}", [128, D], F32) for i in range(2)]
                        xn = [T(C1, f"xnC{i}", [128, D], F32) for i in range(2)]
                        stt = [T(C1, f"stC{i}", [128, 16], F32) for i in range(2)]
                        S.dma(SP, gmT[:], g_mixT[l], w=[B("gmT")])
                        S.dma(SP, g1B[:], ln1_g[l:l + 1, :].partition_broadcast(128), w=[B("cC")])
                        S.dma(SP, b1B[:], ln1_b[l:l + 1, :].partition_broadcast(128), w=[B("cC")])
                        for kc in range(8):
                            load_cast(stg, wout[:, kc, :], w_out[l, kc * 128:(kc + 1) * 128, :], D, B("wout", kc),
                                      scale=gmT[:, kc:kc + 1], engs=[POOL, DVE])
                        WOUT_R = [B("wout", kc) for kc in range(8)] + [B("gmT")]
                        def c1_prep(t):
                            bi = t % 3
                            S.dma(SP, xt[bi][:], xsrc[t], w=[B("xtC", bi)])
                            ti = nxt("pt", 2)
                            pt = [pT0, pT1][ti]

                            def osrc(j, t=t):
                                if j < 3:
                                    return oatt[:, t, j * 128:(j + 1) * 128]
                                if j < 5:
                                    return og[:, t, (j - 3) * 128:(j - 2) * 128]
                                return oatt[:, t, 384 + (j - 5) * 128:384 + (j - 4) * 128]
                            transposes(pt, B("pT", ti), osrc, 8, r=[B("oatt"), B("og")])
                            oi_ = t % 3
                            S.op(ACT, lambda h: h.copy(out=oT[oi_][:], in_=pt[:].rearrange("p (k q) -> p k q", q=128)),
                                 r=[B("pT", ti)], w=[B("oT", oi_)])

                        def c1_banks(t):
                            return [(pA, B("pS", 0)), (pB, B("pS", 1))] if t % 2 == 0 else [(pO0, B("pO", 0)), (pO1, B("pO", 1))]

                        def c1_mix(t):
                            oi_ = t % 3
                            for n_, (pb_, pbB) in enumerate(c1_banks(t)):
                                S.group(PE, [(lambda kc: lambda h: h.matmul(pb_[:, 0:512], lhsT=oT[oi_][:, kc, :],
                                                                             rhs=wout[:, kc, n_ * 512:(n_ + 1) * 512],
                                                                             start=kc == 0, stop=kc == 7))(kc) for kc in range(8)],
                                        r=[B("oT", oi_)] + WOUT_R, w=[pbB])

                        def c1_post(t):
                            bi = t % 3
                            ri = t % 2
                            for n_, (pb_, pbB) in enumerate(c1_banks(t)):
                                S.op(DVE, lambda h: h.scalar_tensor_tensor(
                                    out=res[ri][:, n_ * 512:(n_ + 1) * 512], in0=xt[bi][:, n_ * 512:(n_ + 1) * 512], scalar=ALPHA,
                                    in1=pb_[:, 0:512], op0=ALU.mult, op1=ALU.add), r=[B("xtC", bi), pbB], w=[B("resC", ri)])
                            layer_norm_tile(res[ri][:], xn[ri][:], stt[ri], g1B[:], b1B[:], B("resC", ri), B("xnC", ri), B("stC", ri), B("cC"))
                            S.dma(POOL, xmid.ap()[t], xn[ri][:], r=[B("xnC", ri)], w=[B("xmid", t)])

                        if FLAGS["c1skew"]:
                            c1_prep(0)
                            c1_prep(1)
                            for t in range(NT):
                                c1_mix(t)
                                if t + 2 < NT:
                                    c1_prep(t + 2)
                                c1_post(t)
                        else:
                            for t in range(NT):
                                c1_prep(t)
                                c1_mix(t)
                                c1_post(t)
                        S.barrier()
            with ExitStack() as C2:
                wup = T(C2, "wup", [128, 8, 4 * D], BF16)
                wdn = T(C2, "wdn", [128, 32, D], BF16)
                g2B = T(C2, "g2B", [128, D], F32)
                b2B = T(C2, "b2B", [128, D], F32)
                S.dma(SP, g2B[:], ln2_g[l:l + 1, :].partition_broadcast(128), w=[B("cD")])
                S.dma(SP, b2B[:], ln2_b[l:l + 1, :].partition_broadcast(128), w=[B("cD")])
                stgD = [T(C2, f"stgD{i}", [128, 1024], F32) for i in range(2)]

                def load_quarter(q4):
                    for kc in range(8):
                        load_cast(stgD, wup[:, kc, q4 * 1024:(q4 + 1) * 1024], w_up[l, kc * 128:(kc + 1) * 128, q4 * 1024:(q4 + 1) * 1024],
                                  1024, B("wup", kc, q4))
                    for fc in range(8 * q4, 8 * q4 + 8):
                        load_cast(stgD, wdn[:, fc, :], w_down[l, fc * 128:(fc + 1) * 128, :], 1024, B("wdn", fc))
                load_quarter(0)
                with ExitStack() as C2b:
                    xt = [T(C2b, f"xtD{i}", [128, D], F32) for i in range(4)]
                    xb = [T(C2b, f"xbD{i}", [128, D], BF16) for i in range(2)]
                    x1T = [T(C2b, f"x1T{i}", [128, 8, 256], BF16) for i in range(2)]
                    hT = [T(C2b, f"hT{i}", [128, 256], BF16) for i in range(4)]
                    hr = [T(C2b, f"hr{i}", [128, 256], F32) for i in range(3)]
                    res = [T(C2b, f"resD{i}", [128, D], F32) for i in range(2)]
                    xn = [T(C2b, f"xnD{i}", [128, D], F32) for i in range(2)]
                    stt = [T(C2b, f"stD{i}", [128, 16], F32) for i in range(2)]
                    accs = [[(pA, B("pS", 0)), (pB, B("pS", 1))], [(pO0, B("pO", 0)), (pO1, B("pO", 1))]]

                    def c2_prep(gp):
                        xg = gp % 2
                        for tl in range(2):
                            t = 2 * gp + tl
                            bi = (2 * gp + tl) % 4
                            S.dma(SP, xt[bi][:], xmid.ap()[t], r=[B("xmid", t)], w=[B("xtD", bi)])
                            ci_ = nxt("xbD", 2)
                            S.op(POOL, lambda h: h.tensor_copy(out=xb[ci_][:], in_=xt[bi][:]), r=[B("xtD", bi)], w=[B("xbD", ci_)])
                            ti = nxt("pt", 2)
                            pt = [pT0, pT1][ti]
                            transposes(pt, B("pT", ti), lambda j: xb[ci_][:, j * 128:(j + 1) * 128], 8, r=[B("xbD", ci_)])
                            S.op(ACT, lambda h: h.copy(out=x1T[xg][:, :, tl * 128:(tl + 1) * 128],
                                                       in_=pt[:].rearrange("p (k q) -> p k q", q=128)),
                                 r=[B("pT", ti)], w=[B("x1T", xg)])

                    def c2_up(gp, fc):
                        xg = gp % 2
                        ph, phB = [(pC, B("pS", 2)), (pM, B("pM"))][fc % 2]
                        S.group(PE, [(lambda kc: lambda h: h.matmul(ph[:, 0:256], lhsT=wup[:, kc, fc * 128:(fc + 1) * 128],
                                                                     rhs=x1T[xg][:, kc, :], start=kc == 0, stop=kc == 7))(kc)
                                     for kc in range(8)], r=[B("x1T", xg)] + [B("wup", kc, fc // 8) for kc in range(8)], w=[phB])
                        hj = fc % 4
                        rj = fc % 3
                        S.op(ACT, lambda h: h.activation(out=hr[rj][:], in_=ph[:, 0:256], func=AF.Relu), r=[phB], w=[B("hr", rj)])
                        S.op(DVE if fc % 2 == 0 else POOL, lambda h: h.tensor_tensor(out=hT[hj][:], in0=hr[rj][:], in1=hr[rj][:], op=ALU.mult),
                             r=[B("hr", rj)], w=[B("hT", hj)])

                    def c2_down(gp, fc):
                        hj = fc % 4
                        for tl in range(2):
                            for n_ in range(2):
                                pb_, pbB = accs[tl][n_]
                                S.group(PE, [lambda h: h.matmul(pb_[:, 0:512], lhsT=hT[hj][:, tl * 128:(tl + 1) * 128],
                                                                rhs=wdn[:, fc, n_ * 512:(n_ + 1) * 512], start=fc == 0, stop=fc == 31)],
                                        r=[B("hT", hj), B("wdn", fc)], w=[pbB])

                    def c2_post(gp):
                        for tl in range(2):
                            t = 2 * gp + tl
                            bi = (2 * gp + tl) % 4
                            ri = tl
                            for n_ in range(2):
                                pb_, pbB = accs[tl][n_]
                                S.op(DVE, lambda h: h.scalar_tensor_tensor(
                                    out=res[ri][:, n_ * 512:(n_ + 1) * 512], in0=xt[bi][:, n_ * 512:(n_ + 1) * 512], scalar=ALPHA,
                                    in1=pb_[:, 0:512], op0=ALU.mult, op1=ALU.add), r=[B("xtD", bi), pbB], w=[B("resD", ri)])
                            layer_norm_tile(res[ri][:], xn[ri][:], stt[ri], g2B[:], b2B[:], B("resD", ri), B("xnD", ri), B("stD", ri), B("cD"))
                            S.dma(POOL, xdst[t], xn[ri][:], r=[B("xnD", ri)], w=[B("xdst", l, t)])

                    NG = NT // 2
                    if FLAGS["c2skew"]:
                        c2_prep(0)
                        for gp in range(NG):
                            c2_up(gp, 0)
                            c2_up(gp, 1)
                            if gp + 1 < NG:
                                c2_prep(gp + 1)
                            for fc in range(32):
                                if gp == 0 and fc % 8 == 0 and fc // 8 + 1 < 4:
                                    load_quarter(fc // 8 + 1)
                                if fc + 2 < 32:
                                    c2_up(gp, fc + 2)
                                c2_down(gp, fc)
                            c2_post(gp)
                    else:
                        for q4 in range(1, 4):
                            load_quarter(q4)
                        for gp in range(NG):
                            c2_prep(gp)
                            for fc in range(32):
                                c2_up(gp, fc)
                                c2_down(gp, fc)
                            c2_post(gp)
                    S.barrier()
        S.barrier()
    return nc


_NC = None


def _get_nc():
    global _NC
    if _NC is None:
        _NC = build_nc()
    return _NC


def kernel(x_prompt, x_sample, cache_fox_k, cache_fox_v, cache_fox_logf, cache_sb_k, cache_sb_v,
           w_in, b_f, g_v, b_v, w_s, b_s, g_mix, w_out, ln1_g, ln1_b, w_up, w_down, ln2_g, ln2_b):
    f32 = lambda a: np.ascontiguousarray(np.asarray(a), dtype=np.float32)
    x_prompt, x_sample = f32(x_prompt), f32(x_sample)
    w_in = f32(w_in)
    sp = np.cumsum([384, 384, 384, 6, 256, 256, 384, 384, 384])
    seg = lambda i: slice(0 if i == 0 else sp[i - 1], sp[i])
    qf, kf, vf, fl, ug, vg, qs, ks, vs = [w_in[:, :, seg(i)] for i in range(9)]
    w_tok = np.ascontiguousarray(np.concatenate([kf, vf, ks, vs, ug, vg, fl], axis=2))
    w_feat = np.ascontiguousarray(np.concatenate([qf, kf, qs, ks], axis=2))
    w_sT = np.ascontiguousarray(np.transpose(f32(w_s), (0, 1, 3, 2)))
    b_sT = np.ascontiguousarray(np.transpose(f32(b_s), (0, 2, 1)))
    g_mixT = np.ascontiguousarray(np.transpose(f32(g_mix).reshape(DEPTH, 8, 128), (0, 2, 1)))

    ii = np.arange(128)
    tri_le = (ii[:, None] <= ii[None, :]).astype(np.float32)
    ident = np.eye(128, dtype=np.float32)
    ones = np.ones((128, 128), np.float32)
    zeros = np.zeros((128, 128), np.float32)
    sgs = tri_le * (ii[:, None] < 64) * (ii[None, :] < 64)
    fox_tri = (ii[None, :] <= ii[:, None]).astype(np.float32)
    sb_tri = (ii[None, :] < ii[:, None]).astype(np.float32)

    shared = dict(w_tok=w_tok, w_feat=w_feat, b_f=f32(b_f), g_v=f32(g_v), b_v=f32(b_v), w_sT=w_sT, b_sT=b_sT,
                  g_mixT=g_mixT, w_out=f32(w_out), ln1_g=f32(ln1_g), ln1_b=f32(ln1_b), ln2_g=f32(ln2_g),
                  ln2_b=f32(ln2_b), w_up=f32(w_up), w_down=f32(w_down))
    cfk_a = f32(cache_fox_k).reshape(DEPTH, 32, PAST, 384)
    cfv_a = f32(cache_fox_v).reshape(DEPTH, 32, PAST, 384)
    csk_a = f32(cache_sb_k).reshape(DEPTH, 32, PAST, 384)
    csv_a = f32(cache_sb_v).reshape(DEPTH, 32, PAST, 384)
    cfl_a = f32(cache_fox_logf)
    in_maps = []
    for c in range(8):
        b, j = c // 2, c % 2
        xin = np.zeros((NT, 128, D), np.float32)
        xin[:NPB] = x_prompt[b].reshape(32, 128, D)[j::2]
        xin[NPB:, :64] = x_sample[4 * c:4 * c + 4]
        if j == 0:
            msk = [fox_tri, zeros, sb_tri, zeros]
        else:
            msk = [ones, fox_tri, ones, sb_tri]
        cst = np.stack([ident, tri_le, sgs] + msk + [fox_tri, sb_tri]).astype(np.float32)
        sel = np.zeros((128, 2), np.float32)
        sel[:, j] = 1.0
        m = dict(shared)
        m.update(xin=xin, cfk=np.ascontiguousarray(cfk_a[:, 4 * c:4 * c + 4]), cfv=np.ascontiguousarray(cfv_a[:, 4 * c:4 * c + 4]),
                 csk=np.ascontiguousarray(csk_a[:, 4 * c:4 * c + 4]), csv=np.ascontiguousarray(csv_a[:, 4 * c:4 * c + 4]),
                 cfl=np.ascontiguousarray(cfl_a[:, 4 * c:4 * c + 4]), cst=cst, sel=sel)
        in_maps.append(m)

    res = run_bass_kernel_spmd(_get_nc(), in_maps, core_ids=list(range(8)))
    R = res.results

    y_p = np.zeros((4, 32, 128, D), np.float32)
    y_s = np.zeros((32, 64, D), np.float32)
    pk = {n: np.zeros((DEPTH, 4, 32, 128, 384), np.float32) for n in ("okf", "ovf", "oks", "ovs")}
    pl = np.zeros((DEPTH, 4, 32, 128, 6), np.float32)
    sk = {n: np.zeros((DEPTH, 32, 64, 384), np.float32) for n in ("okf", "ovf", "oks", "ovs")}
    sl = np.zeros((DEPTH, 32, 64, 6), np.float32)
    sg = np.zeros((DEPTH, 32, 64, 256), np.float32)
    for c in range(8):
        b, j = c // 2, c % 2
        r = R[c]
        y_p[b, j::2] = r["y"][:NPB]
        y_s[4 * c:4 * c + 4] = r["y"][NPB:, :64]
        for n in pk:
            pk[n][:, b, j::2] = r[n][:, :NPB]
            sk[n][:, 4 * c:4 * c + 4] = r[n][:, NPB:, :64]
        pl[:, b, j::2] = r["olf"][:, :NPB]
        sl[:, 4 * c:4 * c + 4] = r["olf"][:, NPB:, :64]
        sg[:, 4 * c:4 * c + 4] = r["ogv"][:, :, :64]
    P5 = lambda a: a.reshape(DEPTH, 4, 4096, 6, 64)
    S5 = lambda a: a.reshape(DEPTH, 32, 64, 6, 64)
    return (y_p.reshape(4, 4096, D), y_s,
            P5(pk["okf"]), P5(pk["ovf"]), pl.reshape(DEPTH, 4, 4096, 6), P5(pk["oks"]), P5(pk["ovs"]),
            S5(sk["okf"]), S5(sk["ovf"]), sl, S5(sk["oks"]), S5(sk["ovs"]), sg)
```
